# Optimizing a Trainium2 kernel written in Bass

```python
import math
import jax, jax.numpy as jnp
from jax import lax
import numpy as np

D_MODEL = 1024
BATCH = 1
SEQ = 16384
DEPTH = 1

HEAD_DIM = 128
HEADS_PER_GROUP = 4
ATTN_GROUPS = ((128, 1), (512, 4), (2048, 16))
N_ATTN_GROUPS = len(ATTN_GROUPS)
ATTN_HEADS = HEADS_PER_GROUP * N_ATTN_GROUPS
ATTN_QK_WIDTH = ATTN_HEADS * HEAD_DIM
ATTN_OUT_WIDTH = HEADS_PER_GROUP * HEAD_DIM
BLOCK = 128
ROPE_THETA = 500000.0
ROPE_DIM = HEAD_DIM // 4
SSM_WIDTH = 512
SSM_GROUP = 16
SSM_GROUPS = SSM_WIDTH // SSM_GROUP
SSM_STATE = 64
DT_MIN = 0.001
DT_MAX = 0.1
D_FF = -(-8 * D_MODEL // (3 * 256)) * 256
PLE_DIM = 256
EPS = 1e-6
IN_WIDTH = 3 * ATTN_QK_WIDTH + SSM_WIDTH + 2 * D_MODEL

kernel_name = "hybrid_dilated_attn_s5_gated_block"


def rmsnorm(x, g):
    xf = x.astype(jnp.float32)
    y = xf * lax.rsqrt(jnp.mean(xf * xf, axis=-1, keepdims=True) + EPS)
    return (y * g.astype(jnp.float32)).astype(x.dtype)


def partial_rotary(x, positions):
    half = ROPE_DIM // 2
    inv_freq = ROPE_THETA ** (-jnp.arange(half, dtype=jnp.float32) * 2.0 / ROPE_DIM)
    ang = positions.astype(jnp.float32)[..., None] * inv_freq
    cos = jnp.cos(ang)[:, :, None, :]
    sin = jnp.sin(ang)[:, :, None, :]
    xr = x[..., :ROPE_DIM].astype(jnp.float32)
    x1, x2 = xr[..., :half], xr[..., half:]
    rot = jnp.concatenate([x1 * cos - x2 * sin, x2 * cos + x1 * sin], axis=-1).astype(x.dtype)
    return jnp.concatenate([rot, x[..., ROPE_DIM:]], axis=-1)


def dilated_band_attention(q, k, v, dilation, band):
    B, S, H, Dh = q.shape
    L = S // dilation
    Lp = -(-L // BLOCK) * BLOCK
    nb = Lp // BLOCK

    def to_sub(t):
        t = jnp.moveaxis(t.reshape(B, L, dilation, H, Dh), 2, 1)
        t = jnp.pad(t, ((0, 0), (0, 0), (0, Lp - L), (0, 0), (0, 0)))
        return t.reshape(B, dilation, nb, BLOCK, H, Dh)

    def with_prev(t):
        prev = jnp.pad(t, ((0, 0), (0, 0), (1, 0), (0, 0), (0, 0), (0, 0)))[:, :, :-1]
        return jnp.concatenate([prev, t], axis=3)

    def from_sub(t):
        rest = t.shape[4:]
        t = t.reshape((B, dilation, Lp) + rest)[:, :, :L]
        return jnp.moveaxis(t, 1, 2).reshape((B, S) + rest)

    qb = to_sub(q)
    kk = with_prev(to_sub(k))
    vv = with_prev(to_sub(v))
    scale = 1.0 / math.sqrt(Dh)
    s = jnp.einsum('brnqhd,brnkhd->brnhqk', qb, kk,
                   preferred_element_type=jnp.float32) * scale
    qi = jnp.arange(BLOCK)[:, None]
    kj = jnp.arange(2 * BLOCK)[None, :]
    rel = BLOCK + qi - kj
    in_band = (rel >= 0) & (rel <= band)
    not_first = (jnp.arange(nb) > 0)[:, None, None]
    valid = in_band[None] & (not_first | (kj >= BLOCK)[None])
    s = jnp.where(valid[None, None, :, None], s, -jnp.inf)
    m = jnp.max(s, axis=-1, keepdims=True)
    pexp = jnp.exp(s - m)
    den = jnp.sum(pexp, axis=-1, keepdims=True)
    o = jnp.einsum('brnhqk,brnkhd->brnqhd', pexp, vv.astype(jnp.float32))
    o = o / jnp.swapaxes(den[..., 0], -1, -2)[..., None]
    lse = jnp.swapaxes(m[..., 0] + jnp.log(den[..., 0]), -1, -2)
    return from_sub(o), from_sub(lse)


def _complex_affine_combine(e1, e2):
    a1r, a1i, b1r, b1i = e1
    a2r, a2i, b2r, b2i = e2
    ar = a1r * a2r - a1i * a2i
    ai = a1r * a2i + a1i * a2r
    br = a2r * b1r - a2i * b1i + b2r
    bi = a2r * b1i + a2i * b1r + b2i
    return (ar, ai, br, bi)


def s5_ssm(u, a_re, a_im, log_dt, b_re, b_im, c_re, c_im, d_skip):
    B, S, _ = u.shape
    uf = u.astype(jnp.float32).reshape(B, S, SSM_GROUPS, SSM_GROUP)
    dt = jnp.exp(log_dt.astype(jnp.float32))[:, None]
    lr = a_re.astype(jnp.float32)
    li = a_im.astype(jnp.float32)
    mag = jnp.exp(lr * dt)
    bar_re = mag * jnp.cos(li * dt)
    bar_im = mag * jnp.sin(li * dt)
    nr = bar_re - 1.0
    ni = bar_im
    den = lr * lr + li * li
    z_re = (nr * lr + ni * li) / den
    z_im = (ni * lr - nr * li) / den
    br_ = b_re.astype(jnp.float32)
    bi_ = b_im.astype(jnp.float32)
    bb_re = z_re[..., None] * br_ - z_im[..., None] * bi_
    bb_im = z_re[..., None] * bi_ + z_im[..., None] * br_
    bu_re = jnp.einsum('bsgc,gpc->bsgp', uf, bb_re)
    bu_im = jnp.einsum('bsgc,gpc->bsgp', uf, bb_im)
    ar = jnp.broadcast_to(bar_re, bu_re.shape)
    ai = jnp.broadcast_to(bar_im, bu_im.shape)
    _, _, h_re, h_im = lax.associative_scan(_complex_affine_combine, (ar, ai, bu_re, bu_im), axis=1)
    y = (jnp.einsum('bsgp,gcp->bsgc', h_re, c_re.astype(jnp.float32))
         - jnp.einsum('bsgp,gcp->bsgc', h_im, c_im.astype(jnp.float32))
         + d_skip.astype(jnp.float32) * uf)
    return y.reshape(B, S, SSM_WIDTH).astype(u.dtype)


def hybrid_layer(h, p_l, positions, g_mix, w_in, a_re, a_im, log_dt, b_re, b_im, c_re, c_im,
                 d_skip, w_attn_proj, w_glu_a, w_glu_b, w_out, g_ffn, w_ffn_gate, w_ffn_up,
                 w_ffn_down, w_ple_gate, w_ple_proj):
    B, S, _ = h.shape
    n = rmsnorm(h, g_mix)
    z = n @ w_in
    o0 = ATTN_QK_WIDTH
    q = z[..., 0:o0].reshape(B, S, ATTN_HEADS, HEAD_DIM)
    k = z[..., o0:2 * o0].reshape(B, S, ATTN_HEADS, HEAD_DIM)
    v = z[..., 2 * o0:3 * o0].reshape(B, S, ATTN_HEADS, HEAD_DIM)
    o1 = 3 * o0
    u = z[..., o1:o1 + SSM_WIDTH]
    o2 = o1 + SSM_WIDTH
    gate_attn = jax.nn.sigmoid(z[..., o2:o2 + D_MODEL])
    gate_ssm = jax.nn.sigmoid(z[..., o2 + D_MODEL:o2 + 2 * D_MODEL])

    q = partial_rotary(q, positions)
    k = partial_rotary(k, positions)
    outs = []
    lses = []
    for gi, (window, dilation) in enumerate(ATTN_GROUPS):
        hs = slice(gi * HEADS_PER_GROUP, (gi + 1) * HEADS_PER_GROUP)
        o_g, l_g = dilated_band_attention(q[:, :, hs], k[:, :, hs], v[:, :, hs],
                                          dilation, window // dilation)
        outs.append(o_g)
        lses.append(l_g)
    wts = jax.nn.softmax(jnp.stack(lses, axis=0), axis=0)
    attn = jnp.sum(wts[..., None] * jnp.stack(outs, axis=0), axis=0)
    attn_d = attn.reshape(B, S, ATTN_OUT_WIDTH).astype(h.dtype) @ w_attn_proj

    y = jax.nn.gelu(s5_ssm(u, a_re, a_im, log_dt, b_re, b_im, c_re, c_im, d_skip))
    ssm_d = (y @ w_glu_a) * jax.nn.sigmoid(y @ w_glu_b)

    h = h + (gate_attn * attn_d + gate_ssm * ssm_d) @ w_out

    n2 = rmsnorm(h, g_ffn)
    h = h + (jax.nn.silu(n2 @ w_ffn_gate) * (n2 @ w_ffn_up)) @ w_ffn_down

    h = h + jax.nn.sigmoid(h @ w_ple_gate) * (p_l.astype(h.dtype) @ w_ple_proj)
    return h


def setup_inputs(seed: int = 0) -> dict:
    key = jax.random.key(seed)
    ks = jax.random.split(key, 26)
    f32 = jnp.float32

    def nrm(k, shape, fan_in):
        return jax.random.normal(k, shape, f32) * (fan_in ** -0.5)

    x = jax.random.normal(ks[0], (BATCH, SEQ, D_MODEL), f32)
    p = jax.random.normal(ks[1], (DEPTH, BATCH, SEQ, PLE_DIM), f32)
    positions = jnp.broadcast_to(jnp.arange(SEQ, dtype=jnp.int32)[None, :], (BATCH, SEQ))
    g_mix = 1.0 + 0.05 * jax.random.normal(ks[2], (DEPTH, D_MODEL), f32)
    w_in = nrm(ks[3], (DEPTH, D_MODEL, IN_WIDTH), D_MODEL)
    a_re = -0.5 + 0.01 * jax.random.normal(ks[4], (DEPTH, SSM_GROUPS, SSM_STATE), f32)
    a_im = (jnp.pi * jnp.arange(SSM_STATE, dtype=f32)[None, None, :]
            + 0.01 * jax.random.normal(ks[5], (DEPTH, SSM_GROUPS, SSM_STATE), f32))
    log_dt = jax.random.uniform(ks[6], (DEPTH, SSM_GROUPS), f32,
                                minval=math.log(DT_MIN), maxval=math.log(DT_MAX))
    b_re = nrm(ks[7], (DEPTH, SSM_GROUPS, SSM_STATE, SSM_GROUP), 2 * SSM_GROUP)
    b_im = nrm(ks[8], (DEPTH, SSM_GROUPS, SSM_STATE, SSM_GROUP), 2 * SSM_GROUP)
    c_re = nrm(ks[9], (DEPTH, SSM_GROUPS, SSM_GROUP, SSM_STATE), SSM_STATE)
    c_im = nrm(ks[10], (DEPTH, SSM_GROUPS, SSM_GROUP, SSM_STATE), SSM_STATE)
    d_skip = jax.random.normal(ks[11], (DEPTH, SSM_GROUPS, SSM_GROUP), f32)
    w_attn_proj = nrm(ks[12], (DEPTH, ATTN_OUT_WIDTH, D_MODEL), ATTN_OUT_WIDTH)
    w_glu_a = nrm(ks[13], (DEPTH, SSM_WIDTH, D_MODEL), SSM_WIDTH)
    w_glu_b = nrm(ks[14], (DEPTH, SSM_WIDTH, D_MODEL), SSM_WIDTH)
    w_out = nrm(ks[15], (DEPTH, D_MODEL, D_MODEL), D_MODEL)
    g_ffn = 1.0 + 0.05 * jax.random.normal(ks[16], (DEPTH, D_MODEL), f32)
    w_ffn_gate = nrm(ks[17], (DEPTH, D_MODEL, D_FF), D_MODEL)
    w_ffn_up = nrm(ks[18], (DEPTH, D_MODEL, D_FF), D_MODEL)
    w_ffn_down = nrm(ks[19], (DEPTH, D_FF, D_MODEL), D_FF)
    w_ple_gate = nrm(ks[20], (DEPTH, D_MODEL, D_MODEL), D_MODEL)
    w_ple_proj = nrm(ks[21], (DEPTH, PLE_DIM, D_MODEL), PLE_DIM)
    g_final = 1.0 + 0.05 * jax.random.normal(ks[22], (D_MODEL,), f32)
    return {"x": x, "p": p, "positions": positions, "g_mix": g_mix, "w_in": w_in,
            "a_re": a_re, "a_im": a_im, "log_dt": log_dt, "b_re": b_re, "b_im": b_im,
            "c_re": c_re, "c_im": c_im, "d_skip": d_skip, "w_attn_proj": w_attn_proj,
            "w_glu_a": w_glu_a, "w_glu_b": w_glu_b, "w_out": w_out, "g_ffn": g_ffn,
            "w_ffn_gate": w_ffn_gate, "w_ffn_up": w_ffn_up, "w_ffn_down": w_ffn_down,
            "w_ple_gate": w_ple_gate, "w_ple_proj": w_ple_proj, "g_final": g_final}


def reference(x, p, positions, g_mix, w_in, a_re, a_im, log_dt, b_re, b_im, c_re, c_im, d_skip,
              w_attn_proj, w_glu_a, w_glu_b, w_out, g_ffn, w_ffn_gate, w_ffn_up, w_ffn_down,
              w_ple_gate, w_ple_proj, g_final):
    h = x
    for i in range(DEPTH):
        h = hybrid_layer(h, p[i], positions, g_mix[i], w_in[i], a_re[i], a_im[i], log_dt[i],
                         b_re[i], b_im[i], c_re[i], c_im[i], d_skip[i], w_attn_proj[i],
                         w_glu_a[i], w_glu_b[i], w_out[i], g_ffn[i], w_ffn_gate[i], w_ffn_up[i],
                         w_ffn_down[i], w_ple_gate[i], w_ple_proj[i])
    return rmsnorm(h, g_final)
```

```python
import math
import os
import numpy as np
from contextlib import ExitStack
import concourse.bass as bass
import concourse.mybir as mybir
from concourse.bass_utils import run_bass_kernel_spmd

F32 = mybir.dt.float32
BF16 = mybir.dt.bfloat16
I32 = mybir.dt.int32
AF = mybir.ActivationFunctionType
ALU = mybir.AluOpType
AX = mybir.AxisListType
ds = bass.ds

ENGS = ("pe", "act", "dve", "pool", "sp")
NCORES = 8
TOK = 2048
NT = 16
D = 1024
DFF = 2816
EPS = 1e-6
NEG = -30000.0
SCALE = 1.0 / math.sqrt(128.0)


class Buf:
    __slots__ = ("name", "writers", "readers", "dsem", "dcnt")

    def __init__(self, name):
        self.name = name
        self.writers = {}
        self.readers = {}
        self.dsem = None
        self.dcnt = 0


def _tkey(t):
    return ("eng", t[1]) if t[0] == "eng" else ("dma", id(t[1]))


def _tadd(d, t):
    kx = _tkey(t)
    if kx not in d or d[kx][2] < t[2]:
        d[kx] = t


class KB:
    def __init__(self):
        self.nc = bass.Bass("TRN2", target_bir_lowering=False)
        self.es = ExitStack()
        self.q = {e: [] for e in ENGS}
        self.cnt = {e: 0 for e in ENGS}
        self.waited = {}
        self.psem = {e: self.es.enter_context(self.nc.semaphore("prog_" + e)) for e in ENGS}
        self.dma_toks = {}

    def sb(self, name, shape, dt):
        return self.es.enter_context(self.nc.sbuf_tensor(name, list(shape), dt))

    def ps(self, name, shape, dt):
        return self.es.enter_context(self.nc.psum_tensor(name, list(shape), dt))

    def dram(self, name, shape, dt, kind):
        return self.nc.dram_tensor(name, list(shape), dt, kind=kind).ap()

    def newsem(self, name):
        self.nsem = getattr(self, "nsem", 0) + 1
        return self.es.enter_context(self.nc.semaphore("%s_%d" % (name, self.nsem)))

    def _deps(self, reads, writes):
        toks = []
        for b in reads:
            toks += list(b.writers.values())
        for b in writes:
            toks += list(b.writers.values())
            toks += list(b.readers.values())
        return toks

    def _emit_waits(self, e, toks):
        need = {}
        for t in toks:
            if t[0] == "eng":
                _, te, idx = t
                if te == e and e == "pe":
                    continue
                key = ("eng", te)
                sem = self.psem[te]
                val = idx
            else:
                _, sem, val = t
                key = ("dma", id(sem))
            if self.waited.get((e, key), 0) >= val:
                continue
            if key not in need or need[key][1] < val:
                need[key] = (sem, val)
        for key, (sem, val) in need.items():
            self.waited[(e, key)] = val
            self.q[e].append(lambda eng, sem=sem, val=val: eng.wait_ge(sem, val))

    def _commit(self, tok, reads, writes):
        for b in writes:
            b.writers = {_tkey(tok): tok}
            b.readers = {}
        for b in reads:
            _tadd(b.readers, tok)

    def op(self, e, fn, reads=(), writes=()):
        reads = list(reads)
        writes = list(writes)
        self._emit_waits(e, self._deps(reads, writes))
        self.cnt[e] += 1
        idx = self.cnt[e]
        sem = self.psem[e]
        self.q[e].append(lambda eng, fn=fn, sem=sem: fn(eng).then_inc(sem, 1))
        tok = ("eng", e, idx)
        self._commit(tok, reads, writes)
        return tok

    def dma(self, e, out, in_, reads=(), writes=(), sembuf=None, **kw):
        reads = list(reads)
        writes = list(writes)
        sb_ = sembuf or (writes[0] if writes else reads[0])
        if sb_.dsem is None:
            sb_.dsem = self.newsem("d_" + sb_.name)
        self._emit_waits(e, self._deps(reads, writes))
        sb_.dcnt += 16
        val = sb_.dcnt
        sem = sb_.dsem
        self.q[e].append(lambda eng, out=out, in_=in_, sem=sem, kw=kw:
                         eng.dma_start(out=out, in_=in_, **kw).then_inc(sem, 16))
        tok = ("dma", sem, val)
        _tadd(self.dma_toks, tok)
        self._commit(tok, reads, writes)
        return tok

    def wait_bufs(self, e, bufs):
        toks = []
        for b in bufs:
            toks += list(b.writers.values()) + list(b.readers.values())
        self._emit_waits(e, toks)

    def barrier(self):
        toks = [("eng", e, self.cnt[e]) for e in ENGS if self.cnt[e] > 0]
        toks += list(self.dma_toks.values())
        for e in ENGS:
            self._emit_waits(e, toks)

    def finish(self):
        nc = self.nc
        q = self.q
        with nc.Block() as block:
            @block.tensor
            def _(eng):
                for f in q["pe"]:
                    f(eng)

            @block.scalar
            def _(eng):
                for f in q["act"]:
                    f(eng)

            @block.vector
            def _(eng):
                for f in q["dve"]:
                    f(eng)

            @block.gpsimd
            def _(eng):
                for f in q["pool"]:
                    f(eng)

            @block.sync
            def _(eng):
                for f in q["sp"]:
                    f(eng)
        self.es.close()
        return nc


class Tile:
    def __init__(self, ap, name, nbuf=1):
        self.ap = ap
        self.b = Buf(name)
        self.bs = [self.b] if nbuf == 1 else [Buf("%s_%d" % (name, i)) for i in range(nbuf)]

    def __getitem__(self, key):
        return self.ap[key]


class Arena:
    def __init__(self, k, nbytes):
        self.k = k
        self.t = k.sb("arena", [128, nbytes // 4], F32)
        self.off = 0
        self.nbytes = nbytes

    def alloc(self, name, free_shape, dt, nbuf=1):
        esz = 4 if dt in (F32, I32) else 2
        n = int(np.prod(free_shape)) * esz
        n = (n + 31) // 32 * 32
        assert self.off + n <= self.nbytes, (name, self.off, n, self.nbytes)
        v = self.t[:, self.off // 4:(self.off + n) // 4]
        if dt != F32:
            v = v.bitcast(dt)
        tot = int(np.prod(free_shape))
        v = v[:, 0:tot]
        if len(free_shape) == 2:
            v = v.rearrange("p (a b) -> p a b", a=free_shape[0])
        elif len(free_shape) == 3:
            v = v.rearrange("p (a b c) -> p a b c", a=free_shape[0], b=free_shape[1])
        elif len(free_shape) == 4:
            v = v.rearrange("p (a b c d) -> p a b c d", a=free_shape[0], b=free_shape[1], c=free_shape[2])
        self.off += n
        return Tile(v, name, nbuf)

    def mark(self):
        return self.off

    def seek(self, off):
        self.off = off

    def release(self, m):
        self.off = m


def group_tiles(g):
    if g == 0:
        return [(TOK - 128, 1)], [(TOK + 128 * n, 1) for n in range(16)]
    if g == 1:
        return ([(TOK - 512 + r, 4) for r in range(4)],
                [(TOK + 512 * n + r, 4) for r in range(4) for n in range(4)])
    return [(r, 16) for r in range(16)], [(TOK + r, 16) for r in range(16)]


ROPE_COL0 = {}
_c = 0
for _g in range(3):
    _h, _o = group_tiles(_g)
    ROPE_COL0[_g] = (_c, _c + len(_h))
    _c += len(_h) + len(_o)
NROPE = _c


def _cw_consts():
    two_pi = 2.0 * math.pi
    c1 = 6.28125
    r = two_pi - c1
    c2 = float(np.float32(r))
    m, ex = math.frexp(r)
    c2 = math.ldexp(round(m * 4096) / 4096.0, ex)
    c3 = float(np.float32(two_pi - c1 - c2))
    return c1, c2, c3


def build(stage="full"):
    k = KB()
    nc = k.nc
    dbg = stage != "full"

    def din(name, shape, dt=F32):
        return k.dram(name, shape, dt, "ExternalInput")

    x_own = din("x_own", [TOK, D])
    x_prev = din("x_prev", [TOK, D])
    p_own = din("p_own", [TOK, 256])
    pos_tab = din("pos_tab", [128, NROPE], I32)
    invf_d = din("invf", [128, 16])
    halo_bias_d = din("halo_bias", [128, 1])
    onehot_d = din("onehot", [128, 8])
    g_mix_d = din("g_mix", [1, D])
    g_ffn_d = din("g_ffn", [1, D])
    g_final_d = din("g_final", [1, D])
    w_in = din("w_in", [D, 7168])
    w_ap = din("w_attn_proj", [512, D])
    w_ga = din("w_glu_a", [512, D])
    w_gb = din("w_glu_b", [512, D])
    w_out = din("w_out", [D, D])
    w_fg = din("w_ffn_gate", [D, DFF])
    w_fu = din("w_ffn_up", [D, DFF])
    w_fd = din("w_ffn_down", [DFF, D])
    w_pg = din("w_ple_gate", [D, D])
    w_pp = din("w_ple_proj", [256, D])
    lr2_d = din("lr2", [128, 16])
    li2_d = din("li2", [128, 16])
    ldt2_d = din("ldt2", [128, 16])
    b2re_d = din("b2re", [128, 16, 16])
    b2im_d = din("b2im", [128, 16, 16])
    c2re_d = din("c2re", [128, 16, 16])
    c2im_d = din("c2im", [128, 16, 16])
    dcol_d = din("dcol", [128, 32])
    out_d = k.dram("out", [TOK, D], F32, "ExternalOutput")
    dbg_d = k.dram("dbg", [512, TOK], F32, "ExternalOutput") if dbg else None

    A = Arena(k, 204800)
    ps_big = Tile(k.ps("ps_big", [128, 1024], F32)[:, :], "ps_big")
    ps_pool = [Tile(k.ps("ps%d" % i, [128, 512], F32)[:, :], "ps%d" % i) for i in range(6)]
    ps_i = [0]

    def psum():
        t = ps_pool[ps_i[0] % len(ps_pool)]
        ps_i[0] += 1
        return t

    rr = {"ev": 0}

    def ev_eng():
        rr["ev"] += 1
        return "act" if rr["ev"] % 2 else "dve"

    def copy(eng, out, in_, reads, writes):
        if eng == "act":
            k.op("act", lambda e: e.copy(out, in_), reads, writes)
        else:
            k.op(eng, lambda e: e.tensor_copy(out, in_), reads, writes)

    def mm(out, lhsT, rhs, start, stop, reads, writes):
        k.op("pe", lambda e: e.matmul(out, lhsT=lhsT, rhs=rhs, start=start, stop=stop), reads, writes)

    identf = A.alloc("identf", [128], F32)
    ident = A.alloc("ident", [128], BF16)
    k.op("pool", lambda e: e.memset(identf.ap, 1.0), writes=[identf.b])
    k.op("pool", lambda e: e.affine_select(identf.ap, identf.ap, pattern=[[-1, 128]], compare_op=ALU.is_equal,
                                           fill=0.0, base=0, channel_multiplier=1),
         reads=[identf.b], writes=[identf.b])
    k.op("dve", lambda e: e.tensor_copy(ident.ap, identf.ap), reads=[identf.b], writes=[ident.b])

    gb = {}
    t = A.alloc("g_mix", [D], F32)
    k.dma("sp", t.ap, g_mix_d[0:1, :].to_broadcast([128, D]), writes=[t.b])
    gb["g_mix"] = t

    stg = {}

    def alloc_staging(xw=D):
        stg["xts"] = [A.alloc("xt%d" % i, [xw], F32) for i in range(3)]
        stg["xns"] = [A.alloc("xn%d" % i, [D], BF16) for i in range(2)]
        stg["junk"] = A.alloc("junk", [D], BF16)
        stg["stat"] = A.alloc("stat", [64, 4], F32)

    stat_i = [0]

    def build_T(src_fn, ntiles, dstT, dst_col0, gain, norm=True, ncol=D):
        xns, junk, stat = stg["xns"], stg["junk"], stg["stat"]
        nk = ncol // 128
        for t in range(ntiles):
            src_ap, src_bufs = src_fn(t)
            xn = xns[t % 2]
            if norm:
                si = stat_i[0] % 64
                stat_i[0] += 1
                st = stat
                k.op("act", lambda e, s=src_ap, si=si: e.activation(junk[:, 0:ncol], s, AF.Square,
                                                                      accum_out=st[:, si, 0:1]),
                     reads=src_bufs, writes=[junk.b, st.b])
                k.op("dve", lambda e, si=si: e.tensor_scalar(st[:, si, 1:2], st[:, si, 0:1], 1.0 / ncol, EPS,
                                                              op0=ALU.mult, op1=ALU.add),
                     reads=[st.b], writes=[st.b])
                k.op("act", lambda e, si=si: e.activation(st[:, si, 2:3], st[:, si, 1:2], AF.Sqrt),
                     reads=[st.b], writes=[st.b])
                k.op("dve", lambda e, si=si: e.reciprocal(st[:, si, 3:4], st[:, si, 2:3]),
                     reads=[st.b], writes=[st.b])
                k.op("dve", lambda e, s=src_ap, si=si, xn=xn: e.scalar_tensor_tensor(
                    xn[:, 0:ncol], s, st[:, si, 3:4], gain[:, 0:ncol], op0=ALU.mult, op1=ALU.mult),
                     reads=src_bufs + [st.b, gain.b], writes=[xn.b])
            else:
                k.op("dve", lambda e, s=src_ap, xn=xn: e.tensor_copy(xn[:, 0:ncol], s),
                     reads=src_bufs, writes=[xn.b])
            pt = psum()
            ptv = pt.ap.bitcast(BF16)
            for kk in range(nk):
                k.op("pe", lambda e, kk=kk, xn=xn, ptv=ptv: e.transpose(
                    ptv[:, kk * 128:(kk + 1) * 128], xn[:, kk * 128:(kk + 1) * 128], ident.ap),
                     reads=[xn.b, ident.b], writes=[pt.b])
            c0 = dst_col0 + t * 128
            copy(ev_eng(), dstT[:, 0:nk, c0:c0 + 128],
                 ptv[:, 0:nk * 128].rearrange("p (a b) -> p a b", a=nk),
                 [pt.b], [dstT.bs[(c0 // 128) % len(dstT.bs)]])

    R1 = A.alloc("R1", [8, TOK], BF16, nbuf=16)
    R2_off = A.mark()
    R2 = A.alloc("R2", [8, TOK], BF16, nbuf=16)
    X0 = A.mark()

    def x_src(dram):
        def fn(t):
            xt = stg["xts"][t % 3]
            k.dma("sp", xt.ap, dram[t * 128:(t + 1) * 128, :], writes=[xt.b])
            return xt.ap, [xt.b]
        return fn

    def load_w(dst_tile, dst_ap, src_ap, nk):
        k.dma("pool", dst_ap, src_ap.rearrange("(a p) n -> p a n", p=128), writes=[dst_tile.b])

    need_mix = stage in ("full", "attn", "ssm", "mix", "tailmix")
    attnT = None
    gyT = None
    if need_mix:
        attnT = A.alloc("attnT", [4, TOK], BF16, nbuf=16)
        gyT = A.alloc("gyT", [4, TOK], BF16, nbuf=4)
    X1 = A.mark()

    A.seek(X1)
    alloc_staging()
    build_T(x_src(x_own), NT, R1, 0, gb["g_mix"])
    if need_mix and stage != "ssm":
        build_T(x_src(x_prev), NT, R2, 0, gb["g_mix"])
    k.barrier()

    def dump_T(src, bufs):
        A.seek(X1)
        for j in range(4):
            t = A.alloc("dbgf%d" % j, [TOK], F32)
            k.op("dve", lambda e, j=j, t=t: e.tensor_copy(t.ap, src[:, j, :]), reads=bufs, writes=[t.b])
            k.dma("sp", dbg_d[j * 128:(j + 1) * 128, :], t.ap, reads=[t.b])
            k.wait_bufs("sp", [t.b])
        k.barrier()

    if stage in ("full", "attn", "mix"):
        A.seek(X0 + 16 * 1024)
        attention_phase(k, A, locals())
        k.barrier()
        if stage == "attn":
            dump_T(attnT, attnT.bs)

    if stage in ("full", "ssm", "mix"):
        A.seek(X1)
        ssm_phase(k, A, locals())
        k.barrier()
        if stage == "ssm":
            dump_T(gyT, gyT.bs)

    if stage == "tailmix":
        k.op("dve", lambda e: e.memset(attnT.ap, 0.01), writes=attnT.bs)
        k.op("dve", lambda e: e.memset(gyT.ap, 0.01), writes=gyT.bs)
    if stage in ("full", "ffn", "mix", "tailmix"):
        tail_phase(k, A, locals(), with_mix=(stage != "ffn"))

    k.barrier()
    return k.finish()


def attention_phase(k, A, L):
    ident, R1, R2, attnT = L["ident"], L["R1"], L["R2"], L["attnT"]
    psum, ps_big, copy, mm, ev_eng = L["psum"], L["ps_big"], L["copy"], L["mm"], L["ev_eng"]
    w_in, pos_tab, invf_d, halo_bias_d = L["w_in"], L["pos_tab"], L["invf_d"], L["halo_bias_d"]
    load_w = L["load_w"]

    sint = A.alloc("sint", [NROPE, 16], F32)
    cost = A.alloc("cost", [NROPE, 16], F32)
    mask_std = A.alloc("mask_std", [256], F32)
    mask_first = A.alloc("mask_first", [256], F32)
    hb = A.alloc("hb", [1], F32)
    B1 = [A.alloc("B1_%d" % i, [4 + 512], BF16) for i in range(2)]
    B2 = [A.alloc("B2_%d" % i, [16 + 2048], BF16) for i in range(2)]
    oext = {1: A.alloc("oext1", [16, 520], BF16, nbuf=16), 2: A.alloc("oext2", [16, 520], BF16, nbuf=16)}
    wq = [A.alloc("wqkv%d" % i, [8, 512], BF16) for i in range(3)]
    m_work = A.mark()

    posi = A.alloc("posi", [NROPE], I32)
    posf = A.alloc("posf", [NROPE], F32)
    invf = A.alloc("invf", [16], F32)
    ang = A.alloc("ang", [NROPE, 16], F32)
    kf = A.alloc("kf", [NROPE, 16], F32)
    ki = A.alloc("ki", [NROPE, 16], I32)
    red = A.alloc("red", [NROPE, 16], F32)
    k.dma("sp", posi.ap, pos_tab, writes=[posi.b])
    k.dma("sp", invf.ap, invf_d, writes=[invf.b])
    k.op("dve", lambda e: e.tensor_copy(posf.ap, posi.ap), reads=[posi.b], writes=[posf.b])
    for j in range(16):
        k.op("dve", lambda e, j=j: e.tensor_scalar(ang[:, :, j], posf.ap, invf[:, j:j + 1], None, op0=ALU.mult),
             reads=[posf.b, invf.b], writes=[ang.b])
    c1, c2, c3 = _cw_consts()
    angf = ang.ap.rearrange("p a b -> p (a b)")
    kff = kf.ap.rearrange("p a b -> p (a b)")
    kif = ki.ap.rearrange("p a b -> p (a b)")
    redf = red.ap.rearrange("p a b -> p (a b)")
    sinf = sint.ap.rearrange("p a b -> p (a b)")
    cosf = cost.ap.rearrange("p a b -> p (a b)")
    k.op("dve", lambda e: e.tensor_scalar(kff, angf, 1.0 / (2 * math.pi), None, op0=ALU.mult),
         reads=[ang.b], writes=[kf.b])
    k.op("dve", lambda e: e.tensor_copy(kif, kff), reads=[kf.b], writes=[ki.b])
    k.op("dve", lambda e: e.tensor_copy(kff, kif), reads=[ki.b], writes=[kf.b])
    TWO_PI = 2 * math.pi

    def stt_(out, in0, sc, in1, rd, wr):
        k.op("dve", lambda e: e.scalar_tensor_tensor(out, in0, sc, in1, op0=ALU.mult, op1=ALU.add), reads=rd, writes=wr)

    stt_(redf, kff, -c1, angf, [kf.b, ang.b], [red.b])
    stt_(redf, kff, -c2, redf, [kf.b, red.b], [red.b])
    stt_(redf, kff, -c3, redf, [kf.b, red.b], [red.b])

    def wrap(dst, dstb, shift):
        k.op("dve", lambda e: e.tensor_scalar(dst, redf, float(shift), None, op0=ALU.add), reads=[red.b], writes=[dstb])
        k.op("dve", lambda e: e.tensor_scalar(kff, dst, math.pi, None, op0=ALU.is_gt), reads=[dstb, kf.b], writes=[kf.b])
        stt_(dst, kff, -TWO_PI, dst, [kf.b, dstb], [dstb])
        k.op("dve", lambda e: e.tensor_scalar(kff, dst, -math.pi, None, op0=ALU.is_lt), reads=[dstb, kf.b], writes=[kf.b])
        stt_(dst, kff, TWO_PI, dst, [kf.b, dstb], [dstb])

    wrap(angf, ang.b, 0.0)
    k.op("act", lambda e: e.activation(sinf, angf, AF.Sin), reads=[ang.b], writes=[sint.b])
    wrap(angf, ang.b, math.pi / 2)
    k.op("act", lambda e: e.activation(cosf, angf, AF.Sin), reads=[ang.b], writes=[cost.b])

    k.dma("sp", hb.ap, halo_bias_d, writes=[hb.b])
    k.op("pool", lambda e: e.memset(mask_std.ap, 0.0), writes=[mask_std.b])
    k.op("pool", lambda e: e.affine_select(mask_std.ap, mask_std.ap, pattern=[[1, 256]], compare_op=ALU.is_ge,
                                           fill=NEG, base=0, channel_multiplier=-1),
         reads=[mask_std.b], writes=[mask_std.b])
    k.op("pool", lambda e: e.affine_select(mask_std.ap, mask_std.ap, pattern=[[-1, 256]], compare_op=ALU.is_ge,
                                           fill=NEG, base=128, channel_multiplier=1),
         reads=[mask_std.b], writes=[mask_std.b])
    k.op("dve", lambda e: e.tensor_copy(mask_first[:, 128:256], mask_std[:, 128:256]),
         reads=[mask_std.b], writes=[mask_first.b])
    k.op("dve", lambda e: e.tensor_scalar(mask_first[:, 0:128], mask_std[:, 0:128], hb[:, 0:1], None, op0=ALU.add),
         reads=[mask_std.b, hb.b, mask_first.b], writes=[mask_first.b])

    Bf = A.alloc("Bf", [16 + 2048], F32)
    for (Bt, mult, pads, width) in ((B1, 4, (3, 4), 516), (B2, 16, (15, 16), 2064)):
        for i, pad in enumerate(pads):
            k.op("pool", lambda e, width=width: e.memset(Bf[:, 0:width], 1.0), reads=[Bf.b], writes=[Bf.b])
            k.op("pool", lambda e, width=width, pad=pad, mult=mult: e.affine_select(
                Bf[:, 0:width], Bf[:, 0:width], pattern=[[1, width]], compare_op=ALU.is_equal,
                fill=0.0, base=-pad, channel_multiplier=-mult), reads=[Bf.b], writes=[Bf.b])
            k.op("dve", lambda e, Bt=Bt, i=i, width=width: e.tensor_copy(Bt[i].ap, Bf[:, 0:width]),
                 reads=[Bf.b], writes=[Bt[i].b])
    k.barrier()
    A.seek(m_work)

    qsb = [A.alloc("qsb%d" % i, [512], BF16) for i in range(2)]
    ksb = [A.alloc("ksb%d" % i, [512], BF16) for i in range(2)]
    qT = [A.alloc("qT%d" % i, [512], BF16) for i in range(2)]
    kT = [A.alloc("kT%d" % i, [512], BF16) for i in range(3)]
    vv = [A.alloc("vv%d" % i, [512], BF16) for i in range(3)]
    sm = [A.alloc("sm%d" % i, [4, 256], F32) for i in range(2)]
    Pm = [A.alloc("P%d" % i, [4, 256], BF16) for i in range(2)]
    PT = [A.alloc("PT%d" % i, [8, 128], BF16) for i in range(2)]
    rt = [A.alloc("rt%d" % i, [8, 16], F32) for i in range(2)]
    stt = [A.alloc("stt%d" % i, [8, 4], F32) for i in range(2)]
    o0 = [A.alloc("o0_0", [512], F32)] * 2
    lse0 = [A.alloc("lse0_%d" % i, [4], F32) for i in range(2)]
    mg = [A.alloc("mg%d" % i, [8, 16], F32) for i in range(2)]
    acc = [A.alloc("acc0", [512], F32)] * 2
    attn_tok = [A.alloc("attn_tok%d" % i, [512], BF16) for i in range(2)]

    def ncols(tile, kk):
        s0, d = tile
        if s0 >= TOK:
            return R1[:, kk, ds(s0 - TOK, 128, d)], R1.bs
        return R2[:, kk, ds(s0, 128, d)], R2.bs

    def proj(tile, w):
        pt = psum()
        for kk in range(8):
            lhs, lb = ncols(tile, kk)
            mm(pt.ap, lhs, w[:, kk, :], kk == 0, kk == 7, lb + [w.b], [pt.b])
        return pt

    def rope(pt, dst, col, par):
        copy("act", dst.ap, pt.ap, [pt.b], [dst.b])
        pv = pt.ap.rearrange("p (h d) -> p h d", h=4)
        dv = dst.ap.rearrange("p (h d) -> p h d", h=4)
        x1 = pv[:, :, 0:16]
        x2 = pv[:, :, 16:32]
        cb = L_cost[:, col, :].unsqueeze(1).to_broadcast([128, 4, 16])
        sb_ = L_sint[:, col, :].unsqueeze(1).to_broadcast([128, 4, 16])
        r = rt[par]
        t1 = r[:, 0:4, :]
        t2 = r[:, 4:8, :]
        rd = [pt.b, cost.b, sint.b, dst.b]
        k.op("dve", lambda e: e.tensor_tensor(t1, x1, cb, ALU.mult), reads=rd, writes=[r.b])
        k.op("dve", lambda e: e.tensor_tensor(t2, x2, sb_, ALU.mult), reads=rd + [r.b], writes=[r.b])
        k.op("dve", lambda e: e.tensor_tensor(dv[:, :, 0:16], t1, t2, ALU.subtract), reads=[r.b, dst.b], writes=[dst.b])
        k.op("dve", lambda e: e.tensor_tensor(t1, x2, cb, ALU.mult), reads=rd + [r.b, dst.b], writes=[r.b])
        k.op("dve", lambda e: e.tensor_tensor(t2, x1, sb_, ALU.mult), reads=rd + [r.b], writes=[r.b])
        k.op("dve", lambda e: e.tensor_tensor(dv[:, :, 16:32], t1, t2, ALU.add), reads=[r.b, dst.b], writes=[dst.b])

    L_cost, L_sint = cost.ap, sint.ap

    def transpose4(src, dst):
        pt = psum()
        ptv = pt.ap.bitcast(BF16)
        for h in range(4):
            k.op("pe", lambda e, h=h: e.transpose(ptv[:, h * 128:(h + 1) * 128], src[:, h * 128:(h + 1) * 128], ident.ap),
                 reads=[src.b, ident.b], writes=[pt.b])
        copy(ev_eng(), dst.ap, ptv[:, 0:512], [pt.b], [dst.b])

    unit_ctr = [0]

    def kv_tile(tile, col, slot, wk, wv, par):
        pk = proj(tile, wk)
        rope(pk, ksb[par], col, par)
        transpose4(ksb[par], kT[slot])
        pv = proj(tile, wv)
        copy(ev_eng(), vv[slot].ap, pv.ap, [pv.b], [vv[slot].b])

    CUT = int(os.environ.get("K_CUT", "99"))

    def qpart(tile, col, wq_, par):
        pq = proj(tile, wq_)
        rope(pq, qsb[par], col, par)
        transpose4(qsb[par], qT[par])

    def attend(prev_slot, cur_slot, first, g, uidx, par):
        sv = ps_big.ap.rearrange("p (h c) -> p h c", h=4)
        for h in range(4):
            hs = slice(h * 128, (h + 1) * 128)
            mm(sv[:, h, 0:128], qT[par][:, hs], kT[prev_slot][:, hs], True, True,
               [qT[par].b, kT[prev_slot].b], [ps_big.b])
            mm(sv[:, h, 128:256], qT[par][:, hs], kT[cur_slot][:, hs], True, True,
               [qT[par].b, kT[cur_slot].b], [ps_big.b])
        if CUT <= 2:
            return par
        mk = mask_first if first else mask_std
        s_ = sm[par]
        st_ = stt[par]
        k.op("dve", lambda e: e.tensor_tensor(s_.ap, sv, mk.ap.unsqueeze(1).to_broadcast([128, 4, 256]), ALU.add),
             reads=[ps_big.b, mk.b], writes=[s_.b])
        k.op("dve", lambda e: e.tensor_reduce(st_[:, 0, :], s_.ap, axis=AX.X, op=ALU.max), reads=[s_.b], writes=[st_.b])
        k.op("dve", lambda e: e.tensor_scalar(st_[:, 1, :], st_[:, 0, :], -SCALE, None, op0=ALU.mult),
             reads=[st_.b], writes=[st_.b])
        P_ = Pm[par]
        for h in range(4):
            k.op("act", lambda e, h=h: e.activation(P_[:, h, :], s_[:, h, :], AF.Exp, bias=st_[:, 1, h:h + 1],
                                                    scale=SCALE, accum_out=st_[:, 2, h:h + 1]),
                 reads=[s_.b, st_.b], writes=[P_.b, st_.b])
        if CUT <= 3:
            return par
        pt = psum()
        ptv = pt.ap.bitcast(BF16)
        for h in range(4):
            for half in range(2):
                j = h * 2 + half
                k.op("pe", lambda e, h=h, half=half, j=j: e.transpose(
                    ptv[:, j * 128:(j + 1) * 128], P_[:, h, half * 128:(half + 1) * 128], ident.ap),
                     reads=[P_.b, ident.b], writes=[pt.b])
        PT_ = PT[par]
        copy(ev_eng(), PT_.ap, ptv.rearrange("p (a b) -> p a b", a=8), [pt.b], [PT_.b])
        po = psum()
        for h in range(4):
            hs = slice(h * 128, (h + 1) * 128)
            mm(po[:, hs], PT_[:, 2 * h, :], vv[prev_slot][:, hs], True, False, [PT_.b, vv[prev_slot].b], [po.b])
            mm(po[:, hs], PT_[:, 2 * h + 1, :], vv[cur_slot][:, hs], False, True, [PT_.b, vv[cur_slot].b], [po.b])
        if CUT <= 4:
            return par
        k.op("dve", lambda e: e.reciprocal(st_[:, 3, :], st_[:, 2, :]), reads=[st_.b], writes=[st_.b])
        k.op("act", lambda e: e.activation(st_[:, 4, :], st_[:, 2, :], AF.Ln), reads=[st_.b], writes=[st_.b])
        rden_b = st_[:, 3, :].unsqueeze(2).to_broadcast([128, 4, 128])
        pov = po.ap.rearrange("p (h d) -> p h d", h=4)
        if g == 0:
            o_ = o0[par]
            k.op("dve", lambda e: e.tensor_tensor(o_.ap.rearrange("p (h d) -> p h d", h=4), pov, rden_b, ALU.mult),
                 reads=[po.b, st_.b], writes=[o_.b])
            k.op("dve", lambda e: e.tensor_tensor(lse0[par].ap, st_[:, 4, :], st_[:, 1, :], ALU.subtract),
                 reads=[st_.b], writes=[lse0[par].b])
        else:
            oe = oext[g]
            ob = oe.bs[uidx]
            k.op("dve", lambda e: e.tensor_tensor(oe[:, uidx, 0:512].rearrange("p (h d) -> p h d", h=4), pov, rden_b,
                                                  ALU.mult),
                 reads=[po.b, st_.b], writes=[ob])
            k.op("dve", lambda e: e.tensor_tensor(st_[:, 5, :], st_[:, 4, :], st_[:, 1, :], ALU.subtract),
                 reads=[st_.b], writes=[st_.b])
            k.op("dve", lambda e: e.tensor_copy(oe[:, uidx, 512:516], st_[:, 5, :]), reads=[st_.b, ob], writes=[ob])
            k.op("dve", lambda e: e.tensor_copy(st_[:, 6, :], oe[:, uidx, 512:516]), reads=[ob, st_.b], writes=[st_.b])
            k.op("dve", lambda e: e.tensor_tensor(oe[:, uidx, 516:520], st_[:, 5, :], st_[:, 6, :], ALU.subtract),
                 reads=[st_.b, ob], writes=[ob])
        return par

    def merge(T, par):
        n4, q4 = T // 4, T % 4
        p1 = psum()
        p2 = psum()
        pl = psum()
        def b1v(r):
            bt = B1[0] if r % 2 == 1 else B1[1]
            pad = 3 if r % 2 == 1 else 4
            o_ = pad + 128 * q4 - r
            assert o_ % 2 == 0
            return bt[:, o_:o_ + 128], bt.b

        def b2v(j):
            bt = B2[0] if j % 2 == 1 else B2[1]
            pad = 15 if j % 2 == 1 else 16
            o_ = pad + 128 * T - j
            assert o_ % 2 == 0
            return bt[:, o_:o_ + 128], bt.b

        for r in range(4):
            u = r * 4 + n4
            lhs, lb = b1v(r)
            mm(p1.ap, lhs, oext[1][:, u, 0:512], r == 0, r == 3, [lb, oext[1].bs[u]], [p1.b])
        for r in range(4):
            u = r * 4 + n4
            lhs, lb = b1v(r)
            mm(pl[:, 0:8], lhs, oext[1][:, u, 512:520], r == 0, r == 3, [lb, oext[1].bs[u]], [pl.b])
        for j in range(16):
            lhs, lb = b2v(j)
            mm(p2.ap, lhs, oext[2][:, j, 0:512], j == 0, j == 15, [lb, oext[2].bs[j]], [p2.b])
        for j in range(16):
            lhs, lb = b2v(j)
            mm(pl[:, 8:16], lhs, oext[2][:, j, 512:520], j == 0, j == 15, [lb, oext[2].bs[j]], [pl.b])
        m_ = mg[par]
        lv = m_[:, 0, 0:12].rearrange("p (g h) -> p g h", g=3)
        k.op("dve", lambda e: e.tensor_copy(m_[:, 6, :], pl[:, 0:16]), reads=[pl.b, m_.b], writes=[m_.b])
        k.op("dve", lambda e: e.tensor_copy(lv[:, 0, :], lse0[par].ap), reads=[lse0[par].b, m_.b], writes=[m_.b])
        k.op("dve", lambda e: e.tensor_tensor(lv[:, 1, :], m_[:, 6, 0:4], m_[:, 6, 4:8], ALU.add), reads=[m_.b], writes=[m_.b])
        k.op("dve", lambda e: e.tensor_tensor(lv[:, 2, :], m_[:, 6, 8:12], m_[:, 6, 12:16], ALU.add), reads=[m_.b], writes=[m_.b])
        mx = m_[:, 1, 0:4]
        k.op("dve", lambda e: e.tensor_tensor(mx, lv[:, 0, :], lv[:, 1, :], ALU.max), reads=[m_.b], writes=[m_.b])
        k.op("dve", lambda e: e.tensor_tensor(mx, mx, lv[:, 2, :], ALU.max), reads=[m_.b], writes=[m_.b])
        ev = m_[:, 2, 0:12].rearrange("p (g h) -> p g h", g=3)
        k.op("dve", lambda e: e.tensor_tensor(ev, lv, mx.unsqueeze(1).to_broadcast([128, 3, 4]), ALU.subtract),
             reads=[m_.b], writes=[m_.b])
        k.op("act", lambda e: e.activation(m_[:, 3, 0:12], m_[:, 2, 0:12], AF.Exp), reads=[m_.b], writes=[m_.b])
        e3 = m_[:, 3, 0:12].rearrange("p (g h) -> p g h", g=3)
        sm_ = m_[:, 4, 0:4]
        k.op("dve", lambda e: e.tensor_tensor(sm_, e3[:, 0, :], e3[:, 1, :], ALU.add), reads=[m_.b], writes=[m_.b])
        k.op("dve", lambda e: e.tensor_tensor(sm_, sm_, e3[:, 2, :], ALU.add), reads=[m_.b], writes=[m_.b])
        k.op("dve", lambda e: e.reciprocal(m_[:, 4, 4:8], sm_), reads=[m_.b], writes=[m_.b])
        wv = m_[:, 5, 0:12].rearrange("p (g h) -> p g h", g=3)
        k.op("dve", lambda e: e.tensor_tensor(wv, e3, m_[:, 4, 4:8].unsqueeze(1).to_broadcast([128, 3, 4]), ALU.mult),
             reads=[m_.b], writes=[m_.b])
        a_ = acc[par]
        at = attn_tok[par]
        for h in range(4):
            hs = slice(h * 128, (h + 1) * 128)
            k.op("dve", lambda e, h=h, hs=hs: e.tensor_scalar(a_[:, hs], o0[par][:, hs], wv[:, 0, h:h + 1], None,
                                                             op0=ALU.mult),
                 reads=[o0[par].b, m_.b], writes=[a_.b])
            k.op("dve", lambda e, h=h, hs=hs: e.scalar_tensor_tensor(a_[:, hs], p1[:, hs], wv[:, 1, h:h + 1], a_[:, hs],
                                                                    op0=ALU.mult, op1=ALU.add),
                 reads=[p1.b, m_.b, a_.b], writes=[a_.b])
            k.op("dve", lambda e, h=h, hs=hs: e.scalar_tensor_tensor(at[:, hs], p2[:, hs], wv[:, 2, h:h + 1], a_[:, hs],
                                                                    op0=ALU.mult, op1=ALU.add),
                 reads=[p2.b, m_.b, a_.b], writes=[at.b])
        pt = psum()
        ptv = pt.ap.bitcast(BF16)
        for h in range(4):
            k.op("pe", lambda e, h=h: e.transpose(ptv[:, h * 128:(h + 1) * 128], at[:, h * 128:(h + 1) * 128], ident.ap),
                 reads=[at.b, ident.b], writes=[pt.b])
        copy(ev_eng(), attnT[:, :, T * 128:(T + 1) * 128], ptv[:, 0:512].rearrange("p (a b) -> p a b", a=4),
             [pt.b], [attnT.bs[T]])

    items = []
    for g in (2, 1, 0):
        halo, own = group_tiles(g)
        hc0, oc0 = ROPE_COL0[g]
        nseq = len(halo)
        per = len(own) // nseq
        for s_ in range(nseq):
            items.append(("halo", g, halo[s_], hc0 + s_, len(items) % 3, None, None, None, s_ == 0))
            for n in range(per):
                ui = s_ * per + n
                items.append(("unit", g, own[ui], oc0 + ui, len(items) % 3, (len(items) - 1) % 3, n == 0, ui, False))
    upar = [0]

    def stage1(it):
        kind, g, tile, col, slot, prev, first, ui, newg = it
        if newg:
            for i, c0 in enumerate((g * 512, 1536 + g * 512, 3072 + g * 512)):
                load_w(wq[i], wq[i].ap, w_in[:, c0:c0 + 512], 8)
        par = upar[0] % 2
        upar[0] += 1
        kv_tile(tile, col, slot, wq[1], wq[2], par)
        if kind == "unit":
            qpart(tile, col, wq[0], par)
        return par

    pars = {}
    pars[0] = stage1(items[0])
    for i, it in enumerate(items):
        if i + 1 < len(items):
            pars[i + 1] = stage1(items[i + 1])
        kind, g, tile, col, slot, prev, first, ui, newg = it
        if kind == "unit":
            attend(prev, slot, first, g, ui, pars[i])
            if g == 0:
                merge(ui, pars[i])


def ssm_phase(k, A, L):
    ident, identf, R1, R2, gyT = L["ident"], L["identf"], L["R1"], L["R2"], L["gyT"]
    psum, copy, mm, ev_eng = L["psum"], L["copy"], L["mm"], L["ev_eng"]
    w_in, onehot_d, R2_off = L["w_in"], L["onehot_d"], L["R2_off"]
    nc = k.nc
    NOCC = os.environ.get("K_NOCC", "0") == "1"
    m_top = A.mark()

    def tt(out, a, b, op, rd, wr):
        k.op("dve", lambda e: e.tensor_tensor(out, a, b, op), reads=rd, writes=wr)

    def ts(out, a, s1, op0, rd, wr, s2=None, op1=None):
        if op1 is None:
            k.op("dve", lambda e: e.tensor_scalar(out, a, s1, None, op0=op0), reads=rd, writes=wr)
        else:
            k.op("dve", lambda e: e.tensor_scalar(out, a, s1, s2, op0=op0, op1=op1), reads=rd, writes=wr)

    def stt(out, in0, sc, in1, rd, wr, op0=ALU.mult, op1=ALU.add):
        k.op("dve", lambda e: e.scalar_tensor_tensor(out, in0, sc, in1, op0=op0, op1=op1), reads=rd, writes=wr)

    WinT = A.alloc("WinT", [16, 2, 128], BF16)
    Wout = A.alloc("Wout", [16, 2, 128], BF16)
    Mbf = A.alloc("Mbf", [32, 128], BF16, nbuf=32)
    Sel = A.alloc("Sel", [64, 128], BF16)
    Eb = A.alloc("Eb", [352], BF16)
    sm_ = A.alloc("ssm_small", [90, 16], F32)
    sb_ = [sm_.b]
    names = {}

    def S(name):
        if name not in names:
            names[name] = len(names)
            assert len(names) <= 90
        return sm_[:, names[name], :]

    bb = A.alloc("ssm_bb", [6, 16, 16], F32)
    dcol = A.alloc("dcol", [32], F32)
    oneh = A.alloc("oneh", [8], F32)
    rowmask = A.alloc("rowmask", [8], F32)
    halfpi = A.alloc("halfpi", [1], F32)
    blockmask = A.alloc("blockmask", [128], F32)
    Fs = A.alloc("Fs", [16, 2], F32)
    Hinit = A.alloc("Hinit", [16, 2], F32)
    nHim = A.alloc("nHim", [16], F32)
    G = A.alloc("Ggath", [8, 32], F32)
    m_tmp = A.mark()
    wu = A.alloc("wu", [8, 512], BF16)

    for nm, src in (("lr", L["lr2_d"]), ("li", L["li2_d"]), ("ldt", L["ldt2_d"])):
        k.dma("sp", S(nm), src, writes=sb_)
    for i, src in enumerate((L["b2re_d"], L["b2im_d"], L["c2re_d"], L["c2im_d"])):
        k.dma("sp", bb[:, i, :, :], src, writes=[bb.b])
    k.dma("sp", dcol.ap, L["dcol_d"], writes=[dcol.b])
    k.dma("sp", oneh.ap, onehot_d, writes=[oneh.b])
    k.dma("pool", wu.ap, w_in[:, 4608:5120].rearrange("(a p) n -> p a n", p=128), writes=[wu.b])

    uT = gyT
    for j in range(4):
        for tc in range(4):
            pt = psum()
            ts_ = slice(tc * 512, (tc + 1) * 512)
            for kk in range(8):
                mm(pt.ap, wu[:, kk, j * 128:(j + 1) * 128], R1[:, kk, ts_], kk == 0, kk == 7,
                   [wu.b] + [R1.bs[i] for i in range(tc * 4, tc * 4 + 4)], [pt.b])
            copy(ev_eng(), uT[:, j, ts_], pt.ap, [pt.b], [uT.bs[j]])

    Ef = A.alloc("Ef", [352], F32)
    k.op("pool", lambda e: e.memset(Ef.ap, 1.0), writes=[Ef.b])
    k.op("pool", lambda e: e.affine_select(Ef.ap, Ef.ap, pattern=[[1, 352]], compare_op=ALU.is_equal, fill=0.0,
                                           base=-112, channel_multiplier=-1), reads=[Ef.b], writes=[Ef.b])
    k.op("dve", lambda e: e.tensor_copy(Eb.ap, Ef.ap), reads=[Ef.b], writes=[Eb.b])
    k.op("pool", lambda e: e.memset(rowmask.ap, 1.0), writes=[rowmask.b])
    k.op("pool", lambda e: e.affine_select(rowmask.ap, rowmask.ap, pattern=[[-16, 8]], compare_op=ALU.is_ge, fill=0.0,
                                           base=0, channel_multiplier=1), reads=[rowmask.b], writes=[rowmask.b])
    k.op("pool", lambda e: e.affine_select(rowmask.ap, rowmask.ap, pattern=[[16, 8]], compare_op=ALU.is_ge, fill=0.0,
                                           base=15, channel_multiplier=-1), reads=[rowmask.b], writes=[rowmask.b])
    for a_ in range(8):
        for b_ in range(8):
            o_ = 112 - 16 * (b_ - a_)
            ts(Sel[:, a_ * 8 + b_, :], Eb[:, o_:o_ + 128], rowmask[:, a_:a_ + 1], ALU.mult, [Eb.b, rowmask.b], [Sel.b])
    k.op("pool", lambda e: e.memset(blockmask.ap, 1.0), writes=[blockmask.b])
    k.op("pool", lambda e: e.affine_select(blockmask.ap.rearrange("p (t c) -> p t c", t=8), blockmask.ap.rearrange("p (t c) -> p t c", t=8),
                                           pattern=[[16, 8], [0, 16]], compare_op=ALU.is_ge, fill=0.0,
                                           base=15, channel_multiplier=-1), reads=[blockmask.b], writes=[blockmask.b])
    k.op("dve", lambda e: e.memset(halfpi.ap, math.pi / 2), writes=[halfpi.b])

    def act(out, in_, func, rd, wr, **kw):
        k.op("act", lambda e: e.activation(out, in_, func, **kw), reads=rd, writes=wr)

    act(S("dt"), S("ldt"), AF.Exp, sb_, sb_)
    tt(S("t0"), S("lr"), S("dt"), ALU.mult, sb_, sb_)
    act(S("mag"), S("t0"), AF.Exp, sb_, sb_, scale=1.0 / 16)
    tt(S("t1"), S("li"), S("dt"), ALU.mult, sb_, sb_)
    act(S("sn"), S("t1"), AF.Sin, sb_, sb_, scale=1.0 / 16)
    act(S("cs"), S("t1"), AF.Sin, sb_ + [halfpi.b], sb_, scale=1.0 / 16, bias=halfpi[:, 0:1])
    tt(S("re"), S("mag"), S("cs"), ALU.mult, sb_, sb_)
    tt(S("im"), S("mag"), S("sn"), ALU.mult, sb_, sb_)

    def csquare(ore, oim, ire, iim):
        tt(S("sq_a"), ire, ire, ALU.mult, sb_, sb_)
        tt(S("sq_b"), iim, iim, ALU.mult, sb_, sb_)
        tt(S("sq_c"), ire, iim, ALU.mult, sb_, sb_)
        tt(ore, S("sq_a"), S("sq_b"), ALU.subtract, sb_, sb_)
        ts(oim, S("sq_c"), 2.0, ALU.mult, sb_, sb_)

    def cmul(ore, oim, are, aim, bre, bim):
        tt(S("cm_a"), are, bre, ALU.mult, sb_, sb_)
        tt(S("cm_b"), aim, bim, ALU.mult, sb_, sb_)
        tt(S("cm_c"), are, bim, ALU.mult, sb_, sb_)
        tt(S("cm_d"), aim, bre, ALU.mult, sb_, sb_)
        tt(ore, S("cm_a"), S("cm_b"), ALU.subtract, sb_, sb_)
        tt(oim, S("cm_c"), S("cm_d"), ALU.add, sb_, sb_)

    for i in range(3):
        csquare(S("re"), S("im"), S("re"), S("im"))
    csquare(S("Qr0"), S("Qi0"), S("re"), S("im"))
    for m in range(1, 12):
        csquare(S("Qr%d" % m), S("Qi%d" % m), S("Qr%d" % (m - 1)), S("Qi%d" % (m - 1)))
    k.op("dve", lambda e: e.memset(S("Pr0"), 1.0), reads=sb_, writes=sb_)
    k.op("dve", lambda e: e.memset(S("Pi0"), 0.0), reads=sb_, writes=sb_)
    k.op("dve", lambda e: e.tensor_copy(S("Pr1"), S("Qr0")), reads=sb_, writes=sb_)
    k.op("dve", lambda e: e.tensor_copy(S("Pi1"), S("Qi0")), reads=sb_, writes=sb_)
    for kk in range(2, 9):
        cmul(S("Pr%d" % kk), S("Pi%d" % kk), S("Pr%d" % (kk - 1)), S("Pi%d" % (kk - 1)), S("Qr0"), S("Qi0"))
    for m in range(3, 11):
        ts(S("nQi%d" % m), S("Qi%d" % m), -1.0, ALU.mult, sb_, sb_)
    ts(S("nr"), S("Qr0"), -1.0, ALU.add, sb_, sb_)
    tt(S("z_a"), S("lr"), S("lr"), ALU.mult, sb_, sb_)
    tt(S("z_b"), S("li"), S("li"), ALU.mult, sb_, sb_)
    tt(S("z_a"), S("z_a"), S("z_b"), ALU.add, sb_, sb_)
    k.op("dve", lambda e: e.reciprocal(S("rden"), S("z_a")), reads=sb_, writes=sb_)
    tt(S("z_c"), S("nr"), S("lr"), ALU.mult, sb_, sb_)
    tt(S("z_d"), S("Qi0"), S("li"), ALU.mult, sb_, sb_)
    tt(S("z_c"), S("z_c"), S("z_d"), ALU.add, sb_, sb_)
    tt(S("zre"), S("z_c"), S("rden"), ALU.mult, sb_, sb_)
    tt(S("z_c"), S("Qi0"), S("lr"), ALU.mult, sb_, sb_)
    tt(S("z_d"), S("nr"), S("li"), ALU.mult, sb_, sb_)
    tt(S("z_c"), S("z_c"), S("z_d"), ALU.subtract, sb_, sb_)
    tt(S("zim"), S("z_c"), S("rden"), ALU.mult, sb_, sb_)
    tt(S("z_a"), S("Qr3"), S("Qr3"), ALU.mult, sb_, sb_)
    tt(S("z_b"), S("Qi3"), S("Qi3"), ALU.mult, sb_, sb_)
    tt(S("z_a"), S("z_a"), S("z_b"), ALU.add, sb_, sb_)
    k.op("dve", lambda e: e.reciprocal(S("z_b"), S("z_a")), reads=sb_, writes=sb_)
    tt(S("ir"), S("Qr3"), S("z_b"), ALU.mult, sb_, sb_)
    tt(S("nii"), S("Qi3"), S("z_b"), ALU.mult, sb_, sb_)
    ts(S("ii"), S("nii"), -1.0, ALU.mult, sb_, sb_)

    def bc16(ap):
        return ap.unsqueeze(2).to_broadcast([128, 16, 16])

    def bc128(ap):
        return ap.unsqueeze(2).to_broadcast([128, 16, 128])

    bre, bim, cre, cim, bbre, bbim = (bb[:, i, :, :] for i in range(6))
    tmp3 = A.alloc("tmp3", [2, 16, 16], F32)
    rdb = sb_ + [bb.b, tmp3.b]
    tt(tmp3[:, 0], bre, bc16(S("zre")), ALU.mult, rdb, [tmp3.b])
    tt(tmp3[:, 1], bim, bc16(S("zim")), ALU.mult, rdb, [tmp3.b])
    tt(bbre, tmp3[:, 0], tmp3[:, 1], ALU.subtract, rdb, [bb.b])
    tt(tmp3[:, 0], bim, bc16(S("zre")), ALU.mult, rdb, [tmp3.b])
    tt(tmp3[:, 1], bre, bc16(S("zim")), ALU.mult, rdb, [tmp3.b])
    tt(bbim, tmp3[:, 0], tmp3[:, 1], ALU.add, rdb, [bb.b])

    m_after = A.mark()
    A.seek(R2_off)
    WB = A.alloc("WB", [16, 2, 128], F32)
    WO = A.alloc("WO", [16, 2, 128], F32)
    A.seek(m_after)
    WCp = A.alloc("WCp", [16, 2, 128], F32)
    tmpM = A.alloc("tmpM", [2, 128], F32)
    for sp in range(8):
        kk = 7 - sp
        cs = slice(sp * 16, (sp + 1) * 16)
        pr, pi = bc16(S("Pr%d" % kk)), bc16(S("Pi%d" % kk))
        tt(tmp3[:, 0], bbre, pr, ALU.mult, rdb, [tmp3.b])
        tt(tmp3[:, 1], bbim, pi, ALU.mult, rdb, [tmp3.b])
        tt(WB[:, :, 0, cs], tmp3[:, 0], tmp3[:, 1], ALU.subtract, [tmp3.b], [WB.b])
        tt(tmp3[:, 0], bbim, pr, ALU.mult, rdb, [tmp3.b])
        tt(tmp3[:, 1], bbre, pi, ALU.mult, rdb, [tmp3.b])
        tt(WB[:, :, 1, cs], tmp3[:, 0], tmp3[:, 1], ALU.add, [tmp3.b], [WB.b])
    for tp in range(8):
        kk = tp + 1
        cs = slice(tp * 16, (tp + 1) * 16)
        pr, pi = bc16(S("Pr%d" % kk)), bc16(S("Pi%d" % kk))
        tt(tmp3[:, 0], cre, pr, ALU.mult, rdb, [tmp3.b])
        tt(tmp3[:, 1], cim, pi, ALU.mult, rdb, [tmp3.b])
        tt(WO[:, :, 0, cs], tmp3[:, 0], tmp3[:, 1], ALU.subtract, [tmp3.b], [WO.b])
        tt(tmp3[:, 0], cre, pi, ALU.mult, rdb, [tmp3.b])
        tt(tmp3[:, 1], cim, pr, ALU.mult, rdb, [tmp3.b])
        tt(tmp3[:, 0], tmp3[:, 0], tmp3[:, 1], ALU.add, [tmp3.b], [tmp3.b])
        ts(WO[:, :, 1, cs], tmp3[:, 0], -1.0, ALU.mult, [tmp3.b], [WO.b])
    k.op("dve", lambda e: e.tensor_copy(Wout.ap, WO.ap), reads=[WO.b], writes=[Wout.b])
    tmpW = A.alloc("tmpW", [16, 128], F32)
    rdw = sb_ + [WO.b, tmpW.b]
    tt(WCp[:, :, 0, :], WO[:, :, 0, :], bc128(S("ir")), ALU.mult, rdw, [WCp.b])
    tt(tmpW.ap, WO[:, :, 1, :], bc128(S("ii")), ALU.mult, rdw, [tmpW.b])
    tt(WCp[:, :, 0, :], WCp[:, :, 0, :], tmpW.ap, ALU.add, [WCp.b, tmpW.b], [WCp.b])
    tt(WCp[:, :, 1, :], WO[:, :, 0, :], bc128(S("nii")), ALU.mult, rdw + [WCp.b], [WCp.b])
    tt(tmpW.ap, WO[:, :, 1, :], bc128(S("ir")), ALU.mult, rdw, [tmpW.b])
    tt(WCp[:, :, 1, :], WCp[:, :, 1, :], tmpW.ap, ALU.add, [WCp.b, tmpW.b], [WCp.b])
    for pair in range(16):
        pt = psum()
        for comp in range(2):
            k.op("pe", lambda e, pair=pair, comp=comp, pt=pt: e.transpose(pt[:, comp * 128:(comp + 1) * 128], WB[:, pair, comp, :], identf.ap),
                 reads=[WB.b, identf.b], writes=[pt.b])
        copy(ev_eng(), WinT[:, pair, :, :], pt[:, 0:256].rearrange("p (c q) -> p c q", c=2), [pt.b], [WinT.b])
    for g in range(32):
        pair, j2 = g // 2, g % 2
        ps_ = slice(64 * j2, 64 * j2 + 64)
        pt = psum()
        mm(pt[:, 0:128], WB[ps_, pair, 0, :], WCp[ps_, pair, 0, :], True, False, [WB.b, WCp.b], [pt.b])
        mm(pt[:, 0:128], WB[ps_, pair, 1, :], WCp[ps_, pair, 1, :], False, True, [WB.b, WCp.b], [pt.b])
        tm = tmpM[:, g % 2, :]
        tt(tm, pt[:, 0:128], blockmask.ap, ALU.mult, [pt.b, blockmask.b, tmpM.b], [tmpM.b])
        stt(Mbf[:, g, :], identf.ap, dcol[:, g:g + 1], tm, [identf.b, dcol.b, tmpM.b], [Mbf.bs[g]])
    k.barrier()

    A.seek(m_tmp)
    U = A.alloc("U", [32, 256], BF16, nbuf=32)
    Xp = [[A.alloc("X%d_%d" % (0, j), [2, 256], F32) for j in range(2)]] * 2
    Hb = [A.alloc("Hb%d" % i, [2, 256], BF16) for i in range(2)]
    Tt = A.alloc("Tt", [2, 8, 256], F32)
    tmpT = A.alloc("tmpT", [8, 128], F32)
    A.seek(R2_off)
    Xall = A.alloc("Xall", [16, 2, 256], F32, nbuf=16)

    for j in range(4):
        for g8 in range(0, 8, 2):
            pt = psum()
            for hh in range(2):
                for sp in range(8):
                    mm(pt[:, hh * 256:(hh + 1) * 256], Sel[:, (g8 + hh) * 8 + sp, :], uT[:, j, ds(sp, 256, 8)], sp == 0, sp == 7,
                       [Sel.b, uT.bs[j]], [pt.b])
            g = 8 * j + g8
            copy(ev_eng(), U[:, g:g + 2, :], pt.ap.rearrange("p (a c) -> p a c", a=2), [pt.b], [U.bs[g], U.bs[g + 1]])

    for pair in range(16):
        pt = psum()
        for j2 in range(2):
            for comp in range(2):
                mm(pt[64 * j2:64 * j2 + 64, comp * 256:(comp + 1) * 256], WinT[:, pair, comp, 64 * j2:64 * j2 + 64], U[:, 2 * pair + j2, :],
                   True, True, [WinT.b, U.bs[2 * pair + j2]], [pt.b])
        Xa, Xb = Xp[pair % 2]
        copy("act", Xa.ap, pt.ap.rearrange("p (c n) -> p c n", c=2), [pt.b], [Xa.b])
        src, dst = Xa, Xb
        for lv in range(8):
            sh = 1 << lv
            n = 256 - sh
            er = S("Qr%d" % (3 + lv))[:, pair:pair + 1]
            ei = S("Qi%d" % (3 + lv))[:, pair:pair + 1]
            nei = S("nQi%d" % (3 + lv))[:, pair:pair + 1]
            last = lv == 7
            d_re = Xall[:, pair, 0, :] if last else dst[:, 0, :]
            d_im = Xall[:, pair, 1, :] if last else dst[:, 1, :]
            db = Xall.bs[pair] if last else dst.b
            dfull = Xall[:, pair, :, 0:sh] if last else dst[:, :, 0:sh]
            copy("act", dfull, src[:, :, 0:sh], [src.b], [db])
            rd = sb_ + [src.b, db]
            stt(d_re[:, sh:256], src[:, 0, 0:n], er, src[:, 0, sh:256], rd, [db])
            stt(d_re[:, sh:256], src[:, 1, 0:n], nei, d_re[:, sh:256], rd, [db])
            stt(d_im[:, sh:256], src[:, 1, 0:n], er, src[:, 1, sh:256], rd, [db])
            stt(d_im[:, sh:256], src[:, 0, 0:n], ei, d_im[:, sh:256], rd, [db])
            src, dst = dst, src
        k.op("dve", lambda e, pair=pair: e.tensor_copy(Fs[:, pair, :], Xall[:, pair, :, 255]), reads=[Xall.bs[pair]], writes=[Fs.b])

    if NOCC:
        k.op("dve", lambda e: e.memset(G.ap, 0.0), writes=[G.b])
    else:
        cin = nc.dram_tensor("ssm_cc_in", [128, 32], F32).ap()
        cout = nc.dram_tensor("ssm_cc_out", [NCORES * 128, 32], F32).ap()
        Bcin = Buf("cin")
        Bcout = Buf("cout")
        k.dma("pool", cin, Fs.ap.rearrange("p a b -> p (a b)"), reads=[Fs.b], writes=[Bcin])
        k.wait_bufs("pool", [Bcin])
        ccsem = k.newsem("ccsem")
        k.q["pool"].append(lambda e: e.collective_compute(
            "AllGather", ALU.bypass, replica_groups=[list(range(NCORES))], ins=[cin.opt()], outs=[cout.opt()]).then_inc(ccsem))
        tok = ("dma", ccsem, 1)
        Bcout.writers = {_tkey(tok): tok}
        _tadd(k.dma_toks, tok)
        k.dma("sp", G.ap, cout.rearrange("(r p) c -> p r c", p=128), reads=[Bcout], writes=[G.b])
    Er, Ei = S("Qr11"), S("Qi11")
    Gv = G.ap.rearrange("p r (a c) -> p r a c", c=2)
    k.op("dve", lambda e: e.memset(Hinit.ap, 0.0), writes=[Hinit.b])
    k.op("dve", lambda e: e.memset(S("ac_r"), 0.0), reads=sb_, writes=sb_)
    k.op("dve", lambda e: e.memset(S("ac_i"), 0.0), reads=sb_, writes=sb_)
    for r in range(8):
        rdh = sb_ + [Hinit.b, oneh.b, G.b]
        stt(Hinit[:, :, 0], S("ac_r"), oneh[:, r:r + 1], Hinit[:, :, 0], rdh, [Hinit.b])
        stt(Hinit[:, :, 1], S("ac_i"), oneh[:, r:r + 1], Hinit[:, :, 1], rdh, [Hinit.b])
        if r == 7:
            break
        cmul(S("hn_r"), S("hn_i"), S("ac_r"), S("ac_i"), Er, Ei)
        tt(S("ac_r"), S("hn_r"), Gv[:, r, :, 0], ALU.add, rdh, sb_)
        tt(S("ac_i"), S("hn_i"), Gv[:, r, :, 1], ALU.add, rdh, sb_)
    ts(nHim.ap, Hinit[:, :, 1], -1.0, ALU.mult, [Hinit.b], [nHim.b])

    def bcn(ap, n):
        return ap.unsqueeze(2).to_broadcast([128, 8, n])

    for half in range(2):
        ps8 = slice(half * 8, half * 8 + 8)
        rdt = sb_ + [Tt.b, tmpT.b]
        k.op("dve", lambda e, ps8=ps8: e.tensor_copy(Tt[:, 0, :, 0:1], S("Qr3")[:, ps8].unsqueeze(2)), reads=rdt, writes=[Tt.b])
        k.op("dve", lambda e, ps8=ps8: e.tensor_copy(Tt[:, 1, :, 0:1], S("Qi3")[:, ps8].unsqueeze(2)), reads=rdt, writes=[Tt.b])
        for lv in range(8):
            sh = 1 << lv
            er = bcn(S("Qr%d" % (3 + lv))[:, ps8], sh)
            ei = bcn(S("Qi%d" % (3 + lv))[:, ps8], sh)
            sre, sim = Tt[:, 0, :, 0:sh], Tt[:, 1, :, 0:sh]
            dre, dim = Tt[:, 0, :, sh:2 * sh], Tt[:, 1, :, sh:2 * sh]
            tv = tmpT[:, :, 0:sh]
            tt(dre, sre, er, ALU.mult, rdt, [Tt.b])
            tt(tv, sim, ei, ALU.mult, rdt, [tmpT.b])
            tt(dre, dre, tv, ALU.subtract, rdt, [Tt.b])
            tt(dim, sre, ei, ALU.mult, rdt, [Tt.b])
            tt(tv, sim, er, ALU.mult, rdt, [tmpT.b])
            tt(dim, dim, tv, ALU.add, rdt, [Tt.b])
        for pl_ in range(8):
            pair = half * 8 + pl_
            xb = Xall.bs[pair]
            hr = Hinit[:, pair, 0:1]
            hi = Hinit[:, pair, 1:2]
            nhi = nHim[:, pair:pair + 1]
            rdx = [Tt.b, Hinit.b, nHim.b, xb]
            stt(Xall[:, pair, 0, :], Tt[:, 0, pl_, :], hr, Xall[:, pair, 0, :], rdx, [xb])
            stt(Xall[:, pair, 0, :], Tt[:, 1, pl_, :], nhi, Xall[:, pair, 0, :], rdx, [xb])
            stt(Xall[:, pair, 1, :], Tt[:, 0, pl_, :], hi, Xall[:, pair, 1, :], rdx, [xb])
            stt(Xall[:, pair, 1, :], Tt[:, 1, pl_, :], hr, Xall[:, pair, 1, :], rdx, [xb])
            hb_ = Hb[pair % 2]
            copy("act", hb_[:, :, 1:256], Xall[:, pair, :, 0:255], [xb], [hb_.b])
            k.op("dve", lambda e, hb_=hb_, pair=pair: e.tensor_copy(hb_[:, :, 0:1], Hinit[:, pair, :].unsqueeze(2)),
                 reads=[Hinit.b, hb_.b], writes=[hb_.b])
            for j2 in range(2):
                g = 2 * pair + j2
                ps_ = slice(64 * j2, 64 * j2 + 64)
                pt = psum()
                mm(pt[:, 0:256], Mbf[:, g, :], U[:, g, :], True, False, [Mbf.bs[g], U.bs[g]], [pt.b])
                mm(pt[:, 0:256], Wout[ps_, pair, 0, :], hb_[ps_, 0, :], False, False, [Wout.b, hb_.b], [pt.b])
                mm(pt[:, 0:256], Wout[ps_, pair, 1, :], hb_[ps_, 1, :], False, True, [Wout.b, hb_.b], [pt.b])
                copy(ev_eng(), U[:, g, :], pt[:, 0:256], [pt.b], [U.bs[g]])
    for j in range(4):
        for tp in range(0, 8, 2):
            pt = psum()
            for hh in range(2):
                for g8 in range(8):
                    mm(pt[:, hh * 256:(hh + 1) * 256], Sel[:, (tp + hh) * 8 + g8, :], U[:, 8 * j + g8, :], g8 == 0, g8 == 7,
                       [Sel.b] + [U.bs[8 * j + g8]], [pt.b])
            for hh in range(2):
                k.op("act", lambda e, j=j, tp=tp, hh=hh, pt=pt: e.activation(gyT[:, j, ds(tp + hh, 256, 8)], pt[:, hh * 256:(hh + 1) * 256],
                                                                           AF.Gelu_apprx_tanh),
                     reads=[pt.b], writes=[gyT.bs[j]])


def tail_phase(k, A, L, with_mix):
    ident, R1, R2, attnT, gyT, gb = L["ident"], L["R1"], L["R2"], L["attnT"], L["gyT"], L["gb"]
    psum, copy, mm, ev_eng, build_T, load_w = L["psum"], L["copy"], L["mm"], L["ev_eng"], L["build_T"], L["load_w"]
    x_own, p_own, out_d = L["x_own"], L["p_own"], L["out_d"]
    w_in, w_ap, w_ga, w_gb, w_out = L["w_in"], L["w_ap"], L["w_ga"], L["w_gb"], L["w_out"]
    w_fg, w_fu, w_fd, w_pg, w_pp = L["w_fg"], L["w_fu"], L["w_fd"], L["w_pg"], L["w_pp"]
    stg, X0, R2_off, alloc_staging = L["stg"], L["X0"], L["R2_off"], L["alloc_staging"]
    g_ffn_d, g_final_d = L["g_ffn_d"], L["g_final_d"]
    KB_ = 1024

    A.seek(X0 + 64 * KB_)
    ws = [A.alloc("ws%d" % i, [8, 512], BF16) for i in range(4)]
    ws_i = [0]

    def wslot():
        w = ws[ws_i[0] % 4]
        ws_i[0] += 1
        return w

    M0 = A.mark()

    if with_mix:
        A.seek(X0 + 32 * KB_)
        tmp = [A.alloc("mt%d" % i, [3, 512], F32) for i in range(2)]
        for f in range(8):
            wa = wslot()
            fs = slice(f * 128, (f + 1) * 128)
            k.dma("pool", wa[:, 0:4, 0:128], w_ap[:, fs].rearrange("(a p) n -> p a n", p=128), writes=[wa.b])
            k.dma("pool", wa[:, 4:8, 0:128], w_ga[:, fs].rearrange("(a p) n -> p a n", p=128), writes=[wa.b])
            k.dma("pool", wa[:, 0:4, 128:256], w_gb[:, fs].rearrange("(a p) n -> p a n", p=128), writes=[wa.b])
            wg = wslot()
            k.dma("pool", wg[:, :, 0:128], w_in[:, 5120 + f * 128:5120 + (f + 1) * 128].rearrange("(a p) n -> p a n", p=128),
                  writes=[wg.b])
            k.dma("pool", wg[:, :, 128:256], w_in[:, 6144 + f * 128:6144 + (f + 1) * 128].rearrange("(a p) n -> p a n", p=128),
                  writes=[wg.b])
            for tc in range(4):
                ts_ = slice(tc * 512, (tc + 1) * 512)
                tb = [R1.bs[i] for i in range(tc * 4, tc * 4 + 4)]
                tm = tmp[tc % 2]
                pa = psum()
                for kk in range(4):
                    mm(pa.ap, wa[:, kk, 0:128], attnT[:, kk, ts_], kk == 0, kk == 3, [wa.b] + attnT.bs, [pa.b])
                pga = psum()
                for kk in range(8):
                    mm(pga.ap, wg[:, kk, 0:128], R1[:, kk, ts_], kk == 0, kk == 7, [wg.b] + tb, [pga.b])
                k.op("act", lambda e, tm=tm, pga=pga: e.activation(tm[:, 0, :], pga.ap, AF.Sigmoid), reads=[pga.b], writes=[tm.b])
                k.op("dve", lambda e, tm=tm, pa=pa: e.tensor_tensor(tm[:, 0, :], tm[:, 0, :], pa.ap, ALU.mult),
                     reads=[pa.b, tm.b], writes=[tm.b])
                pya = psum()
                for kk in range(4):
                    mm(pya.ap, wa[:, 4 + kk, 0:128], gyT[:, kk, ts_], kk == 0, kk == 3, [wa.b] + gyT.bs, [pya.b])
                pyb = psum()
                for kk in range(4):
                    mm(pyb.ap, wa[:, kk, 128:256], gyT[:, kk, ts_], kk == 0, kk == 3, [wa.b] + gyT.bs, [pyb.b])
                pgs = psum()
                for kk in range(8):
                    mm(pgs.ap, wg[:, kk, 128:256], R1[:, kk, ts_], kk == 0, kk == 7, [wg.b] + tb, [pgs.b])
                k.op("act", lambda e, tm=tm, pyb=pyb: e.activation(tm[:, 1, :], pyb.ap, AF.Sigmoid), reads=[pyb.b], writes=[tm.b])
                k.op("act", lambda e, tm=tm, pgs=pgs: e.activation(tm[:, 2, :], pgs.ap, AF.Sigmoid), reads=[pgs.b], writes=[tm.b])
                k.op("dve", lambda e, tm=tm, pya=pya: e.tensor_tensor(tm[:, 1, :], tm[:, 1, :], pya.ap, ALU.mult),
                     reads=[pya.b, tm.b], writes=[tm.b])
                k.op("dve", lambda e, tm=tm: e.tensor_tensor(tm[:, 1, :], tm[:, 1, :], tm[:, 2, :], ALU.mult),
                     reads=[tm.b], writes=[tm.b])
                k.op("dve", lambda e, tm=tm, f=f, ts_=ts_: e.tensor_tensor(R2[:, f, ts_], tm[:, 0, :], tm[:, 1, :], ALU.add),
                     reads=[tm.b], writes=[R2.bs[i] for i in range(tc * 4, tc * 4 + 4)])
        k.barrier()

    A.seek(X0)
    resid = A.alloc("resid", [NT, D], F32, nbuf=NT)
    for t in range(NT):
        k.dma("sp", resid[:, t, :], x_own[t * 128:(t + 1) * 128, :], writes=[resid.bs[t]])

    def add_resid(t, half, pt):
        cs = slice(half * 512, (half + 1) * 512)
        k.op("dve", lambda e: e.tensor_tensor(resid[:, t, cs], resid[:, t, cs], pt.ap, ALU.add),
             reads=[pt.b, resid.bs[t]], writes=[resid.bs[t]])

    if with_mix:
        for half in range(2):
            wo = wslot()
            load_w(wo, wo.ap, w_out[:, half * 512:(half + 1) * 512], 8)
            for t in range(NT):
                pt = psum()
                for kk in range(8):
                    mm(pt.ap, R2[:, kk, t * 128:(t + 1) * 128], wo[:, kk, :], kk == 0, kk == 7, [R2.bs[t], wo.b], [pt.b])
                add_resid(t, half, pt)
        k.barrier()

    A.seek(R2_off + 16 * KB_)
    for nm, src in (("g_ffn", g_ffn_d), ("g_final", g_final_d)):
        t_ = A.alloc(nm, [D], F32)
        k.dma("sp", t_.ap, src[0:1, :].to_broadcast([128, D]), writes=[t_.b])
        gb[nm] = t_

    A.seek(M0)
    alloc_staging(256)
    M1 = A.mark()

    def resid_src(t):
        return resid[:, t, :], [resid.bs[t]]

    build_T(resid_src, NT, R1, 0, gb["g_ffn"])
    actT = A.alloc("actT", [4, TOK], BF16, nbuf=16)
    sg = [A.alloc("sg%d" % i, [512], F32) for i in range(2)]
    nfb = (DFF + 511) // 512
    for fb in range(nfb):
        c0 = fb * 512
        cw = min(512, DFF - c0)
        nft = cw // 128
        wg_ = wslot()
        wu_ = wslot()
        load_w(wg_, wg_[:, :, 0:cw], w_fg[:, c0:c0 + cw], 8)
        load_w(wu_, wu_[:, :, 0:cw], w_fu[:, c0:c0 + cw], 8)
        wd_ = [wslot(), wslot()]
        for half in range(2):
            load_w(wd_[half], wd_[half][:, 0:nft, :], w_fd[c0:c0 + cw, half * 512:(half + 1) * 512], nft)
        for ft in range(nft):
            for tc in range(4):
                ts_ = slice(tc * 512, (tc + 1) * 512)
                tb = [R1.bs[i] for i in range(tc * 4, tc * 4 + 4)]
                pg_ = psum()
                for kk in range(8):
                    mm(pg_.ap, wg_[:, kk, ft * 128:(ft + 1) * 128], R1[:, kk, ts_], kk == 0, kk == 7, [wg_.b] + tb, [pg_.b])
                pu_ = psum()
                for kk in range(8):
                    mm(pu_.ap, wu_[:, kk, ft * 128:(ft + 1) * 128], R1[:, kk, ts_], kk == 0, kk == 7, [wu_.b] + tb, [pu_.b])
                s_ = sg[(ft * 4 + tc) % 2]
                k.op("act", lambda e, s_=s_, pg_=pg_: e.activation(s_.ap, pg_.ap, AF.Silu), reads=[pg_.b], writes=[s_.b])
                k.op("dve", lambda e, s_=s_, pu_=pu_, ft=ft, ts_=ts_: e.tensor_tensor(actT[:, ft, ts_], s_.ap, pu_.ap, ALU.mult),
                     reads=[s_.b, pu_.b], writes=[actT.bs[i] for i in range(tc * 4, tc * 4 + 4)])
        for half in range(2):
            for t in range(NT):
                pt = psum()
                for kk in range(nft):
                    mm(pt.ap, actT[:, kk, t * 128:(t + 1) * 128], wd_[half][:, kk, :], kk == 0, kk == nft - 1,
                       [actT.bs[t], wd_[half].b], [pt.b])
                add_resid(t, half, pt)
    k.barrier()

    A.seek(M1)
    build_T(resid_src, NT, R1, 0, None, norm=False)

    def p_src(t):
        xt = stg["xts"][t % 3]
        k.dma("sp", xt[:, 0:256], p_own[t * 128:(t + 1) * 128, :], writes=[xt.b])
        return xt[:, 0:256], [xt.b]

    build_T(p_src, NT, R2, 0, None, norm=False, ncol=256)
    sgp = [A.alloc("sgp%d" % i, [512], F32) for i in range(2)]
    for half in range(2):
        wg_ = wslot()
        wp_ = wslot()
        load_w(wg_, wg_.ap, w_pg[:, half * 512:(half + 1) * 512], 8)
        load_w(wp_, wp_[:, 0:2, :], w_pp[:, half * 512:(half + 1) * 512], 2)
        for t in range(NT):
            pg_ = psum()
            for kk in range(8):
                mm(pg_.ap, R1[:, kk, t * 128:(t + 1) * 128], wg_[:, kk, :], kk == 0, kk == 7, [R1.bs[t], wg_.b], [pg_.b])
            pp_ = psum()
            for kk in range(2):
                mm(pp_.ap, R2[:, kk, t * 128:(t + 1) * 128], wp_[:, kk, :], kk == 0, kk == 1, [R2.bs[t], wp_.b], [pp_.b])
            s_ = sgp[t % 2]
            k.op("act", lambda e, s_=s_, pg_=pg_: e.activation(s_.ap, pg_.ap, AF.Sigmoid), reads=[pg_.b], writes=[s_.b])
            k.op("dve", lambda e, s_=s_, pp_=pp_: e.tensor_tensor(s_.ap, s_.ap, pp_.ap, ALU.mult), reads=[s_.b, pp_.b], writes=[s_.b])
            cs = slice(half * 512, (half + 1) * 512)
            k.op("dve", lambda e, s_=s_, t=t, cs=cs: e.tensor_tensor(resid[:, t, cs], resid[:, t, cs], s_.ap, ALU.add),
                 reads=[s_.b, resid.bs[t]], writes=[resid.bs[t]])
    fst = A.alloc("fst", [NT, 4], F32)
    fj = A.alloc("fj", [D], BF16)
    oo = [A.alloc("oo%d" % i, [D], F32) for i in range(2)]
    gfin = gb["g_final"]
    for t in range(NT):
        k.op("act", lambda e, t=t: e.activation(fj.ap, resid[:, t, :], AF.Square, accum_out=fst[:, t, 0:1]),
             reads=[resid.bs[t]], writes=[fj.b, fst.b])
        k.op("dve", lambda e, t=t: e.tensor_scalar(fst[:, t, 1:2], fst[:, t, 0:1], 1.0 / D, EPS, op0=ALU.mult, op1=ALU.add),
             reads=[fst.b], writes=[fst.b])
        k.op("act", lambda e, t=t: e.activation(fst[:, t, 2:3], fst[:, t, 1:2], AF.Sqrt), reads=[fst.b], writes=[fst.b])
        k.op("dve", lambda e, t=t: e.reciprocal(fst[:, t, 3:4], fst[:, t, 2:3]), reads=[fst.b], writes=[fst.b])
        o_ = oo[t % 2]
        k.op("dve", lambda e, t=t, o_=o_: e.scalar_tensor_tensor(o_.ap, resid[:, t, :], fst[:, t, 3:4], gfin.ap,
                                                                  op0=ALU.mult, op1=ALU.mult),
             reads=[resid.bs[t], fst.b, gfin.b], writes=[o_.b])
        k.dma("sp", out_d[t * 128:(t + 1) * 128, :], o_.ap, reads=[o_.b])
    k.wait_bufs("sp", [oo[0].b, oo[1].b])


_NC_CACHE = {}


def _host_inputs(inp):
    x = np.ascontiguousarray(inp["x"][0])
    p = np.ascontiguousarray(inp["p"][0, 0])
    pos = np.ascontiguousarray(inp["positions"][0]).astype(np.int32)
    half = 16
    invf = (np.float32(500000.0) ** (-np.arange(half, dtype=np.float32) * np.float32(2.0) / np.float32(32))).astype(np.float32)
    invf_t = np.ascontiguousarray(np.broadcast_to(invf[None, :], (128, 16))).astype(np.float32)

    def l2(a):
        return np.ascontiguousarray(a.reshape(16, 2, 64).transpose(1, 2, 0).reshape(128, 16))

    a_re, a_im = inp["a_re"][0], inp["a_im"][0]
    ldt = inp["log_dt"][0]
    ldt2 = np.ascontiguousarray(np.broadcast_to(ldt.reshape(16, 2, 1).transpose(1, 2, 0), (2, 64, 16)).reshape(128, 16))
    b_re, b_im = inp["b_re"][0], inp["b_im"][0]
    c_re, c_im = inp["c_re"][0], inp["c_im"][0]

    def lb(b):
        return np.ascontiguousarray(b.reshape(16, 2, 64, 16).transpose(1, 2, 0, 3).reshape(128, 16, 16))

    def lc(c):
        return np.ascontiguousarray(c.reshape(16, 2, 16, 64).transpose(1, 3, 0, 2).reshape(128, 16, 16))

    d = inp["d_skip"][0]
    dcol = np.ascontiguousarray(np.broadcast_to(d.T[None, :, :], (8, 16, 32)).reshape(128, 32))
    shared = {
        "invf": invf_t,
        "g_mix": np.ascontiguousarray(inp["g_mix"]), "g_ffn": np.ascontiguousarray(inp["g_ffn"]),
        "g_final": np.ascontiguousarray(inp["g_final"].reshape(1, D)),
        "w_in": np.ascontiguousarray(inp["w_in"][0]), "w_attn_proj": np.ascontiguousarray(inp["w_attn_proj"][0]),
        "w_glu_a": np.ascontiguousarray(inp["w_glu_a"][0]), "w_glu_b": np.ascontiguousarray(inp["w_glu_b"][0]),
        "w_out": np.ascontiguousarray(inp["w_out"][0]), "w_ffn_gate": np.ascontiguousarray(inp["w_ffn_gate"][0]),
        "w_ffn_up": np.ascontiguousarray(inp["w_ffn_up"][0]), "w_ffn_down": np.ascontiguousarray(inp["w_ffn_down"][0]),
        "w_ple_gate": np.ascontiguousarray(inp["w_ple_gate"][0]), "w_ple_proj": np.ascontiguousarray(inp["w_ple_proj"][0]),
        "lr2": l2(a_re), "li2": l2(a_im), "ldt2": ldt2.astype(np.float32),
        "b2re": lb(b_re), "b2im": lb(b_im), "c2re": lc(c_re), "c2im": lc(c_im), "dcol": dcol.astype(np.float32),
    }
    maps = []
    for c in range(NCORES):
        xo = x[c * TOK:(c + 1) * TOK]
        if c == 0:
            xp = np.zeros_like(xo)
            pp = np.zeros(TOK, np.int32)
        else:
            xp = x[(c - 1) * TOK:c * TOK]
            pp = pos[(c - 1) * TOK:c * TOK]
        pcat = np.concatenate([pp, pos[c * TOK:(c + 1) * TOK]])
        cols = []
        for g in range(3):
            h, o = group_tiles(g)
            for (s0, dd) in h + o:
                cols.append(pcat[s0 + dd * np.arange(128)])
        pos_tab = np.ascontiguousarray(np.stack(cols, axis=1)).astype(np.int32)
        m = dict(shared)
        m["x_own"] = np.ascontiguousarray(xo)
        m["x_prev"] = np.ascontiguousarray(xp)
        m["p_own"] = np.ascontiguousarray(p[c * TOK:(c + 1) * TOK])
        m["pos_tab"] = pos_tab
        m["halo_bias"] = np.full((128, 1), NEG if c == 0 else 0.0, np.float32)
        oh = np.zeros((128, 8), np.float32)
        oh[:, c] = 1.0
        m["onehot"] = oh
        maps.append(m)
    return maps


def run(inp, stage="full", trace=False):
    if stage not in _NC_CACHE:
        _NC_CACHE[stage] = build(stage)
    nc = _NC_CACHE[stage]
    maps = _host_inputs(inp)
    res = run_bass_kernel_spmd(nc, maps, core_ids=list(range(NCORES)), **({"trace": True} if trace else {}))
    return res


def kernel(**inputs):
    res = run(inputs, "full")
    out = np.concatenate([res.results[c]["out"] for c in range(NCORES)], axis=0)
    return out.reshape(1, NCORES * TOK, D).astype(np.float32)
```

```python
import math
import os
import numpy as np
from contextlib import ExitStack
import concourse.bass as bass
import concourse.mybir as mybir
from concourse.bass_utils import run_bass_kernel_spmd

F32 = mybir.dt.float32
BF16 = mybir.dt.bfloat16
I32 = mybir.dt.int32
AF = mybir.ActivationFunctionType
ALU = mybir.AluOpType
AX = mybir.AxisListType
ds = bass.ds

ENGS = ("pe", "act", "dve", "pool", "sp")
NCORES = 8
TOK = 2048
NT = 16
D = 1024
DFF = 2816
EPS = 1e-6
NEG = -30000.0
SCALE = 1.0 / math.sqrt(128.0)


class Buf:
    __slots__ = ("name", "writers", "readers", "dsem", "dcnt")

    def __init__(self, name):
        self.name = name
        self.writers = {}
        self.readers = {}
        self.dsem = None
        self.dcnt = 0


def _tkey(t):
    return ("eng", t[1]) if t[0] == "eng" else ("dma", id(t[1]))


def _tadd(d, t):
    kx = _tkey(t)
    if kx not in d or d[kx][2] < t[2]:
        d[kx] = t


class KB:
    def __init__(self):
        self.nc = bass.Bass("TRN2", target_bir_lowering=False)
        self.es = ExitStack()
        self.q = {e: [] for e in ENGS}
        self.cnt = {e: 0 for e in ENGS}
        self.waited = {}
        self.psem = {e: self.es.enter_context(self.nc.semaphore("prog_" + e)) for e in ENGS}
        self.dma_toks = {}
        self.needed = {e: set() for e in ENGS}
        self.rank = {}

    def sb(self, name, shape, dt):
        return self.es.enter_context(self.nc.sbuf_tensor(name, list(shape), dt))

    def ps(self, name, shape, dt):
        return self.es.enter_context(self.nc.psum_tensor(name, list(shape), dt))

    def dram(self, name, shape, dt, kind):
        return self.nc.dram_tensor(name, list(shape), dt, kind=kind).ap()

    def newsem(self, name):
        self.nsem = getattr(self, "nsem", 0) + 1
        return self.es.enter_context(self.nc.semaphore("%s_%d" % (name, self.nsem)))

    def _deps(self, reads, writes):
        toks = []
        for b in reads:
            toks += list(b.writers.values())
        for b in writes:
            toks += list(b.writers.values())
            toks += list(b.readers.values())
        return toks

    def _emit_waits(self, e, toks):
        need = {}
        for t in toks:
            if t[0] == "eng":
                _, te, idx = t
                if te == e and e == "pe":
                    continue
                key = ("eng", te)
                sem = self.psem[te]
                val = idx
            else:
                _, sem, val = t
                key = ("dma", id(sem))
            if self.waited.get((e, key), 0) >= val:
                continue
            if key not in need or need[key][1] < val:
                need[key] = (sem, val)
        for key, (sem, val) in need.items():
            self.waited[(e, key)] = val
            if key[0] == "eng":
                te = key[1]
                self.needed[te].add(val)
                self.q[e].append(lambda eng, sem=sem, te=te, val=val: eng.wait_ge(sem, self.rank[te][val]))
            else:
                self.q[e].append(lambda eng, sem=sem, val=val: eng.wait_ge(sem, val))

    def _commit(self, tok, reads, writes):
        for b in writes:
            b.writers = {_tkey(tok): tok}
            b.readers = {}
        for b in reads:
            _tadd(b.readers, tok)

    def op(self, e, fn, reads=(), writes=()):
        reads = list(reads)
        writes = list(writes)
        self._emit_waits(e, self._deps(reads, writes))
        self.cnt[e] += 1
        idx = self.cnt[e]
        sem = self.psem[e]
        def _emit(eng, fn=fn, sem=sem, e=e, idx=idx):
            ins = fn(eng)
            if idx in self.needed[e]:
                ins.then_inc(sem, 1)
        self.q[e].append(_emit)
        tok = ("eng", e, idx)
        self._commit(tok, reads, writes)
        return tok

    def dma(self, e, out, in_, reads=(), writes=(), sembuf=None, **kw):
        reads = list(reads)
        writes = list(writes)
        sb_ = sembuf or (writes[0] if writes else reads[0])
        if sb_.dsem is None:
            sb_.dsem = self.newsem("d_" + sb_.name)
        self._emit_waits(e, self._deps(reads, writes))
        sb_.dcnt += 16
        val = sb_.dcnt
        sem = sb_.dsem
        self.q[e].append(lambda eng, out=out, in_=in_, sem=sem, kw=kw:
                         eng.dma_start(out=out, in_=in_, **kw).then_inc(sem, 16))
        tok = ("dma", sem, val)
        _tadd(self.dma_toks, tok)
        self._commit(tok, reads, writes)
        return tok

    def wait_bufs(self, e, bufs):
        toks = []
        for b in bufs:
            toks += list(b.writers.values()) + list(b.readers.values())
        self._emit_waits(e, toks)

    def barrier(self):
        toks = [("eng", e, self.cnt[e]) for e in ENGS if self.cnt[e] > 0]
        toks += list(self.dma_toks.values())
        for e in ENGS:
            self._emit_waits(e, toks)

    def finish(self):
        nc = self.nc
        q = self.q
        for e in ENGS:
            self.rank[e] = {idx: r + 1 for r, idx in enumerate(sorted(self.needed[e]))}
        with nc.Block() as block:
            @block.tensor
            def _(eng):
                for f in q["pe"]:
                    f(eng)

            @block.scalar
            def _(eng):
                for f in q["act"]:
                    f(eng)

            @block.vector
            def _(eng):
                for f in q["dve"]:
                    f(eng)

            @block.gpsimd
            def _(eng):
                for f in q["pool"]:
                    f(eng)

            @block.sync
            def _(eng):
                for f in q["sp"]:
                    f(eng)
        self.es.close()
        return nc


class Tile:
    def __init__(self, ap, name, nbuf=1):
        self.ap = ap
        self.b = Buf(name)
        self.bs = [self.b] if nbuf == 1 else [Buf("%s_%d" % (name, i)) for i in range(nbuf)]

    def __getitem__(self, key):
        return self.ap[key]


class Arena:
    def __init__(self, k, nbytes):
        self.k = k
        self.t = k.sb("arena", [128, nbytes // 4], F32)
        self.off = 0
        self.nbytes = nbytes

    def alloc(self, name, free_shape, dt, nbuf=1):
        esz = 4 if dt in (F32, I32) else 2
        n = int(np.prod(free_shape)) * esz
        n = (n + 31) // 32 * 32
        assert self.off + n <= self.nbytes, (name, self.off, n, self.nbytes)
        v = self.t[:, self.off // 4:(self.off + n) // 4]
        if dt != F32:
            v = v.bitcast(dt)
        tot = int(np.prod(free_shape))
        v = v[:, 0:tot]
        if len(free_shape) == 2:
            v = v.rearrange("p (a b) -> p a b", a=free_shape[0])
        elif len(free_shape) == 3:
            v = v.rearrange("p (a b c) -> p a b c", a=free_shape[0], b=free_shape[1])
        elif len(free_shape) == 4:
            v = v.rearrange("p (a b c d) -> p a b c d", a=free_shape[0], b=free_shape[1], c=free_shape[2])
        self.off += n
        return Tile(v, name, nbuf)

    def mark(self):
        return self.off

    def seek(self, off):
        self.off = off

    def release(self, m):
        self.off = m


def group_tiles(g):
    if g == 0:
        return [(TOK - 128, 1)], [(TOK + 128 * n, 1) for n in range(16)]
    if g == 1:
        return ([(TOK - 512 + r, 4) for r in range(4)],
                [(TOK + 512 * n + r, 4) for r in range(4) for n in range(4)])
    return [(r, 16) for r in range(16)], [(TOK + r, 16) for r in range(16)]


ROPE_COL0 = {}
_c = 0
for _g in range(3):
    _h, _o = group_tiles(_g)
    ROPE_COL0[_g] = (_c, _c + len(_h))
    _c += len(_h) + len(_o)
NROPE = _c


def _cw_consts():
    two_pi = 2.0 * math.pi
    c1 = 6.28125
    r = two_pi - c1
    c2 = float(np.float32(r))
    m, ex = math.frexp(r)
    c2 = math.ldexp(round(m * 4096) / 4096.0, ex)
    c3 = float(np.float32(two_pi - c1 - c2))
    return c1, c2, c3


def build(stage="full"):
    k = KB()
    nc = k.nc
    dbg = stage != "full"

    def din(name, shape, dt=F32):
        return k.dram(name, shape, dt, "ExternalInput")

    x_own = din("x_own", [TOK, D])
    x_prev = din("x_prev", [TOK, D])
    p_own = din("p_own", [TOK, 256])
    pos_tab = din("pos_tab", [128, NROPE], I32)
    invf_d = din("invf", [128, 16])
    halo_bias_d = din("halo_bias", [128, 1])
    onehot_d = din("onehot", [128, 8])
    g_mix_d = din("g_mix", [1, D])
    g_ffn_d = din("g_ffn", [1, D])
    g_final_d = din("g_final", [1, D])
    w_in = din("w_in", [D, 7168])
    w_ap = din("w_attn_proj", [512, D])
    w_ga = din("w_glu_a", [512, D])
    w_gb = din("w_glu_b", [512, D])
    w_out = din("w_out", [D, D])
    w_fg = din("w_ffn_gate", [D, DFF])
    w_fu = din("w_ffn_up", [D, DFF])
    w_fd = din("w_ffn_down", [DFF, D])
    w_pg = din("w_ple_gate", [D, D])
    w_pp = din("w_ple_proj", [256, D])
    lr2_d = din("lr2", [128, 16])
    li2_d = din("li2", [128, 16])
    ldt2_d = din("ldt2", [128, 16])
    b2re_d = din("b2re", [128, 16, 16])
    b2im_d = din("b2im", [128, 16, 16])
    c2re_d = din("c2re", [128, 16, 16])
    c2im_d = din("c2im", [128, 16, 16])
    dcol_d = din("dcol", [128, 32])
    out_d = k.dram("out", [TOK, D], F32, "ExternalOutput")
    dbg_d = k.dram("dbg", [512, TOK], F32, "ExternalOutput") if dbg else None

    A = Arena(k, 204800)
    ps_big = Tile(k.ps("ps_big", [128, 1024], F32)[:, :], "ps_big")
    ps_pool = [Tile(k.ps("ps%d" % i, [128, 512], F32)[:, :], "ps%d" % i) for i in range(6)]
    ps_i = [0]

    def psum():
        t = ps_pool[ps_i[0] % len(ps_pool)]
        ps_i[0] += 1
        return t

    rr = {"ev": 0}

    def ev_eng():
        rr["ev"] += 1
        return "act" if rr["ev"] % 2 else "dve"

    def copy(eng, out, in_, reads, writes):
        if eng == "act":
            k.op("act", lambda e: e.copy(out, in_), reads, writes)
        else:
            k.op(eng, lambda e: e.tensor_copy(out, in_), reads, writes)

    def mm(out, lhsT, rhs, start, stop, reads, writes):
        k.op("pe", lambda e: e.matmul(out, lhsT=lhsT, rhs=rhs, start=start, stop=stop), reads, writes)

    identf = A.alloc("identf", [128], F32)
    ident = A.alloc("ident", [128], BF16)
    k.op("pool", lambda e: e.memset(identf.ap, 1.0), writes=[identf.b])
    k.op("pool", lambda e: e.affine_select(identf.ap, identf.ap, pattern=[[-1, 128]], compare_op=ALU.is_equal,
                                           fill=0.0, base=0, channel_multiplier=1),
         reads=[identf.b], writes=[identf.b])
    k.op("dve", lambda e: e.tensor_copy(ident.ap, identf.ap), reads=[identf.b], writes=[ident.b])

    gb = {}
    t = A.alloc("g_mix", [D], F32)
    k.dma("sp", t.ap, g_mix_d[0:1, :].to_broadcast([128, D]), writes=[t.b])
    gb["g_mix"] = t

    stg = {}

    def alloc_staging(xw=D):
        stg["xts"] = [A.alloc("xt%d" % i, [xw], F32) for i in range(3)]
        stg["xns"] = [A.alloc("xn%d" % i, [D], BF16) for i in range(2)]
        stg["junk"] = A.alloc("junk", [D], BF16)
        stg["stat"] = A.alloc("stat", [64, 4], F32)

    stat_i = [0]

    def build_T(src_fn, ntiles, dstT, dst_col0, gain, norm=True, ncol=D):
        xns, junk, stat = stg["xns"], stg["junk"], stg["stat"]
        nk = ncol // 128
        for t in range(ntiles):
            src_ap, src_bufs = src_fn(t)
            xn = xns[t % 2]
            if norm:
                si = stat_i[0] % 64
                stat_i[0] += 1
                st = stat
                k.op("act", lambda e, s=src_ap, si=si: e.activation(junk[:, 0:ncol], s, AF.Square,
                                                                      accum_out=st[:, si, 0:1]),
                     reads=src_bufs, writes=[junk.b, st.b])
                k.op("dve", lambda e, si=si: e.tensor_scalar(st[:, si, 1:2], st[:, si, 0:1], 1.0 / ncol, EPS,
                                                              op0=ALU.mult, op1=ALU.add),
                     reads=[st.b], writes=[st.b])
                k.op("act", lambda e, si=si: e.activation(st[:, si, 2:3], st[:, si, 1:2], AF.Sqrt),
                     reads=[st.b], writes=[st.b])
                k.op("dve", lambda e, si=si: e.reciprocal(st[:, si, 3:4], st[:, si, 2:3]),
                     reads=[st.b], writes=[st.b])
                k.op("dve", lambda e, s=src_ap, si=si, xn=xn: e.scalar_tensor_tensor(
                    xn[:, 0:ncol], s, st[:, si, 3:4], gain[:, 0:ncol], op0=ALU.mult, op1=ALU.mult),
                     reads=src_bufs + [st.b, gain.b], writes=[xn.b])
            else:
                k.op("dve", lambda e, s=src_ap, xn=xn: e.tensor_copy(xn[:, 0:ncol], s),
                     reads=src_bufs, writes=[xn.b])
            pt = psum()
            ptv = pt.ap.bitcast(BF16)
            for kk in range(nk):
                k.op("pe", lambda e, kk=kk, xn=xn, ptv=ptv: e.transpose(
                    ptv[:, kk * 128:(kk + 1) * 128], xn[:, kk * 128:(kk + 1) * 128], ident.ap),
                     reads=[xn.b, ident.b], writes=[pt.b])
            c0 = dst_col0 + t * 128
            copy(ev_eng(), dstT[:, 0:nk, c0:c0 + 128],
                 ptv[:, 0:nk * 128].rearrange("p (a b) -> p a b", a=nk),
                 [pt.b], [dstT.bs[(c0 // 128) % len(dstT.bs)]])

    R1 = A.alloc("R1", [8, TOK], BF16, nbuf=16)
    R2_off = A.mark()
    R2 = A.alloc("R2", [8, TOK], BF16, nbuf=16)
    X0 = A.mark()

    def x_src(dram):
        def fn(t):
            xt = stg["xts"][t % 3]
            k.dma("sp", xt.ap, dram[t * 128:(t + 1) * 128, :], writes=[xt.b])
            return xt.ap, [xt.b]
        return fn

    def load_w(dst_tile, dst_ap, src_ap, nk):
        k.dma("pool", dst_ap, src_ap.rearrange("(a p) n -> p a n", p=128), writes=[dst_tile.b])

    need_mix = stage in ("full", "attn", "ssm", "mix", "tailmix")
    attnT = None
    gyT = None
    if need_mix:
        attnT = A.alloc("attnT", [4, TOK], BF16, nbuf=16)
        gyT = A.alloc("gyT", [4, TOK], BF16, nbuf=4)
    X1 = A.mark()

    A.seek(X1)
    alloc_staging()
    build_T(x_src(x_own), NT, R1, 0, gb["g_mix"])
    if need_mix and stage != "ssm":
        build_T(x_src(x_prev), NT, R2, 0, gb["g_mix"])
    k.barrier()

    def dump_T(src, bufs):
        A.seek(X1)
        for j in range(4):
            t = A.alloc("dbgf%d" % j, [TOK], F32)
            k.op("dve", lambda e, j=j, t=t: e.tensor_copy(t.ap, src[:, j, :]), reads=bufs, writes=[t.b])
            k.dma("sp", dbg_d[j * 128:(j + 1) * 128, :], t.ap, reads=[t.b])
            k.wait_bufs("sp", [t.b])
        k.barrier()

    if stage in ("full", "attn", "mix"):
        A.seek(X0 + 16 * 1024)
        attention_phase(k, A, locals())
        k.barrier()
        if stage == "attn":
            dump_T(attnT, attnT.bs)

    if stage in ("full", "ssm", "mix"):
        A.seek(X1)
        ssm_phase(k, A, locals())
        k.barrier()
        if stage == "ssm":
            dump_T(gyT, gyT.bs)

    if stage == "tailmix":
        k.op("dve", lambda e: e.memset(attnT.ap, 0.01), writes=attnT.bs)
        k.op("dve", lambda e: e.memset(gyT.ap, 0.01), writes=gyT.bs)
    if stage in ("full", "ffn", "mix", "tailmix"):
        tail_phase(k, A, locals(), with_mix=(stage != "ffn"))

    k.barrier()
    return k.finish()


def attention_phase(k, A, L):
    ident, R1, R2, attnT = L["ident"], L["R1"], L["R2"], L["attnT"]
    psum, ps_big, copy, mm, ev_eng = L["psum"], L["ps_big"], L["copy"], L["mm"], L["ev_eng"]
    w_in, pos_tab, invf_d, halo_bias_d = L["w_in"], L["pos_tab"], L["invf_d"], L["halo_bias_d"]
    load_w = L["load_w"]

    sint = A.alloc("sint", [NROPE, 16], F32)
    cost = A.alloc("cost", [NROPE, 16], F32)
    mask_std = A.alloc("mask_std", [256], F32)
    mask_first = A.alloc("mask_first", [256], F32)
    hb = A.alloc("hb", [1], F32)
    B1 = [A.alloc("B1_%d" % i, [4 + 512], BF16) for i in range(2)]
    B2 = [A.alloc("B2_%d" % i, [16 + 2048], BF16) for i in range(2)]
    oext = {1: A.alloc("oext1", [16, 520], BF16, nbuf=16), 2: A.alloc("oext2", [16, 520], BF16, nbuf=16)}
    wq = [A.alloc("wqkv%d" % i, [8, 512], BF16) for i in range(3)]
    m_work = A.mark()

    posi = A.alloc("posi", [NROPE], I32)
    posf = A.alloc("posf", [NROPE], F32)
    invf = A.alloc("invf", [16], F32)
    ang = A.alloc("ang", [NROPE, 16], F32)
    kf = A.alloc("kf", [NROPE, 16], F32)
    ki = A.alloc("ki", [NROPE, 16], I32)
    red = A.alloc("red", [NROPE, 16], F32)
    k.dma("sp", posi.ap, pos_tab, writes=[posi.b])
    k.dma("sp", invf.ap, invf_d, writes=[invf.b])
    k.op("dve", lambda e: e.tensor_copy(posf.ap, posi.ap), reads=[posi.b], writes=[posf.b])
    for j in range(16):
        k.op("dve", lambda e, j=j: e.tensor_scalar(ang[:, :, j], posf.ap, invf[:, j:j + 1], None, op0=ALU.mult),
             reads=[posf.b, invf.b], writes=[ang.b])
    c1, c2, c3 = _cw_consts()
    angf = ang.ap.rearrange("p a b -> p (a b)")
    kff = kf.ap.rearrange("p a b -> p (a b)")
    kif = ki.ap.rearrange("p a b -> p (a b)")
    redf = red.ap.rearrange("p a b -> p (a b)")
    sinf = sint.ap.rearrange("p a b -> p (a b)")
    cosf = cost.ap.rearrange("p a b -> p (a b)")
    k.op("dve", lambda e: e.tensor_scalar(kff, angf, 1.0 / (2 * math.pi), None, op0=ALU.mult),
         reads=[ang.b], writes=[kf.b])
    k.op("dve", lambda e: e.tensor_copy(kif, kff), reads=[kf.b], writes=[ki.b])
    k.op("dve", lambda e: e.tensor_copy(kff, kif), reads=[ki.b], writes=[kf.b])
    TWO_PI = 2 * math.pi

    def stt_(out, in0, sc, in1, rd, wr):
        k.op("dve", lambda e: e.scalar_tensor_tensor(out, in0, sc, in1, op0=ALU.mult, op1=ALU.add), reads=rd, writes=wr)

    stt_(redf, kff, -c1, angf, [kf.b, ang.b], [red.b])
    stt_(redf, kff, -c2, redf, [kf.b, red.b], [red.b])
    stt_(redf, kff, -c3, redf, [kf.b, red.b], [red.b])

    def wrap(dst, dstb, shift):
        k.op("dve", lambda e: e.tensor_scalar(dst, redf, float(shift), None, op0=ALU.add), reads=[red.b], writes=[dstb])
        k.op("dve", lambda e: e.tensor_scalar(kff, dst, math.pi, None, op0=ALU.is_gt), reads=[dstb, kf.b], writes=[kf.b])
        stt_(dst, kff, -TWO_PI, dst, [kf.b, dstb], [dstb])
        k.op("dve", lambda e: e.tensor_scalar(kff, dst, -math.pi, None, op0=ALU.is_lt), reads=[dstb, kf.b], writes=[kf.b])
        stt_(dst, kff, TWO_PI, dst, [kf.b, dstb], [dstb])

    wrap(angf, ang.b, 0.0)
    k.op("act", lambda e: e.activation(sinf, angf, AF.Sin), reads=[ang.b], writes=[sint.b])
    wrap(angf, ang.b, math.pi / 2)
    k.op("act", lambda e: e.activation(cosf, angf, AF.Sin), reads=[ang.b], writes=[cost.b])

    k.dma("sp", hb.ap, halo_bias_d, writes=[hb.b])
    k.op("pool", lambda e: e.memset(mask_std.ap, 0.0), writes=[mask_std.b])
    k.op("pool", lambda e: e.affine_select(mask_std.ap, mask_std.ap, pattern=[[1, 256]], compare_op=ALU.is_ge,
                                           fill=NEG, base=0, channel_multiplier=-1),
         reads=[mask_std.b], writes=[mask_std.b])
    k.op("pool", lambda e: e.affine_select(mask_std.ap, mask_std.ap, pattern=[[-1, 256]], compare_op=ALU.is_ge,
                                           fill=NEG, base=128, channel_multiplier=1),
         reads=[mask_std.b], writes=[mask_std.b])
    k.op("dve", lambda e: e.tensor_copy(mask_first[:, 128:256], mask_std[:, 128:256]),
         reads=[mask_std.b], writes=[mask_first.b])
    k.op("dve", lambda e: e.tensor_scalar(mask_first[:, 0:128], mask_std[:, 0:128], hb[:, 0:1], None, op0=ALU.add),
         reads=[mask_std.b, hb.b, mask_first.b], writes=[mask_first.b])

    Bf = A.alloc("Bf", [16 + 2048], F32)
    for (Bt, mult, pads, width) in ((B1, 4, (3, 4), 516), (B2, 16, (15, 16), 2064)):
        for i, pad in enumerate(pads):
            k.op("pool", lambda e, width=width: e.memset(Bf[:, 0:width], 1.0), reads=[Bf.b], writes=[Bf.b])
            k.op("pool", lambda e, width=width, pad=pad, mult=mult: e.affine_select(
                Bf[:, 0:width], Bf[:, 0:width], pattern=[[1, width]], compare_op=ALU.is_equal,
                fill=0.0, base=-pad, channel_multiplier=-mult), reads=[Bf.b], writes=[Bf.b])
            k.op("dve", lambda e, Bt=Bt, i=i, width=width: e.tensor_copy(Bt[i].ap, Bf[:, 0:width]),
                 reads=[Bf.b], writes=[Bt[i].b])
    k.barrier()
    A.seek(m_work)

    qsb = [A.alloc("qsb%d" % i, [512], BF16) for i in range(2)]
    ksb = [A.alloc("ksb%d" % i, [512], BF16) for i in range(2)]
    qT = [A.alloc("qT%d" % i, [512], BF16) for i in range(2)]
    kT = [A.alloc("kT%d" % i, [512], BF16) for i in range(3)]
    vv = [A.alloc("vv%d" % i, [512], BF16) for i in range(3)]
    sm = [A.alloc("sm%d" % i, [4, 256], F32) for i in range(2)]
    Pm = [A.alloc("P%d" % i, [4, 256], BF16) for i in range(2)]
    PT = [A.alloc("PT%d" % i, [8, 128], BF16) for i in range(2)]
    rt = [A.alloc("rt%d" % i, [8, 16], F32) for i in range(2)]
    stt = [A.alloc("stt%d" % i, [8, 4], F32) for i in range(2)]
    o0 = [A.alloc("o0_0", [512], F32)] * 2
    lse0 = [A.alloc("lse0_%d" % i, [4], F32) for i in range(2)]
    mg = [A.alloc("mg%d" % i, [8, 16], F32) for i in range(2)]
    acc = [A.alloc("acc0", [512], F32)] * 2
    attn_tok = [A.alloc("attn_tok%d" % i, [512], BF16) for i in range(2)]

    def ncols(tile, kk):
        s0, d = tile
        if s0 >= TOK:
            return R1[:, kk, ds(s0 - TOK, 128, d)], R1.bs
        return R2[:, kk, ds(s0, 128, d)], R2.bs

    def proj(tile, w):
        pt = psum()
        for kk in range(8):
            lhs, lb = ncols(tile, kk)
            mm(pt.ap, lhs, w[:, kk, :], kk == 0, kk == 7, lb + [w.b], [pt.b])
        return pt

    def rope(pt, dst, col, par):
        copy("act", dst.ap, pt.ap, [pt.b], [dst.b])
        pv = pt.ap.rearrange("p (h d) -> p h d", h=4)
        dv = dst.ap.rearrange("p (h d) -> p h d", h=4)
        x1 = pv[:, :, 0:16]
        x2 = pv[:, :, 16:32]
        cb = L_cost[:, col, :].unsqueeze(1).to_broadcast([128, 4, 16])
        sb_ = L_sint[:, col, :].unsqueeze(1).to_broadcast([128, 4, 16])
        r = rt[par]
        t1 = r[:, 0:4, :]
        t2 = r[:, 4:8, :]
        rd = [pt.b, cost.b, sint.b, dst.b]
        k.op("dve", lambda e: e.tensor_tensor(t1, x1, cb, ALU.mult), reads=rd, writes=[r.b])
        k.op("dve", lambda e: e.tensor_tensor(t2, x2, sb_, ALU.mult), reads=rd + [r.b], writes=[r.b])
        k.op("dve", lambda e: e.tensor_tensor(dv[:, :, 0:16], t1, t2, ALU.subtract), reads=[r.b, dst.b], writes=[dst.b])
        k.op("dve", lambda e: e.tensor_tensor(t1, x2, cb, ALU.mult), reads=rd + [r.b, dst.b], writes=[r.b])
        k.op("dve", lambda e: e.tensor_tensor(t2, x1, sb_, ALU.mult), reads=rd + [r.b], writes=[r.b])
        k.op("dve", lambda e: e.tensor_tensor(dv[:, :, 16:32], t1, t2, ALU.add), reads=[r.b, dst.b], writes=[dst.b])

    L_cost, L_sint = cost.ap, sint.ap

    def transpose4(src, dst):
        pt = psum()
        ptv = pt.ap.bitcast(BF16)
        for h in range(4):
            k.op("pe", lambda e, h=h: e.transpose(ptv[:, h * 128:(h + 1) * 128], src[:, h * 128:(h + 1) * 128], ident.ap),
                 reads=[src.b, ident.b], writes=[pt.b])
        copy(ev_eng(), dst.ap, ptv[:, 0:512], [pt.b], [dst.b])

    unit_ctr = [0]

    def kv_tile(tile, col, slot, wk, wv, par):
        pk = proj(tile, wk)
        rope(pk, ksb[par], col, par)
        transpose4(ksb[par], kT[slot])
        pv = proj(tile, wv)
        copy(ev_eng(), vv[slot].ap, pv.ap, [pv.b], [vv[slot].b])

    CUT = int(os.environ.get("K_CUT", "99"))

    def qpart(tile, col, wq_, par):
        pq = proj(tile, wq_)
        rope(pq, qsb[par], col, par)
        transpose4(qsb[par], qT[par])

    def attend_a(prev_slot, cur_slot, par):
        sv = ps_big.ap.rearrange("p (h c) -> p h c", h=4)
        for h in range(4):
            hs = slice(h * 128, (h + 1) * 128)
            mm(sv[:, h, 0:128], qT[par][:, hs], kT[prev_slot][:, hs], True, True,
               [qT[par].b, kT[prev_slot].b], [ps_big.b])
            mm(sv[:, h, 128:256], qT[par][:, hs], kT[cur_slot][:, hs], True, True,
               [qT[par].b, kT[cur_slot].b], [ps_big.b])

    def attend_b(prev_slot, cur_slot, first, g, uidx, par):
        sv = ps_big.ap.rearrange("p (h c) -> p h c", h=4)
        mk = mask_first if first else mask_std
        s_ = sm[par]
        st_ = stt[par]
        k.op("dve", lambda e: e.tensor_tensor(s_.ap, sv, mk.ap.unsqueeze(1).to_broadcast([128, 4, 256]), ALU.add),
             reads=[ps_big.b, mk.b], writes=[s_.b])
        k.op("dve", lambda e: e.tensor_reduce(st_[:, 0, :], s_.ap, axis=AX.X, op=ALU.max), reads=[s_.b], writes=[st_.b])
        k.op("dve", lambda e: e.tensor_scalar(st_[:, 1, :], st_[:, 0, :], -SCALE, None, op0=ALU.mult),
             reads=[st_.b], writes=[st_.b])
        P_ = Pm[par]
        for h in range(4):
            k.op("act", lambda e, h=h: e.activation(P_[:, h, :], s_[:, h, :], AF.Exp, bias=st_[:, 1, h:h + 1],
                                                    scale=SCALE, accum_out=st_[:, 2, h:h + 1]),
                 reads=[s_.b, st_.b], writes=[P_.b, st_.b])
        if CUT <= 3:
            return par
        pt = psum()
        ptv = pt.ap.bitcast(BF16)
        for h in range(4):
            for half in range(2):
                j = h * 2 + half
                k.op("pe", lambda e, h=h, half=half, j=j: e.transpose(
                    ptv[:, j * 128:(j + 1) * 128], P_[:, h, half * 128:(half + 1) * 128], ident.ap),
                     reads=[P_.b, ident.b], writes=[pt.b])
        PT_ = PT[par]
        copy(ev_eng(), PT_.ap, ptv.rearrange("p (a b) -> p a b", a=8), [pt.b], [PT_.b])
        po = psum()
        for h in range(4):
            hs = slice(h * 128, (h + 1) * 128)
            mm(po[:, hs], PT_[:, 2 * h, :], vv[prev_slot][:, hs], True, False, [PT_.b, vv[prev_slot].b], [po.b])
            mm(po[:, hs], PT_[:, 2 * h + 1, :], vv[cur_slot][:, hs], False, True, [PT_.b, vv[cur_slot].b], [po.b])
        if CUT <= 4:
            return par
        k.op("dve", lambda e: e.reciprocal(st_[:, 3, :], st_[:, 2, :]), reads=[st_.b], writes=[st_.b])
        k.op("act", lambda e: e.activation(st_[:, 4, :], st_[:, 2, :], AF.Ln), reads=[st_.b], writes=[st_.b])
        rden_b = st_[:, 3, :].unsqueeze(2).to_broadcast([128, 4, 128])
        pov = po.ap.rearrange("p (h d) -> p h d", h=4)
        if g == 0:
            o_ = o0[par]
            k.op("dve", lambda e: e.tensor_tensor(o_.ap.rearrange("p (h d) -> p h d", h=4), pov, rden_b, ALU.mult),
                 reads=[po.b, st_.b], writes=[o_.b])
            k.op("dve", lambda e: e.tensor_tensor(lse0[par].ap, st_[:, 4, :], st_[:, 1, :], ALU.subtract),
                 reads=[st_.b], writes=[lse0[par].b])
        else:
            oe = oext[g]
            ob = oe.bs[uidx]
            k.op("dve", lambda e: e.tensor_tensor(oe[:, uidx, 0:512].rearrange("p (h d) -> p h d", h=4), pov, rden_b,
                                                  ALU.mult),
                 reads=[po.b, st_.b], writes=[ob])
            k.op("dve", lambda e: e.tensor_tensor(st_[:, 5, :], st_[:, 4, :], st_[:, 1, :], ALU.subtract),
                 reads=[st_.b], writes=[st_.b])
            k.op("dve", lambda e: e.tensor_copy(oe[:, uidx, 512:516], st_[:, 5, :]), reads=[st_.b, ob], writes=[ob])
            k.op("dve", lambda e: e.tensor_copy(st_[:, 6, :], oe[:, uidx, 512:516]), reads=[ob, st_.b], writes=[st_.b])
            k.op("dve", lambda e: e.tensor_tensor(oe[:, uidx, 516:520], st_[:, 5, :], st_[:, 6, :], ALU.subtract),
                 reads=[st_.b, ob], writes=[ob])
        return par

    def merge(T, par):
        n4, q4 = T // 4, T % 4
        p1 = psum()
        p2 = psum()
        pl = psum()
        def b1v(r):
            bt = B1[0] if r % 2 == 1 else B1[1]
            pad = 3 if r % 2 == 1 else 4
            o_ = pad + 128 * q4 - r
            assert o_ % 2 == 0
            return bt[:, o_:o_ + 128], bt.b

        def b2v(j):
            bt = B2[0] if j % 2 == 1 else B2[1]
            pad = 15 if j % 2 == 1 else 16
            o_ = pad + 128 * T - j
            assert o_ % 2 == 0
            return bt[:, o_:o_ + 128], bt.b

        for r in range(4):
            u = r * 4 + n4
            lhs, lb = b1v(r)
            mm(p1.ap, lhs, oext[1][:, u, 0:512], r == 0, r == 3, [lb, oext[1].bs[u]], [p1.b])
        for r in range(4):
            u = r * 4 + n4
            lhs, lb = b1v(r)
            mm(pl[:, 0:8], lhs, oext[1][:, u, 512:520], r == 0, r == 3, [lb, oext[1].bs[u]], [pl.b])
        for j in range(16):
            lhs, lb = b2v(j)
            mm(p2.ap, lhs, oext[2][:, j, 0:512], j == 0, j == 15, [lb, oext[2].bs[j]], [p2.b])
        for j in range(16):
            lhs, lb = b2v(j)
            mm(pl[:, 8:16], lhs, oext[2][:, j, 512:520], j == 0, j == 15, [lb, oext[2].bs[j]], [pl.b])
        m_ = mg[par]
        lv = m_[:, 0, 0:12].rearrange("p (g h) -> p g h", g=3)
        k.op("dve", lambda e: e.tensor_copy(m_[:, 6, :], pl[:, 0:16]), reads=[pl.b, m_.b], writes=[m_.b])
        k.op("dve", lambda e: e.tensor_copy(lv[:, 0, :], lse0[par].ap), reads=[lse0[par].b, m_.b], writes=[m_.b])
        k.op("dve", lambda e: e.tensor_tensor(lv[:, 1, :], m_[:, 6, 0:4], m_[:, 6, 4:8], ALU.add), reads=[m_.b], writes=[m_.b])
        k.op("dve", lambda e: e.tensor_tensor(lv[:, 2, :], m_[:, 6, 8:12], m_[:, 6, 12:16], ALU.add), reads=[m_.b], writes=[m_.b])
        mx = m_[:, 1, 0:4]
        k.op("dve", lambda e: e.tensor_tensor(mx, lv[:, 0, :], lv[:, 1, :], ALU.max), reads=[m_.b], writes=[m_.b])
        k.op("dve", lambda e: e.tensor_tensor(mx, mx, lv[:, 2, :], ALU.max), reads=[m_.b], writes=[m_.b])
        ev = m_[:, 2, 0:12].rearrange("p (g h) -> p g h", g=3)
        k.op("dve", lambda e: e.tensor_tensor(ev, lv, mx.unsqueeze(1).to_broadcast([128, 3, 4]), ALU.subtract),
             reads=[m_.b], writes=[m_.b])
        k.op("act", lambda e: e.activation(m_[:, 3, 0:12], m_[:, 2, 0:12], AF.Exp), reads=[m_.b], writes=[m_.b])
        e3 = m_[:, 3, 0:12].rearrange("p (g h) -> p g h", g=3)
        sm_ = m_[:, 4, 0:4]
        k.op("dve", lambda e: e.tensor_tensor(sm_, e3[:, 0, :], e3[:, 1, :], ALU.add), reads=[m_.b], writes=[m_.b])
        k.op("dve", lambda e: e.tensor_tensor(sm_, sm_, e3[:, 2, :], ALU.add), reads=[m_.b], writes=[m_.b])
        k.op("dve", lambda e: e.reciprocal(m_[:, 4, 4:8], sm_), reads=[m_.b], writes=[m_.b])
        wv = m_[:, 5, 0:12].rearrange("p (g h) -> p g h", g=3)
        k.op("dve", lambda e: e.tensor_tensor(wv, e3, m_[:, 4, 4:8].unsqueeze(1).to_broadcast([128, 3, 4]), ALU.mult),
             reads=[m_.b], writes=[m_.b])
        a_ = acc[par]
        at = attn_tok[par]
        for h in range(4):
            hs = slice(h * 128, (h + 1) * 128)
            k.op("dve", lambda e, h=h, hs=hs: e.tensor_scalar(a_[:, hs], o0[par][:, hs], wv[:, 0, h:h + 1], None,
                                                             op0=ALU.mult),
                 reads=[o0[par].b, m_.b], writes=[a_.b])
            k.op("dve", lambda e, h=h, hs=hs: e.scalar_tensor_tensor(a_[:, hs], p1[:, hs], wv[:, 1, h:h + 1], a_[:, hs],
                                                                    op0=ALU.mult, op1=ALU.add),
                 reads=[p1.b, m_.b, a_.b], writes=[a_.b])
            k.op("dve", lambda e, h=h, hs=hs: e.scalar_tensor_tensor(at[:, hs], p2[:, hs], wv[:, 2, h:h + 1], a_[:, hs],
                                                                    op0=ALU.mult, op1=ALU.add),
                 reads=[p2.b, m_.b, a_.b], writes=[at.b])
        pt = psum()
        ptv = pt.ap.bitcast(BF16)
        for h in range(4):
            k.op("pe", lambda e, h=h: e.transpose(ptv[:, h * 128:(h + 1) * 128], at[:, h * 128:(h + 1) * 128], ident.ap),
                 reads=[at.b, ident.b], writes=[pt.b])
        copy(ev_eng(), attnT[:, :, T * 128:(T + 1) * 128], ptv[:, 0:512].rearrange("p (a b) -> p a b", a=4),
             [pt.b], [attnT.bs[T]])

    items = []
    for g in (2, 1, 0):
        halo, own = group_tiles(g)
        hc0, oc0 = ROPE_COL0[g]
        nseq = len(halo)
        per = len(own) // nseq
        for s_ in range(nseq):
            items.append(("halo", g, halo[s_], hc0 + s_, len(items) % 3, None, None, None, s_ == 0))
            for n in range(per):
                ui = s_ * per + n
                items.append(("unit", g, own[ui], oc0 + ui, len(items) % 3, (len(items) - 1) % 3, n == 0, ui, False))
    upar = [0]

    def stage1a(it):
        kind, g, tile, col, slot, prev, first, ui, newg = it
        if newg:
            for i, c0 in enumerate((g * 512, 1536 + g * 512, 3072 + g * 512)):
                load_w(wq[i], wq[i].ap, w_in[:, c0:c0 + 512], 8)
        par = upar[0] % 2
        upar[0] += 1
        pk = proj(tile, wq[1])
        pv = proj(tile, wq[2])
        pq = proj(tile, wq[0]) if kind == "unit" else None
        copy(ev_eng(), vv[slot].ap, pv.ap, [pv.b], [vv[slot].b])
        return (par, pk, pq)

    def stage1b(it, h_):
        kind, g, tile, col, slot, prev, first, ui, newg = it
        par, pk, pq = h_
        rope(pk, ksb[par], col, par)
        if pq is not None:
            rope(pq, qsb[par], col, par)
        transpose4(ksb[par], kT[slot])
        if pq is not None:
            transpose4(qsb[par], qT[par])

    hs_ = {0: stage1a(items[0])}
    stage1b(items[0], hs_[0])
    for i, it in enumerate(items):
        kind, g, tile, col, slot, prev, first, ui, newg = it
        if i + 1 < len(items):
            hs_[i + 1] = stage1a(items[i + 1])
        if kind == "unit":
            attend_a(prev, slot, hs_[i][0])
        if i + 1 < len(items):
            stage1b(items[i + 1], hs_[i + 1])
        if kind == "unit":
            attend_b(prev, slot, first, g, ui, hs_[i][0])
            if g == 0:
                merge(ui, hs_[i][0])


def ssm_phase(k, A, L):
    ident, identf, R1, R2, gyT = L["ident"], L["identf"], L["R1"], L["R2"], L["gyT"]
    psum, copy, mm, ev_eng = L["psum"], L["copy"], L["mm"], L["ev_eng"]
    w_in, onehot_d, R2_off = L["w_in"], L["onehot_d"], L["R2_off"]
    nc = k.nc
    NOCC = os.environ.get("K_NOCC", "0") == "1"
    m_top = A.mark()

    def tt(out, a, b, op, rd, wr):
        k.op("dve", lambda e: e.tensor_tensor(out, a, b, op), reads=rd, writes=wr)

    def ts(out, a, s1, op0, rd, wr, s2=None, op1=None):
        if op1 is None:
            k.op("dve", lambda e: e.tensor_scalar(out, a, s1, None, op0=op0), reads=rd, writes=wr)
        else:
            k.op("dve", lambda e: e.tensor_scalar(out, a, s1, s2, op0=op0, op1=op1), reads=rd, writes=wr)

    def stt(out, in0, sc, in1, rd, wr, op0=ALU.mult, op1=ALU.add):
        k.op("dve", lambda e: e.scalar_tensor_tensor(out, in0, sc, in1, op0=op0, op1=op1), reads=rd, writes=wr)

    WinT = A.alloc("WinT", [16, 2, 128], BF16)
    Wout = A.alloc("Wout", [16, 2, 128], BF16)
    Mbf = A.alloc("Mbf", [32, 128], BF16, nbuf=32)
    Sel = A.alloc("Sel", [64, 128], BF16)
    Eb = A.alloc("Eb", [352], BF16)
    sm_ = A.alloc("ssm_small", [90, 16], F32)
    sb_ = [sm_.b]
    names = {}

    def S(name):
        if name not in names:
            names[name] = len(names)
            assert len(names) <= 90
        return sm_[:, names[name], :]

    bb = A.alloc("ssm_bb", [6, 16, 16], F32)
    dcol = A.alloc("dcol", [32], F32)
    oneh = A.alloc("oneh", [8], F32)
    rowmask = A.alloc("rowmask", [8], F32)
    halfpi = A.alloc("halfpi", [1], F32)
    blockmask = A.alloc("blockmask", [128], F32)
    Fs = A.alloc("Fs", [16, 2], F32)
    Hinit = A.alloc("Hinit", [16, 2], F32)
    nHim = A.alloc("nHim", [16], F32)
    G = A.alloc("Ggath", [8, 32], F32)
    m_tmp = A.mark()
    wu = A.alloc("wu", [8, 512], BF16)

    for nm, src in (("lr", L["lr2_d"]), ("li", L["li2_d"]), ("ldt", L["ldt2_d"])):
        k.dma("sp", S(nm), src, writes=sb_)
    for i, src in enumerate((L["b2re_d"], L["b2im_d"], L["c2re_d"], L["c2im_d"])):
        k.dma("sp", bb[:, i, :, :], src, writes=[bb.b])
    k.dma("sp", dcol.ap, L["dcol_d"], writes=[dcol.b])
    k.dma("sp", oneh.ap, onehot_d, writes=[oneh.b])
    k.dma("pool", wu.ap, w_in[:, 4608:5120].rearrange("(a p) n -> p a n", p=128), writes=[wu.b])

    uT = gyT
    for j in range(4):
        for tc in range(4):
            pt = psum()
            ts_ = slice(tc * 512, (tc + 1) * 512)
            for kk in range(8):
                mm(pt.ap, wu[:, kk, j * 128:(j + 1) * 128], R1[:, kk, ts_], kk == 0, kk == 7,
                   [wu.b] + [R1.bs[i] for i in range(tc * 4, tc * 4 + 4)], [pt.b])
            copy(ev_eng(), uT[:, j, ts_], pt.ap, [pt.b], [uT.bs[j]])

    Ef = A.alloc("Ef", [352], F32)
    k.op("pool", lambda e: e.memset(Ef.ap, 1.0), writes=[Ef.b])
    k.op("pool", lambda e: e.affine_select(Ef.ap, Ef.ap, pattern=[[1, 352]], compare_op=ALU.is_equal, fill=0.0,
                                           base=-112, channel_multiplier=-1), reads=[Ef.b], writes=[Ef.b])
    k.op("dve", lambda e: e.tensor_copy(Eb.ap, Ef.ap), reads=[Ef.b], writes=[Eb.b])
    k.op("pool", lambda e: e.memset(rowmask.ap, 1.0), writes=[rowmask.b])
    k.op("pool", lambda e: e.affine_select(rowmask.ap, rowmask.ap, pattern=[[-16, 8]], compare_op=ALU.is_ge, fill=0.0,
                                           base=0, channel_multiplier=1), reads=[rowmask.b], writes=[rowmask.b])
    k.op("pool", lambda e: e.affine_select(rowmask.ap, rowmask.ap, pattern=[[16, 8]], compare_op=ALU.is_ge, fill=0.0,
                                           base=15, channel_multiplier=-1), reads=[rowmask.b], writes=[rowmask.b])
    for a_ in range(8):
        for b_ in range(8):
            o_ = 112 - 16 * (b_ - a_)
            ts(Sel[:, a_ * 8 + b_, :], Eb[:, o_:o_ + 128], rowmask[:, a_:a_ + 1], ALU.mult, [Eb.b, rowmask.b], [Sel.b])
    k.op("pool", lambda e: e.memset(blockmask.ap, 1.0), writes=[blockmask.b])
    k.op("pool", lambda e: e.affine_select(blockmask.ap.rearrange("p (t c) -> p t c", t=8), blockmask.ap.rearrange("p (t c) -> p t c", t=8),
                                           pattern=[[16, 8], [0, 16]], compare_op=ALU.is_ge, fill=0.0,
                                           base=15, channel_multiplier=-1), reads=[blockmask.b], writes=[blockmask.b])
    k.op("dve", lambda e: e.memset(halfpi.ap, math.pi / 2), writes=[halfpi.b])

    def act(out, in_, func, rd, wr, **kw):
        k.op("act", lambda e: e.activation(out, in_, func, **kw), reads=rd, writes=wr)

    act(S("dt"), S("ldt"), AF.Exp, sb_, sb_)
    tt(S("t0"), S("lr"), S("dt"), ALU.mult, sb_, sb_)
    act(S("mag"), S("t0"), AF.Exp, sb_, sb_, scale=1.0 / 16)
    tt(S("t1"), S("li"), S("dt"), ALU.mult, sb_, sb_)
    act(S("sn"), S("t1"), AF.Sin, sb_, sb_, scale=1.0 / 16)
    act(S("cs"), S("t1"), AF.Sin, sb_ + [halfpi.b], sb_, scale=1.0 / 16, bias=halfpi[:, 0:1])
    tt(S("re"), S("mag"), S("cs"), ALU.mult, sb_, sb_)
    tt(S("im"), S("mag"), S("sn"), ALU.mult, sb_, sb_)

    def csquare(ore, oim, ire, iim):
        tt(S("sq_a"), ire, ire, ALU.mult, sb_, sb_)
        tt(S("sq_b"), iim, iim, ALU.mult, sb_, sb_)
        tt(S("sq_c"), ire, iim, ALU.mult, sb_, sb_)
        tt(ore, S("sq_a"), S("sq_b"), ALU.subtract, sb_, sb_)
        ts(oim, S("sq_c"), 2.0, ALU.mult, sb_, sb_)

    def cmul(ore, oim, are, aim, bre, bim):
        tt(S("cm_a"), are, bre, ALU.mult, sb_, sb_)
        tt(S("cm_b"), aim, bim, ALU.mult, sb_, sb_)
        tt(S("cm_c"), are, bim, ALU.mult, sb_, sb_)
        tt(S("cm_d"), aim, bre, ALU.mult, sb_, sb_)
        tt(ore, S("cm_a"), S("cm_b"), ALU.subtract, sb_, sb_)
        tt(oim, S("cm_c"), S("cm_d"), ALU.add, sb_, sb_)

    for i in range(3):
        csquare(S("re"), S("im"), S("re"), S("im"))
    csquare(S("Qr0"), S("Qi0"), S("re"), S("im"))
    for m in range(1, 12):
        csquare(S("Qr%d" % m), S("Qi%d" % m), S("Qr%d" % (m - 1)), S("Qi%d" % (m - 1)))
    k.op("dve", lambda e: e.memset(S("Pr0"), 1.0), reads=sb_, writes=sb_)
    k.op("dve", lambda e: e.memset(S("Pi0"), 0.0), reads=sb_, writes=sb_)
    k.op("dve", lambda e: e.tensor_copy(S("Pr1"), S("Qr0")), reads=sb_, writes=sb_)
    k.op("dve", lambda e: e.tensor_copy(S("Pi1"), S("Qi0")), reads=sb_, writes=sb_)
    for kk in range(2, 9):
        cmul(S("Pr%d" % kk), S("Pi%d" % kk), S("Pr%d" % (kk - 1)), S("Pi%d" % (kk - 1)), S("Qr0"), S("Qi0"))
    for m in range(3, 11):
        ts(S("nQi%d" % m), S("Qi%d" % m), -1.0, ALU.mult, sb_, sb_)
    ts(S("nr"), S("Qr0"), -1.0, ALU.add, sb_, sb_)
    tt(S("z_a"), S("lr"), S("lr"), ALU.mult, sb_, sb_)
    tt(S("z_b"), S("li"), S("li"), ALU.mult, sb_, sb_)
    tt(S("z_a"), S("z_a"), S("z_b"), ALU.add, sb_, sb_)
    k.op("dve", lambda e: e.reciprocal(S("rden"), S("z_a")), reads=sb_, writes=sb_)
    tt(S("z_c"), S("nr"), S("lr"), ALU.mult, sb_, sb_)
    tt(S("z_d"), S("Qi0"), S("li"), ALU.mult, sb_, sb_)
    tt(S("z_c"), S("z_c"), S("z_d"), ALU.add, sb_, sb_)
    tt(S("zre"), S("z_c"), S("rden"), ALU.mult, sb_, sb_)
    tt(S("z_c"), S("Qi0"), S("lr"), ALU.mult, sb_, sb_)
    tt(S("z_d"), S("nr"), S("li"), ALU.mult, sb_, sb_)
    tt(S("z_c"), S("z_c"), S("z_d"), ALU.subtract, sb_, sb_)
    tt(S("zim"), S("z_c"), S("rden"), ALU.mult, sb_, sb_)
    tt(S("z_a"), S("Qr3"), S("Qr3"), ALU.mult, sb_, sb_)
    tt(S("z_b"), S("Qi3"), S("Qi3"), ALU.mult, sb_, sb_)
    tt(S("z_a"), S("z_a"), S("z_b"), ALU.add, sb_, sb_)
    k.op("dve", lambda e: e.reciprocal(S("z_b"), S("z_a")), reads=sb_, writes=sb_)
    tt(S("ir"), S("Qr3"), S("z_b"), ALU.mult, sb_, sb_)
    tt(S("nii"), S("Qi3"), S("z_b"), ALU.mult, sb_, sb_)
    ts(S("ii"), S("nii"), -1.0, ALU.mult, sb_, sb_)

    def bc16(ap):
        return ap.unsqueeze(2).to_broadcast([128, 16, 16])

    def bc128(ap):
        return ap.unsqueeze(2).to_broadcast([128, 16, 128])

    bre, bim, cre, cim, bbre, bbim = (bb[:, i, :, :] for i in range(6))
    tmp3 = A.alloc("tmp3", [2, 16, 16], F32)
    rdb = sb_ + [bb.b, tmp3.b]
    tt(tmp3[:, 0], bre, bc16(S("zre")), ALU.mult, rdb, [tmp3.b])
    tt(tmp3[:, 1], bim, bc16(S("zim")), ALU.mult, rdb, [tmp3.b])
    tt(bbre, tmp3[:, 0], tmp3[:, 1], ALU.subtract, rdb, [bb.b])
    tt(tmp3[:, 0], bim, bc16(S("zre")), ALU.mult, rdb, [tmp3.b])
    tt(tmp3[:, 1], bre, bc16(S("zim")), ALU.mult, rdb, [tmp3.b])
    tt(bbim, tmp3[:, 0], tmp3[:, 1], ALU.add, rdb, [bb.b])

    m_after = A.mark()
    A.seek(R2_off)
    WB = A.alloc("WB", [16, 2, 128], F32)
    WO = A.alloc("WO", [16, 2, 128], F32)
    A.seek(m_after)
    WCp = A.alloc("WCp", [16, 2, 128], F32)
    tmpM = A.alloc("tmpM", [2, 128], F32)
    for sp in range(8):
        kk = 7 - sp
        cs = slice(sp * 16, (sp + 1) * 16)
        pr, pi = bc16(S("Pr%d" % kk)), bc16(S("Pi%d" % kk))
        tt(tmp3[:, 0], bbre, pr, ALU.mult, rdb, [tmp3.b])
        tt(tmp3[:, 1], bbim, pi, ALU.mult, rdb, [tmp3.b])
        tt(WB[:, :, 0, cs], tmp3[:, 0], tmp3[:, 1], ALU.subtract, [tmp3.b], [WB.b])
        tt(tmp3[:, 0], bbim, pr, ALU.mult, rdb, [tmp3.b])
        tt(tmp3[:, 1], bbre, pi, ALU.mult, rdb, [tmp3.b])
        tt(WB[:, :, 1, cs], tmp3[:, 0], tmp3[:, 1], ALU.add, [tmp3.b], [WB.b])
    for tp in range(8):
        kk = tp + 1
        cs = slice(tp * 16, (tp + 1) * 16)
        pr, pi = bc16(S("Pr%d" % kk)), bc16(S("Pi%d" % kk))
        tt(tmp3[:, 0], cre, pr, ALU.mult, rdb, [tmp3.b])
        tt(tmp3[:, 1], cim, pi, ALU.mult, rdb, [tmp3.b])
        tt(WO[:, :, 0, cs], tmp3[:, 0], tmp3[:, 1], ALU.subtract, [tmp3.b], [WO.b])
        tt(tmp3[:, 0], cre, pi, ALU.mult, rdb, [tmp3.b])
        tt(tmp3[:, 1], cim, pr, ALU.mult, rdb, [tmp3.b])
        tt(tmp3[:, 0], tmp3[:, 0], tmp3[:, 1], ALU.add, [tmp3.b], [tmp3.b])
        ts(WO[:, :, 1, cs], tmp3[:, 0], -1.0, ALU.mult, [tmp3.b], [WO.b])
    k.op("dve", lambda e: e.tensor_copy(Wout.ap, WO.ap), reads=[WO.b], writes=[Wout.b])
    tmpW = A.alloc("tmpW", [16, 128], F32)
    rdw = sb_ + [WO.b, tmpW.b]
    tt(WCp[:, :, 0, :], WO[:, :, 0, :], bc128(S("ir")), ALU.mult, rdw, [WCp.b])
    tt(tmpW.ap, WO[:, :, 1, :], bc128(S("ii")), ALU.mult, rdw, [tmpW.b])
    tt(WCp[:, :, 0, :], WCp[:, :, 0, :], tmpW.ap, ALU.add, [WCp.b, tmpW.b], [WCp.b])
    tt(WCp[:, :, 1, :], WO[:, :, 0, :], bc128(S("nii")), ALU.mult, rdw + [WCp.b], [WCp.b])
    tt(tmpW.ap, WO[:, :, 1, :], bc128(S("ir")), ALU.mult, rdw, [tmpW.b])
    tt(WCp[:, :, 1, :], WCp[:, :, 1, :], tmpW.ap, ALU.add, [WCp.b, tmpW.b], [WCp.b])
    for pair in range(16):
        pt = psum()
        for comp in range(2):
            k.op("pe", lambda e, pair=pair, comp=comp, pt=pt: e.transpose(pt[:, comp * 128:(comp + 1) * 128], WB[:, pair, comp, :], identf.ap),
                 reads=[WB.b, identf.b], writes=[pt.b])
        copy(ev_eng(), WinT[:, pair, :, :], pt[:, 0:256].rearrange("p (c q) -> p c q", c=2), [pt.b], [WinT.b])
    for g in range(32):
        pair, j2 = g // 2, g % 2
        ps_ = slice(64 * j2, 64 * j2 + 64)
        pt = psum()
        mm(pt[:, 0:128], WB[ps_, pair, 0, :], WCp[ps_, pair, 0, :], True, False, [WB.b, WCp.b], [pt.b])
        mm(pt[:, 0:128], WB[ps_, pair, 1, :], WCp[ps_, pair, 1, :], False, True, [WB.b, WCp.b], [pt.b])
        tm = tmpM[:, g % 2, :]
        tt(tm, pt[:, 0:128], blockmask.ap, ALU.mult, [pt.b, blockmask.b, tmpM.b], [tmpM.b])
        stt(Mbf[:, g, :], identf.ap, dcol[:, g:g + 1], tm, [identf.b, dcol.b, tmpM.b], [Mbf.bs[g]])
    k.barrier()

    A.seek(m_tmp)
    U = A.alloc("U", [32, 256], BF16, nbuf=32)
    Xp = [[A.alloc("X%d_%d" % (0, j), [2, 256], F32) for j in range(2)]] * 2
    Hb = [A.alloc("Hb%d" % i, [2, 256], BF16) for i in range(2)]
    Tt = A.alloc("Tt", [2, 8, 256], F32)
    tmpT = A.alloc("tmpT", [8, 128], F32)
    A.seek(R2_off)
    Xall = A.alloc("Xall", [16, 2, 256], F32, nbuf=16)

    for j in range(4):
        for g8 in range(0, 8, 2):
            pt = psum()
            for hh in range(2):
                for sp in range(8):
                    mm(pt[:, hh * 256:(hh + 1) * 256], Sel[:, (g8 + hh) * 8 + sp, :], uT[:, j, ds(sp, 256, 8)], sp == 0, sp == 7,
                       [Sel.b, uT.bs[j]], [pt.b])
            g = 8 * j + g8
            copy(ev_eng(), U[:, g:g + 2, :], pt.ap.rearrange("p (a c) -> p a c", a=2), [pt.b], [U.bs[g], U.bs[g + 1]])

    for pair in range(16):
        pt = psum()
        for j2 in range(2):
            for comp in range(2):
                mm(pt[64 * j2:64 * j2 + 64, comp * 256:(comp + 1) * 256], WinT[:, pair, comp, 64 * j2:64 * j2 + 64], U[:, 2 * pair + j2, :],
                   True, True, [WinT.b, U.bs[2 * pair + j2]], [pt.b])
        Xa, Xb = Xp[pair % 2]
        copy("act", Xa.ap, pt.ap.rearrange("p (c n) -> p c n", c=2), [pt.b], [Xa.b])
        src, dst = Xa, Xb
        for lv in range(8):
            sh = 1 << lv
            n = 256 - sh
            er = S("Qr%d" % (3 + lv))[:, pair:pair + 1]
            ei = S("Qi%d" % (3 + lv))[:, pair:pair + 1]
            nei = S("nQi%d" % (3 + lv))[:, pair:pair + 1]
            last = lv == 7
            d_re = Xall[:, pair, 0, :] if last else dst[:, 0, :]
            d_im = Xall[:, pair, 1, :] if last else dst[:, 1, :]
            db = Xall.bs[pair] if last else dst.b
            dfull = Xall[:, pair, :, 0:sh] if last else dst[:, :, 0:sh]
            copy("act", dfull, src[:, :, 0:sh], [src.b], [db])
            rd = sb_ + [src.b, db]
            stt(d_re[:, sh:256], src[:, 0, 0:n], er, src[:, 0, sh:256], rd, [db])
            stt(d_re[:, sh:256], src[:, 1, 0:n], nei, d_re[:, sh:256], rd, [db])
            stt(d_im[:, sh:256], src[:, 1, 0:n], er, src[:, 1, sh:256], rd, [db])
            stt(d_im[:, sh:256], src[:, 0, 0:n], ei, d_im[:, sh:256], rd, [db])
            src, dst = dst, src
        k.op("dve", lambda e, pair=pair: e.tensor_copy(Fs[:, pair, :], Xall[:, pair, :, 255]), reads=[Xall.bs[pair]], writes=[Fs.b])

    if NOCC:
        k.op("dve", lambda e: e.memset(G.ap, 0.0), writes=[G.b])
    else:
        cin = nc.dram_tensor("ssm_cc_in", [128, 32], F32).ap()
        cout = nc.dram_tensor("ssm_cc_out", [NCORES * 128, 32], F32).ap()
        Bcin = Buf("cin")
        Bcout = Buf("cout")
        k.dma("pool", cin, Fs.ap.rearrange("p a b -> p (a b)"), reads=[Fs.b], writes=[Bcin])
        k.wait_bufs("pool", [Bcin])
        ccsem = k.newsem("ccsem")
        k.q["pool"].append(lambda e: e.collective_compute(
            "AllGather", ALU.bypass, replica_groups=[list(range(NCORES))], ins=[cin.opt()], outs=[cout.opt()]).then_inc(ccsem))
        tok = ("dma", ccsem, 1)
        Bcout.writers = {_tkey(tok): tok}
        _tadd(k.dma_toks, tok)
        k.dma("sp", G.ap, cout.rearrange("(r p) c -> p r c", p=128), reads=[Bcout], writes=[G.b])
    Er, Ei = S("Qr11"), S("Qi11")
    Gv = G.ap.rearrange("p r (a c) -> p r a c", c=2)
    k.op("dve", lambda e: e.memset(Hinit.ap, 0.0), writes=[Hinit.b])
    k.op("dve", lambda e: e.memset(S("ac_r"), 0.0), reads=sb_, writes=sb_)
    k.op("dve", lambda e: e.memset(S("ac_i"), 0.0), reads=sb_, writes=sb_)
    for r in range(8):
        rdh = sb_ + [Hinit.b, oneh.b, G.b]
        stt(Hinit[:, :, 0], S("ac_r"), oneh[:, r:r + 1], Hinit[:, :, 0], rdh, [Hinit.b])
        stt(Hinit[:, :, 1], S("ac_i"), oneh[:, r:r + 1], Hinit[:, :, 1], rdh, [Hinit.b])
        if r == 7:
            break
        cmul(S("hn_r"), S("hn_i"), S("ac_r"), S("ac_i"), Er, Ei)
        tt(S("ac_r"), S("hn_r"), Gv[:, r, :, 0], ALU.add, rdh, sb_)
        tt(S("ac_i"), S("hn_i"), Gv[:, r, :, 1], ALU.add, rdh, sb_)
    ts(nHim.ap, Hinit[:, :, 1], -1.0, ALU.mult, [Hinit.b], [nHim.b])

    def bcn(ap, n):
        return ap.unsqueeze(2).to_broadcast([128, 8, n])

    for half in range(2):
        ps8 = slice(half * 8, half * 8 + 8)
        rdt = sb_ + [Tt.b, tmpT.b]
        k.op("dve", lambda e, ps8=ps8: e.tensor_copy(Tt[:, 0, :, 0:1], S("Qr3")[:, ps8].unsqueeze(2)), reads=rdt, writes=[Tt.b])
        k.op("dve", lambda e, ps8=ps8: e.tensor_copy(Tt[:, 1, :, 0:1], S("Qi3")[:, ps8].unsqueeze(2)), reads=rdt, writes=[Tt.b])
        for lv in range(8):
            sh = 1 << lv
            er = bcn(S("Qr%d" % (3 + lv))[:, ps8], sh)
            ei = bcn(S("Qi%d" % (3 + lv))[:, ps8], sh)
            sre, sim = Tt[:, 0, :, 0:sh], Tt[:, 1, :, 0:sh]
            dre, dim = Tt[:, 0, :, sh:2 * sh], Tt[:, 1, :, sh:2 * sh]
            tv = tmpT[:, :, 0:sh]
            tt(dre, sre, er, ALU.mult, rdt, [Tt.b])
            tt(tv, sim, ei, ALU.mult, rdt, [tmpT.b])
            tt(dre, dre, tv, ALU.subtract, rdt, [Tt.b])
            tt(dim, sre, ei, ALU.mult, rdt, [Tt.b])
            tt(tv, sim, er, ALU.mult, rdt, [tmpT.b])
            tt(dim, dim, tv, ALU.add, rdt, [Tt.b])
        for pl_ in range(8):
            pair = half * 8 + pl_
            xb = Xall.bs[pair]
            hr = Hinit[:, pair, 0:1]
            hi = Hinit[:, pair, 1:2]
            nhi = nHim[:, pair:pair + 1]
            rdx = [Tt.b, Hinit.b, nHim.b, xb]
            stt(Xall[:, pair, 0, :], Tt[:, 0, pl_, :], hr, Xall[:, pair, 0, :], rdx, [xb])
            stt(Xall[:, pair, 0, :], Tt[:, 1, pl_, :], nhi, Xall[:, pair, 0, :], rdx, [xb])
            stt(Xall[:, pair, 1, :], Tt[:, 0, pl_, :], hi, Xall[:, pair, 1, :], rdx, [xb])
            stt(Xall[:, pair, 1, :], Tt[:, 1, pl_, :], hr, Xall[:, pair, 1, :], rdx, [xb])
            hb_ = Hb[pair % 2]
            copy("act", hb_[:, :, 1:256], Xall[:, pair, :, 0:255], [xb], [hb_.b])
            k.op("dve", lambda e, hb_=hb_, pair=pair: e.tensor_copy(hb_[:, :, 0:1], Hinit[:, pair, :].unsqueeze(2)),
                 reads=[Hinit.b, hb_.b], writes=[hb_.b])
            for j2 in range(2):
                g = 2 * pair + j2
                ps_ = slice(64 * j2, 64 * j2 + 64)
                pt = psum()
                mm(pt[:, 0:256], Mbf[:, g, :], U[:, g, :], True, False, [Mbf.bs[g], U.bs[g]], [pt.b])
                mm(pt[:, 0:256], Wout[ps_, pair, 0, :], hb_[ps_, 0, :], False, False, [Wout.b, hb_.b], [pt.b])
                mm(pt[:, 0:256], Wout[ps_, pair, 1, :], hb_[ps_, 1, :], False, True, [Wout.b, hb_.b], [pt.b])
                copy(ev_eng(), U[:, g, :], pt[:, 0:256], [pt.b], [U.bs[g]])
    for j in range(4):
        for tp in range(0, 8, 2):
            pt = psum()
            for hh in range(2):
                for g8 in range(8):
                    mm(pt[:, hh * 256:(hh + 1) * 256], Sel[:, (tp + hh) * 8 + g8, :], U[:, 8 * j + g8, :], g8 == 0, g8 == 7,
                       [Sel.b] + [U.bs[8 * j + g8]], [pt.b])
            for hh in range(2):
                k.op("act", lambda e, j=j, tp=tp, hh=hh, pt=pt: e.activation(gyT[:, j, ds(tp + hh, 256, 8)], pt[:, hh * 256:(hh + 1) * 256],
                                                                           AF.Gelu_apprx_tanh),
                     reads=[pt.b], writes=[gyT.bs[j]])


def tail_phase(k, A, L, with_mix):
    ident, R1, R2, attnT, gyT, gb = L["ident"], L["R1"], L["R2"], L["attnT"], L["gyT"], L["gb"]
    psum, copy, mm, ev_eng, build_T, load_w = L["psum"], L["copy"], L["mm"], L["ev_eng"], L["build_T"], L["load_w"]
    x_own, p_own, out_d = L["x_own"], L["p_own"], L["out_d"]
    w_in, w_ap, w_ga, w_gb, w_out = L["w_in"], L["w_ap"], L["w_ga"], L["w_gb"], L["w_out"]
    w_fg, w_fu, w_fd, w_pg, w_pp = L["w_fg"], L["w_fu"], L["w_fd"], L["w_pg"], L["w_pp"]
    stg, X0, R2_off, alloc_staging = L["stg"], L["X0"], L["R2_off"], L["alloc_staging"]
    g_ffn_d, g_final_d = L["g_ffn_d"], L["g_final_d"]
    KB_ = 1024

    A.seek(X0 + 64 * KB_)
    ws = [A.alloc("ws%d" % i, [8, 512], BF16) for i in range(4)]
    ws_i = [0]

    def wslot():
        w = ws[ws_i[0] % 4]
        ws_i[0] += 1
        return w

    M0 = A.mark()

    if with_mix:
        A.seek(X0 + 32 * KB_)
        tmp = [A.alloc("mt%d" % i, [3, 512], F32) for i in range(2)]
        wA = A.alloc("wA", [4, 1024], BF16)
        wGa = A.alloc("wGa", [4, 1024], BF16)
        wGb = A.alloc("wGb", [4, 1024], BF16)
        wGt = A.alloc("wGt", [2, 8, 1024], BF16, nbuf=4)
        for wt, src in ((wA, w_ap), (wGa, w_ga), (wGb, w_gb)):
            k.dma("pool", wt.ap, src.rearrange("(a p) n -> p a n", p=128), writes=[wt.b])
        for gi in range(2):
            for hb_ in range(2):
                c0 = 5120 + gi * 1024 + hb_ * 512
                k.dma("pool", wGt[:, gi, :, hb_ * 512:(hb_ + 1) * 512],
                      w_in[:, c0:c0 + 512].rearrange("(a p) n -> p a n", p=128), writes=[wGt.bs[gi * 2 + hb_]])

        class _V:
            def __init__(self, fn, b):
                self.fn, self.b = fn, b

            def __getitem__(self, key):
                return self.fn(key)

        for f in range(8):
            fs = slice(f * 128, (f + 1) * 128)
            fo = f * 128

            def _wa(key, fo=fo):
                p_, kk, cs = key
                if cs.start == 0:
                    return (wA if kk < 4 else wGa)[:, kk % 4, fo:fo + 128]
                return wGb[:, kk, fo:fo + 128]

            def _wg(key, fo=fo):
                p_, kk, cs = key
                return wGt[:, 0 if cs.start == 0 else 1, kk, fo:fo + 128]

            wa = _V(_wa, Buf("wa_dummy"))
            wg = _V(_wg, Buf("wg_dummy"))
            wa.bl = [wA.b, wGa.b, wGb.b]
            wg.bl = [wGt.bs[(f // 4)], wGt.bs[2 + (f // 4)]]
            for tc in range(4):
                ts_ = slice(tc * 512, (tc + 1) * 512)
                tb = [R1.bs[i] for i in range(tc * 4, tc * 4 + 4)]
                tm = tmp[tc % 2]
                pa = psum()
                for kk in range(4):
                    mm(pa.ap, wa[:, kk, 0:128], attnT[:, kk, ts_], kk == 0, kk == 3, wa.bl + attnT.bs, [pa.b])
                pga = psum()
                for kk in range(8):
                    mm(pga.ap, wg[:, kk, 0:128], R1[:, kk, ts_], kk == 0, kk == 7, wg.bl + tb, [pga.b])
                k.op("act", lambda e, tm=tm, pga=pga: e.activation(tm[:, 0, :], pga.ap, AF.Sigmoid), reads=[pga.b], writes=[tm.b])
                k.op("dve", lambda e, tm=tm, pa=pa: e.tensor_tensor(tm[:, 0, :], tm[:, 0, :], pa.ap, ALU.mult),
                     reads=[pa.b, tm.b], writes=[tm.b])
                pya = psum()
                for kk in range(4):
                    mm(pya.ap, wa[:, 4 + kk, 0:128], gyT[:, kk, ts_], kk == 0, kk == 3, wa.bl + gyT.bs, [pya.b])
                pyb = psum()
                for kk in range(4):
                    mm(pyb.ap, wa[:, kk, 128:256], gyT[:, kk, ts_], kk == 0, kk == 3, wa.bl + gyT.bs, [pyb.b])
                pgs = psum()
                for kk in range(8):
                    mm(pgs.ap, wg[:, kk, 128:256], R1[:, kk, ts_], kk == 0, kk == 7, wg.bl + tb, [pgs.b])
                k.op("act", lambda e, tm=tm, pyb=pyb: e.activation(tm[:, 1, :], pyb.ap, AF.Sigmoid), reads=[pyb.b], writes=[tm.b])
                k.op("act", lambda e, tm=tm, pgs=pgs: e.activation(tm[:, 2, :], pgs.ap, AF.Sigmoid), reads=[pgs.b], writes=[tm.b])
                k.op("dve", lambda e, tm=tm, pya=pya: e.tensor_tensor(tm[:, 1, :], tm[:, 1, :], pya.ap, ALU.mult),
                     reads=[pya.b, tm.b], writes=[tm.b])
                k.op("dve", lambda e, tm=tm: e.tensor_tensor(tm[:, 1, :], tm[:, 1, :], tm[:, 2, :], ALU.mult),
                     reads=[tm.b], writes=[tm.b])
                k.op("dve", lambda e, tm=tm, f=f, ts_=ts_: e.tensor_tensor(R2[:, f, ts_], tm[:, 0, :], tm[:, 1, :], ALU.add),
                     reads=[tm.b], writes=[R2.bs[i] for i in range(tc * 4, tc * 4 + 4)])
        k.barrier()

    A.seek(X0)
    resid = A.alloc("resid", [NT, D], F32, nbuf=NT)
    for t in range(NT):
        k.dma("sp", resid[:, t, :], x_own[t * 128:(t + 1) * 128, :], writes=[resid.bs[t]])

    def add_resid(t, half, pt):
        cs = slice(half * 512, (half + 1) * 512)
        k.op("dve", lambda e: e.tensor_tensor(resid[:, t, cs], resid[:, t, cs], pt.ap, ALU.add),
             reads=[pt.b, resid.bs[t]], writes=[resid.bs[t]])

    if with_mix:
        for half in range(2):
            wo = wslot()
            load_w(wo, wo.ap, w_out[:, half * 512:(half + 1) * 512], 8)
            for t in range(NT):
                pt = psum()
                for kk in range(8):
                    mm(pt.ap, R2[:, kk, t * 128:(t + 1) * 128], wo[:, kk, :], kk == 0, kk == 7, [R2.bs[t], wo.b], [pt.b])
                add_resid(t, half, pt)
        k.barrier()

    A.seek(R2_off + 16 * KB_)
    for nm, src in (("g_ffn", g_ffn_d), ("g_final", g_final_d)):
        t_ = A.alloc(nm, [D], F32)
        k.dma("sp", t_.ap, src[0:1, :].to_broadcast([128, D]), writes=[t_.b])
        gb[nm] = t_

    A.seek(R2_off)
    ws_x = [A.alloc("wsx0", [8, 512], BF16), A.alloc("wsx1", [8, 512], BF16)]
    A.seek(R2_off + 24 * KB_)
    ws_x.append(A.alloc("wsx2", [8, 512], BF16))
    ffn_ws = ws + ws_x
    ffn_i = [0]

    def fslot():
        w = ffn_ws[ffn_i[0] % len(ffn_ws)]
        ffn_i[0] += 1
        return w

    A.seek(M0)
    alloc_staging(256)
    M1 = A.mark()

    def resid_src(t):
        return resid[:, t, :], [resid.bs[t]]

    build_T(resid_src, NT, R1, 0, gb["g_ffn"])
    actT = A.alloc("actT", [4, TOK], BF16, nbuf=16)
    sg = [A.alloc("sg%d" % i, [512], F32) for i in range(2)]
    nfb = (DFF + 511) // 512
    for fb in range(nfb):
        c0 = fb * 512
        cw = min(512, DFF - c0)
        nft = cw // 128
        wg_ = fslot()
        wu_ = fslot()
        load_w(wg_, wg_[:, :, 0:cw], w_fg[:, c0:c0 + cw], 8)
        load_w(wu_, wu_[:, :, 0:cw], w_fu[:, c0:c0 + cw], 8)
        wd_ = [fslot(), fslot()]
        for half in range(2):
            load_w(wd_[half], wd_[half][:, 0:nft, :], w_fd[c0:c0 + cw, half * 512:(half + 1) * 512], nft)
        for ft in range(nft):
            for tc in range(4):
                ts_ = slice(tc * 512, (tc + 1) * 512)
                tb = [R1.bs[i] for i in range(tc * 4, tc * 4 + 4)]
                pg_ = psum()
                for kk in range(8):
                    mm(pg_.ap, wg_[:, kk, ft * 128:(ft + 1) * 128], R1[:, kk, ts_], kk == 0, kk == 7, [wg_.b] + tb, [pg_.b])
                pu_ = psum()
                for kk in range(8):
                    mm(pu_.ap, wu_[:, kk, ft * 128:(ft + 1) * 128], R1[:, kk, ts_], kk == 0, kk == 7, [wu_.b] + tb, [pu_.b])
                s_ = sg[(ft * 4 + tc) % 2]
                k.op("act", lambda e, s_=s_, pg_=pg_: e.activation(s_.ap, pg_.ap, AF.Silu), reads=[pg_.b], writes=[s_.b])
                k.op("dve", lambda e, s_=s_, pu_=pu_, ft=ft, ts_=ts_: e.tensor_tensor(actT[:, ft, ts_], s_.ap, pu_.ap, ALU.mult),
                     reads=[s_.b, pu_.b], writes=[actT.bs[i] for i in range(tc * 4, tc * 4 + 4)])
        for half in range(2):
            for t in range(NT):
                pt = psum()
                for kk in range(nft):
                    mm(pt.ap, actT[:, kk, t * 128:(t + 1) * 128], wd_[half][:, kk, :], kk == 0, kk == nft - 1,
                       [actT.bs[t], wd_[half].b], [pt.b])
                add_resid(t, half, pt)
    k.barrier()

    A.seek(M1)
    build_T(resid_src, NT, R1, 0, None, norm=False)

    def p_src(t):
        xt = stg["xts"][t % 3]
        k.dma("sp", xt[:, 0:256], p_own[t * 128:(t + 1) * 128, :], writes=[xt.b])
        return xt[:, 0:256], [xt.b]

    build_T(p_src, NT, R2, 0, None, norm=False, ncol=256)
    sgp = [A.alloc("sgp%d" % i, [512], F32) for i in range(2)]
    for half in range(2):
        wg_ = wslot()
        wp_ = wslot()
        load_w(wg_, wg_.ap, w_pg[:, half * 512:(half + 1) * 512], 8)
        load_w(wp_, wp_[:, 0:2, :], w_pp[:, half * 512:(half + 1) * 512], 2)
        for t in range(NT):
            pg_ = psum()
            for kk in range(8):
                mm(pg_.ap, R1[:, kk, t * 128:(t + 1) * 128], wg_[:, kk, :], kk == 0, kk == 7, [R1.bs[t], wg_.b], [pg_.b])
            pp_ = psum()
            for kk in range(2):
                mm(pp_.ap, R2[:, kk, t * 128:(t + 1) * 128], wp_[:, kk, :], kk == 0, kk == 1, [R2.bs[t], wp_.b], [pp_.b])
            s_ = sgp[t % 2]
            k.op("act", lambda e, s_=s_, pg_=pg_: e.activation(s_.ap, pg_.ap, AF.Sigmoid), reads=[pg_.b], writes=[s_.b])
            k.op("dve", lambda e, s_=s_, pp_=pp_: e.tensor_tensor(s_.ap, s_.ap, pp_.ap, ALU.mult), reads=[s_.b, pp_.b], writes=[s_.b])
            cs = slice(half * 512, (half + 1) * 512)
            k.op("dve", lambda e, s_=s_, t=t, cs=cs: e.tensor_tensor(resid[:, t, cs], resid[:, t, cs], s_.ap, ALU.add),
                 reads=[s_.b, resid.bs[t]], writes=[resid.bs[t]])
    fst = A.alloc("fst", [NT, 4], F32)
    fj = A.alloc("fj", [D], BF16)
    oo = [A.alloc("oo%d" % i, [D], F32) for i in range(2)]
    gfin = gb["g_final"]
    for t in range(NT):
        k.op("act", lambda e, t=t: e.activation(fj.ap, resid[:, t, :], AF.Square, accum_out=fst[:, t, 0:1]),
             reads=[resid.bs[t]], writes=[fj.b, fst.b])
        k.op("dve", lambda e, t=t: e.tensor_scalar(fst[:, t, 1:2], fst[:, t, 0:1], 1.0 / D, EPS, op0=ALU.mult, op1=ALU.add),
             reads=[fst.b], writes=[fst.b])
        k.op("act", lambda e, t=t: e.activation(fst[:, t, 2:3], fst[:, t, 1:2], AF.Sqrt), reads=[fst.b], writes=[fst.b])
        k.op("dve", lambda e, t=t: e.reciprocal(fst[:, t, 3:4], fst[:, t, 2:3]), reads=[fst.b], writes=[fst.b])
        o_ = oo[t % 2]
        k.op("dve", lambda e, t=t, o_=o_: e.scalar_tensor_tensor(o_.ap, resid[:, t, :], fst[:, t, 3:4], gfin.ap,
                                                                  op0=ALU.mult, op1=ALU.mult),
             reads=[resid.bs[t], fst.b, gfin.b], writes=[o_.b])
        k.dma("sp", out_d[t * 128:(t + 1) * 128, :], o_.ap, reads=[o_.b])
    k.wait_bufs("sp", [oo[0].b, oo[1].b])


_NC_CACHE = {}


def _host_inputs(inp):
    x = np.ascontiguousarray(inp["x"][0])
    p = np.ascontiguousarray(inp["p"][0, 0])
    pos = np.ascontiguousarray(inp["positions"][0]).astype(np.int32)
    half = 16
    invf = (np.float32(500000.0) ** (-np.arange(half, dtype=np.float32) * np.float32(2.0) / np.float32(32))).astype(np.float32)
    invf_t = np.ascontiguousarray(np.broadcast_to(invf[None, :], (128, 16))).astype(np.float32)

    def l2(a):
        return np.ascontiguousarray(a.reshape(16, 2, 64).transpose(1, 2, 0).reshape(128, 16))

    a_re, a_im = inp["a_re"][0], inp["a_im"][0]
    ldt = inp["log_dt"][0]
    ldt2 = np.ascontiguousarray(np.broadcast_to(ldt.reshape(16, 2, 1).transpose(1, 2, 0), (2, 64, 16)).reshape(128, 16))
    b_re, b_im = inp["b_re"][0], inp["b_im"][0]
    c_re, c_im = inp["c_re"][0], inp["c_im"][0]

    def lb(b):
        return np.ascontiguousarray(b.reshape(16, 2, 64, 16).transpose(1, 2, 0, 3).reshape(128, 16, 16))

    def lc(c):
        return np.ascontiguousarray(c.reshape(16, 2, 16, 64).transpose(1, 3, 0, 2).reshape(128, 16, 16))

    d = inp["d_skip"][0]
    dcol = np.ascontiguousarray(np.broadcast_to(d.T[None, :, :], (8, 16, 32)).reshape(128, 32))
    shared = {
        "invf": invf_t,
        "g_mix": np.ascontiguousarray(inp["g_mix"]), "g_ffn": np.ascontiguousarray(inp["g_ffn"]),
        "g_final": np.ascontiguousarray(inp["g_final"].reshape(1, D)),
        "w_in": np.ascontiguousarray(inp["w_in"][0]), "w_attn_proj": np.ascontiguousarray(inp["w_attn_proj"][0]),
        "w_glu_a": np.ascontiguousarray(inp["w_glu_a"][0]), "w_glu_b": np.ascontiguousarray(inp["w_glu_b"][0]),
        "w_out": np.ascontiguousarray(inp["w_out"][0]), "w_ffn_gate": np.ascontiguousarray(inp["w_ffn_gate"][0]),
        "w_ffn_up": np.ascontiguousarray(inp["w_ffn_up"][0]), "w_ffn_down": np.ascontiguousarray(inp["w_ffn_down"][0]),
        "w_ple_gate": np.ascontiguousarray(inp["w_ple_gate"][0]), "w_ple_proj": np.ascontiguousarray(inp["w_ple_proj"][0]),
        "lr2": l2(a_re), "li2": l2(a_im), "ldt2": ldt2.astype(np.float32),
        "b2re": lb(b_re), "b2im": lb(b_im), "c2re": lc(c_re), "c2im": lc(c_im), "dcol": dcol.astype(np.float32),
    }
    maps = []
    for c in range(NCORES):
        xo = x[c * TOK:(c + 1) * TOK]
        if c == 0:
            xp = np.zeros_like(xo)
            pp = np.zeros(TOK, np.int32)
        else:
            xp = x[(c - 1) * TOK:c * TOK]
            pp = pos[(c - 1) * TOK:c * TOK]
        pcat = np.concatenate([pp, pos[c * TOK:(c + 1) * TOK]])
        cols = []
        for g in range(3):
            h, o = group_tiles(g)
            for (s0, dd) in h + o:
                cols.append(pcat[s0 + dd * np.arange(128)])
        pos_tab = np.ascontiguousarray(np.stack(cols, axis=1)).astype(np.int32)
        m = dict(shared)
        m["x_own"] = np.ascontiguousarray(xo)
        m["x_prev"] = np.ascontiguousarray(xp)
        m["p_own"] = np.ascontiguousarray(p[c * TOK:(c + 1) * TOK])
        m["pos_tab"] = pos_tab
        m["halo_bias"] = np.full((128, 1), NEG if c == 0 else 0.0, np.float32)
        oh = np.zeros((128, 8), np.float32)
        oh[:, c] = 1.0
        m["onehot"] = oh
        maps.append(m)
    return maps


def run(inp, stage="full", trace=False):
    if stage not in _NC_CACHE:
        _NC_CACHE[stage] = build(stage)
    nc = _NC_CACHE[stage]
    maps = _host_inputs(inp)
    res = run_bass_kernel_spmd(nc, maps, core_ids=list(range(NCORES)), **({"trace": True} if trace else {}))
    return res


def kernel(**inputs):
    res = run(inputs, "full")
    out = np.concatenate([res.results[c]["out"] for c in range(NCORES)], axis=0)
    return out.reshape(1, NCORES * TOK, D).astype(np.float32)
```

```python
import math
import os
import numpy as np
from contextlib import ExitStack
import concourse.bass as bass
import concourse.mybir as mybir
from concourse.bass_utils import run_bass_kernel_spmd

F32 = mybir.dt.float32
BF16 = mybir.dt.bfloat16
I32 = mybir.dt.int32
AF = mybir.ActivationFunctionType
ALU = mybir.AluOpType
AX = mybir.AxisListType
ds = bass.ds

ENGS = ("pe", "act", "dve", "pool", "sp")
NCORES = 8
TOK = 2048
NT = 16
D = 1024
DFF = 2816
EPS = 1e-6
NEG = -30000.0
SCALE = 1.0 / math.sqrt(128.0)


class Buf:
    __slots__ = ("name", "writers", "readers", "dsem", "dcnt")

    def __init__(self, name):
        self.name = name
        self.writers = {}
        self.readers = {}
        self.dsem = None
        self.dcnt = 0


def _tkey(t):
    return ("eng", t[1]) if t[0] == "eng" else ("dma", id(t[1]))


def _tadd(d, t):
    kx = _tkey(t)
    if kx not in d or d[kx][2] < t[2]:
        d[kx] = t


class KB:
    def __init__(self):
        self.nc = bass.Bass("TRN2", target_bir_lowering=False)
        self.es = ExitStack()
        self.q = {e: [] for e in ENGS}
        self.cnt = {e: 0 for e in ENGS}
        self.waited = {}
        self.psem = {e: self.es.enter_context(self.nc.semaphore("prog_" + e)) for e in ENGS}
        self.dma_toks = {}
        self.needed = {e: set() for e in ENGS}
        self.rank = {}

    def sb(self, name, shape, dt):
        return self.es.enter_context(self.nc.sbuf_tensor(name, list(shape), dt))

    def ps(self, name, shape, dt):
        return self.es.enter_context(self.nc.psum_tensor(name, list(shape), dt))

    def dram(self, name, shape, dt, kind):
        return self.nc.dram_tensor(name, list(shape), dt, kind=kind).ap()

    def newsem(self, name):
        self.nsem = getattr(self, "nsem", 0) + 1
        return self.es.enter_context(self.nc.semaphore("%s_%d" % (name, self.nsem)))

    def _deps(self, reads, writes):
        toks = []
        for b in reads:
            toks += list(b.writers.values())
        for b in writes:
            toks += list(b.writers.values())
            toks += list(b.readers.values())
        return toks

    def _emit_waits(self, e, toks):
        need = {}
        for t in toks:
            if t[0] == "eng":
                _, te, idx = t
                if te == e and e == "pe":
                    continue
                key = ("eng", te)
                sem = self.psem[te]
                val = idx
            else:
                _, sem, val = t
                key = ("dma", id(sem))
            if self.waited.get((e, key), 0) >= val:
                continue
            if key not in need or need[key][1] < val:
                need[key] = (sem, val)
        for key, (sem, val) in need.items():
            self.waited[(e, key)] = val
            if key[0] == "eng":
                te = key[1]
                self.needed[te].add(val)
                self.q[e].append(lambda eng, sem=sem, te=te, val=val: eng.wait_ge(sem, self.rank[te][val]))
            else:
                self.q[e].append(lambda eng, sem=sem, val=val: eng.wait_ge(sem, val))

    def _commit(self, tok, reads, writes):
        for b in writes:
            b.writers = {_tkey(tok): tok}
            b.readers = {}
        for b in reads:
            _tadd(b.readers, tok)

    def op(self, e, fn, reads=(), writes=()):
        reads = list(reads)
        writes = list(writes)
        self._emit_waits(e, self._deps(reads, writes))
        self.cnt[e] += 1
        idx = self.cnt[e]
        sem = self.psem[e]
        def _emit(eng, fn=fn, sem=sem, e=e, idx=idx):
            ins = fn(eng)
            if idx in self.needed[e]:
                ins.then_inc(sem, 1)
        self.q[e].append(_emit)
        tok = ("eng", e, idx)
        self._commit(tok, reads, writes)
        return tok

    def dma(self, e, out, in_, reads=(), writes=(), sembuf=None, **kw):
        reads = list(reads)
        writes = list(writes)
        sb_ = sembuf or (writes[0] if writes else reads[0])
        if sb_.dsem is None:
            sb_.dsem = self.newsem("d_" + sb_.name)
        self._emit_waits(e, self._deps(reads, writes))
        sb_.dcnt += 16
        val = sb_.dcnt
        sem = sb_.dsem
        self.q[e].append(lambda eng, out=out, in_=in_, sem=sem, kw=kw:
                         eng.dma_start(out=out, in_=in_, **kw).then_inc(sem, 16))
        tok = ("dma", sem, val)
        _tadd(self.dma_toks, tok)
        self._commit(tok, reads, writes)
        return tok

    def wait_bufs(self, e, bufs):
        toks = []
        for b in bufs:
            toks += list(b.writers.values()) + list(b.readers.values())
        self._emit_waits(e, toks)

    def barrier(self):
        toks = [("eng", e, self.cnt[e]) for e in ENGS if self.cnt[e] > 0]
        toks += list(self.dma_toks.values())
        for e in ENGS:
            self._emit_waits(e, toks)

    def finish(self):
        nc = self.nc
        q = self.q
        for e in ENGS:
            self.rank[e] = {idx: r + 1 for r, idx in enumerate(sorted(self.needed[e]))}
        with nc.Block() as block:
            @block.tensor
            def _(eng):
                for f in q["pe"]:
                    f(eng)

            @block.scalar
            def _(eng):
                for f in q["act"]:
                    f(eng)

            @block.vector
            def _(eng):
                for f in q["dve"]:
                    f(eng)

            @block.gpsimd
            def _(eng):
                for f in q["pool"]:
                    f(eng)

            @block.sync
            def _(eng):
                for f in q["sp"]:
                    f(eng)
        self.es.close()
        return nc


class Tile:
    def __init__(self, ap, name, nbuf=1):
        self.ap = ap
        self.b = Buf(name)
        self.bs = [self.b] if nbuf == 1 else [Buf("%s_%d" % (name, i)) for i in range(nbuf)]

    def __getitem__(self, key):
        return self.ap[key]


class Arena:
    def __init__(self, k, nbytes):
        self.k = k
        self.t = k.sb("arena", [128, nbytes // 4], F32)
        self.off = 0
        self.nbytes = nbytes

    def alloc(self, name, free_shape, dt, nbuf=1):
        esz = 4 if dt in (F32, I32) else 2
        n = int(np.prod(free_shape)) * esz
        n = (n + 31) // 32 * 32
        assert self.off + n <= self.nbytes, (name, self.off, n, self.nbytes)
        v = self.t[:, self.off // 4:(self.off + n) // 4]
        if dt != F32:
            v = v.bitcast(dt)
        tot = int(np.prod(free_shape))
        v = v[:, 0:tot]
        if len(free_shape) == 2:
            v = v.rearrange("p (a b) -> p a b", a=free_shape[0])
        elif len(free_shape) == 3:
            v = v.rearrange("p (a b c) -> p a b c", a=free_shape[0], b=free_shape[1])
        elif len(free_shape) == 4:
            v = v.rearrange("p (a b c d) -> p a b c d", a=free_shape[0], b=free_shape[1], c=free_shape[2])
        self.off += n
        return Tile(v, name, nbuf)

    def mark(self):
        return self.off

    def seek(self, off):
        self.off = off

    def release(self, m):
        self.off = m


def group_tiles(g):
    if g == 0:
        return [(TOK - 128, 1)], [(TOK + 128 * n, 1) for n in range(16)]
    if g == 1:
        return ([(TOK - 512 + r, 4) for r in range(4)],
                [(TOK + 512 * n + r, 4) for r in range(4) for n in range(4)])
    return [(r, 16) for r in range(16)], [(TOK + r, 16) for r in range(16)]


ROPE_COL0 = {}
_c = 0
for _g in range(3):
    _h, _o = group_tiles(_g)
    ROPE_COL0[_g] = (_c, _c + len(_h))
    _c += len(_h) + len(_o)
NROPE = _c


def _cw_consts():
    two_pi = 2.0 * math.pi
    c1 = 6.28125
    r = two_pi - c1
    c2 = float(np.float32(r))
    m, ex = math.frexp(r)
    c2 = math.ldexp(round(m * 4096) / 4096.0, ex)
    c3 = float(np.float32(two_pi - c1 - c2))
    return c1, c2, c3


def build(stage="full"):
    k = KB()
    nc = k.nc
    dbg = stage != "full"

    def din(name, shape, dt=F32):
        return k.dram(name, shape, dt, "ExternalInput")

    x_own = din("x_own", [TOK, D])
    x_prev = din("x_prev", [TOK, D])
    p_own = din("p_own", [TOK, 256])
    pos_tab = din("pos_tab", [128, NROPE], I32)
    invf_d = din("invf", [128, 16])
    halo_bias_d = din("halo_bias", [128, 1])
    onehot_d = din("onehot", [128, 8])
    g_mix_d = din("g_mix", [1, D])
    g_ffn_d = din("g_ffn", [1, D])
    g_final_d = din("g_final", [1, D])
    w_in = din("w_in", [D, 7168])
    w_ap = din("w_attn_proj", [512, D])
    w_ga = din("w_glu_a", [512, D])
    w_gb = din("w_glu_b", [512, D])
    w_out = din("w_out", [D, D])
    w_fg = din("w_ffn_gate", [D, DFF])
    w_fu = din("w_ffn_up", [D, DFF])
    w_fd = din("w_ffn_down", [DFF, D])
    w_pg = din("w_ple_gate", [D, D])
    w_pp = din("w_ple_proj", [256, D])
    lr2_d = din("lr2", [128, 16])
    li2_d = din("li2", [128, 16])
    ldt2_d = din("ldt2", [128, 16])
    b2re_d = din("b2re", [128, 16, 16])
    b2im_d = din("b2im", [128, 16, 16])
    c2re_d = din("c2re", [128, 16, 16])
    c2im_d = din("c2im", [128, 16, 16])
    dcol_d = din("dcol", [128, 32])
    out_d = k.dram("out", [TOK, D], F32, "ExternalOutput")
    dbg_d = k.dram("dbg", [512, TOK], F32, "ExternalOutput") if dbg else None

    A = Arena(k, 204800)
    ps_big = Tile(k.ps("ps_big", [128, 1024], F32)[:, :], "ps_big")
    ps_pool = [Tile(k.ps("ps%d" % i, [128, 512], F32)[:, :], "ps%d" % i) for i in range(6)]
    ps_i = [0]

    def psum():
        t = ps_pool[ps_i[0] % len(ps_pool)]
        ps_i[0] += 1
        return t

    rr = {"ev": 0}

    def ev_eng():
        rr["ev"] += 1
        return "act" if rr["ev"] % 2 else "dve"

    def copy(eng, out, in_, reads, writes):
        if eng == "act":
            k.op("act", lambda e: e.copy(out, in_), reads, writes)
        else:
            k.op(eng, lambda e: e.tensor_copy(out, in_), reads, writes)

    def mm(out, lhsT, rhs, start, stop, reads, writes):
        k.op("pe", lambda e: e.matmul(out, lhsT=lhsT, rhs=rhs, start=start, stop=stop), reads, writes)

    identf = A.alloc("identf", [128], F32)
    ident = A.alloc("ident", [128], BF16)
    k.op("pool", lambda e: e.memset(identf.ap, 1.0), writes=[identf.b])
    k.op("pool", lambda e: e.affine_select(identf.ap, identf.ap, pattern=[[-1, 128]], compare_op=ALU.is_equal,
                                           fill=0.0, base=0, channel_multiplier=1),
         reads=[identf.b], writes=[identf.b])
    k.op("dve", lambda e: e.tensor_copy(ident.ap, identf.ap), reads=[identf.b], writes=[ident.b])

    gb = {}
    t = A.alloc("g_mix", [D], F32)
    k.dma("sp", t.ap, g_mix_d[0:1, :].to_broadcast([128, D]), writes=[t.b])
    gb["g_mix"] = t

    stg = {}

    def alloc_staging(xw=D):
        stg["xts"] = [A.alloc("xt%d" % i, [xw], F32) for i in range(3)]
        stg["xns"] = [A.alloc("xn%d" % i, [D], BF16) for i in range(2)]
        stg["junk"] = A.alloc("junk", [D], BF16)
        stg["stat"] = A.alloc("stat", [64, 4], F32)

    stat_i = [0]

    def build_T(src_fn, ntiles, dstT, dst_col0, gain, norm=True, ncol=D):
        xns, junk, stat = stg["xns"], stg["junk"], stg["stat"]
        nk = ncol // 128
        for t in range(ntiles):
            src_ap, src_bufs = src_fn(t)
            xn = xns[t % 2]
            if norm:
                si = stat_i[0] % 64
                stat_i[0] += 1
                st = stat
                k.op("act", lambda e, s=src_ap, si=si: e.activation(junk[:, 0:ncol], s, AF.Square,
                                                                      accum_out=st[:, si, 0:1]),
                     reads=src_bufs, writes=[junk.b, st.b])
                k.op("dve", lambda e, si=si: e.tensor_scalar(st[:, si, 1:2], st[:, si, 0:1], 1.0 / ncol, EPS,
                                                              op0=ALU.mult, op1=ALU.add),
                     reads=[st.b], writes=[st.b])
                k.op("act", lambda e, si=si: e.activation(st[:, si, 2:3], st[:, si, 1:2], AF.Sqrt),
                     reads=[st.b], writes=[st.b])
                k.op("dve", lambda e, si=si: e.reciprocal(st[:, si, 3:4], st[:, si, 2:3]),
                     reads=[st.b], writes=[st.b])
                k.op("dve", lambda e, s=src_ap, si=si, xn=xn: e.scalar_tensor_tensor(
                    xn[:, 0:ncol], s, st[:, si, 3:4], gain[:, 0:ncol], op0=ALU.mult, op1=ALU.mult),
                     reads=src_bufs + [st.b, gain.b], writes=[xn.b])
            else:
                k.op("dve", lambda e, s=src_ap, xn=xn: e.tensor_copy(xn[:, 0:ncol], s),
                     reads=src_bufs, writes=[xn.b])
            pt = psum()
            ptv = pt.ap.bitcast(BF16)
            for kk in range(nk):
                k.op("pe", lambda e, kk=kk, xn=xn, ptv=ptv: e.transpose(
                    ptv[:, kk * 128:(kk + 1) * 128], xn[:, kk * 128:(kk + 1) * 128], ident.ap),
                     reads=[xn.b, ident.b], writes=[pt.b])
            c0 = dst_col0 + t * 128
            copy(ev_eng(), dstT[:, 0:nk, c0:c0 + 128],
                 ptv[:, 0:nk * 128].rearrange("p (a b) -> p a b", a=nk),
                 [pt.b], [dstT.bs[(c0 // 128) % len(dstT.bs)]])

    R1 = A.alloc("R1", [8, TOK], BF16, nbuf=16)
    R2_off = A.mark()
    R2 = A.alloc("R2", [8, TOK], BF16, nbuf=16)
    X0 = A.mark()

    def x_src(dram):
        def fn(t):
            xt = stg["xts"][t % 3]
            k.dma("sp", xt.ap, dram[t * 128:(t + 1) * 128, :], writes=[xt.b])
            return xt.ap, [xt.b]
        return fn

    def load_w(dst_tile, dst_ap, src_ap, nk):
        k.dma("pool", dst_ap, src_ap.rearrange("(a p) n -> p a n", p=128), writes=[dst_tile.b])

    need_mix = stage in ("full", "attn", "ssm", "mix", "tailmix")
    attnT = None
    gyT = None
    if need_mix:
        attnT = A.alloc("attnT", [4, TOK], BF16, nbuf=16)
        gyT = A.alloc("gyT", [4, TOK], BF16, nbuf=4)
    X1 = A.mark()

    A.seek(X1)
    alloc_staging()
    build_T(x_src(x_own), NT, R1, 0, gb["g_mix"])
    if need_mix and stage != "ssm":
        build_T(x_src(x_prev), NT, R2, 0, gb["g_mix"])
    k.barrier()

    def dump_T(src, bufs):
        A.seek(X1)
        for j in range(4):
            t = A.alloc("dbgf%d" % j, [TOK], F32)
            k.op("dve", lambda e, j=j, t=t: e.tensor_copy(t.ap, src[:, j, :]), reads=bufs, writes=[t.b])
            k.dma("sp", dbg_d[j * 128:(j + 1) * 128, :], t.ap, reads=[t.b])
            k.wait_bufs("sp", [t.b])
        k.barrier()

    if stage in ("full", "attn", "mix"):
        A.seek(X0 + 16 * 1024)
        attention_phase(k, A, locals())
        k.barrier()
        if stage == "attn":
            dump_T(attnT, attnT.bs)

    if stage in ("full", "ssm", "mix"):
        A.seek(X1)
        ssm_phase(k, A, locals())
        k.barrier()
        if stage == "ssm":
            dump_T(gyT, gyT.bs)

    if stage == "tailmix":
        k.op("dve", lambda e: e.memset(attnT.ap, 0.01), writes=attnT.bs)
        k.op("dve", lambda e: e.memset(gyT.ap, 0.01), writes=gyT.bs)
    if stage in ("full", "ffn", "mix", "tailmix"):
        tail_phase(k, A, locals(), with_mix=(stage != "ffn"))

    k.barrier()
    return k.finish()


def attention_phase(k, A, L):
    ident, R1, R2, attnT = L["ident"], L["R1"], L["R2"], L["attnT"]
    psum, ps_big, copy, mm, ev_eng = L["psum"], L["ps_big"], L["copy"], L["mm"], L["ev_eng"]
    w_in, pos_tab, invf_d, halo_bias_d = L["w_in"], L["pos_tab"], L["invf_d"], L["halo_bias_d"]
    load_w = L["load_w"]

    sint = A.alloc("sint", [NROPE, 16], F32)
    cost = A.alloc("cost", [NROPE, 16], F32)
    mask_std = A.alloc("mask_std", [256], F32)
    mask_first = A.alloc("mask_first", [256], F32)
    hb = A.alloc("hb", [1], F32)
    B1 = [A.alloc("B1_%d" % i, [4 + 512], BF16) for i in range(2)]
    B2 = [A.alloc("B2_%d" % i, [16 + 2048], BF16) for i in range(2)]
    oext = {1: A.alloc("oext1", [16, 520], BF16, nbuf=16), 2: A.alloc("oext2", [16, 520], BF16, nbuf=16)}
    wq = [A.alloc("wqkv%d" % i, [8, 512], BF16) for i in range(3)]
    m_work = A.mark()

    posi = A.alloc("posi", [NROPE], I32)
    posf = A.alloc("posf", [NROPE], F32)
    invf = A.alloc("invf", [16], F32)
    ang = A.alloc("ang", [NROPE, 16], F32)
    kf = A.alloc("kf", [NROPE, 16], F32)
    ki = A.alloc("ki", [NROPE, 16], I32)
    red = A.alloc("red", [NROPE, 16], F32)
    k.dma("sp", posi.ap, pos_tab, writes=[posi.b])
    k.dma("sp", invf.ap, invf_d, writes=[invf.b])
    k.op("dve", lambda e: e.tensor_copy(posf.ap, posi.ap), reads=[posi.b], writes=[posf.b])
    for j in range(16):
        k.op("dve", lambda e, j=j: e.tensor_scalar(ang[:, :, j], posf.ap, invf[:, j:j + 1], None, op0=ALU.mult),
             reads=[posf.b, invf.b], writes=[ang.b])
    c1, c2, c3 = _cw_consts()
    angf = ang.ap.rearrange("p a b -> p (a b)")
    kff = kf.ap.rearrange("p a b -> p (a b)")
    kif = ki.ap.rearrange("p a b -> p (a b)")
    redf = red.ap.rearrange("p a b -> p (a b)")
    sinf = sint.ap.rearrange("p a b -> p (a b)")
    cosf = cost.ap.rearrange("p a b -> p (a b)")
    k.op("dve", lambda e: e.tensor_scalar(kff, angf, 1.0 / (2 * math.pi), None, op0=ALU.mult),
         reads=[ang.b], writes=[kf.b])
    k.op("dve", lambda e: e.tensor_copy(kif, kff), reads=[kf.b], writes=[ki.b])
    k.op("dve", lambda e: e.tensor_copy(kff, kif), reads=[ki.b], writes=[kf.b])
    TWO_PI = 2 * math.pi

    def stt_(out, in0, sc, in1, rd, wr):
        k.op("dve", lambda e: e.scalar_tensor_tensor(out, in0, sc, in1, op0=ALU.mult, op1=ALU.add), reads=rd, writes=wr)

    stt_(redf, kff, -c1, angf, [kf.b, ang.b], [red.b])
    stt_(redf, kff, -c2, redf, [kf.b, red.b], [red.b])
    stt_(redf, kff, -c3, redf, [kf.b, red.b], [red.b])

    def wrap(dst, dstb, shift):
        k.op("dve", lambda e: e.tensor_scalar(dst, redf, float(shift), None, op0=ALU.add), reads=[red.b], writes=[dstb])
        k.op("dve", lambda e: e.tensor_scalar(kff, dst, math.pi, None, op0=ALU.is_gt), reads=[dstb, kf.b], writes=[kf.b])
        stt_(dst, kff, -TWO_PI, dst, [kf.b, dstb], [dstb])
        k.op("dve", lambda e: e.tensor_scalar(kff, dst, -math.pi, None, op0=ALU.is_lt), reads=[dstb, kf.b], writes=[kf.b])
        stt_(dst, kff, TWO_PI, dst, [kf.b, dstb], [dstb])

    wrap(angf, ang.b, 0.0)
    k.op("act", lambda e: e.activation(sinf, angf, AF.Sin), reads=[ang.b], writes=[sint.b])
    wrap(angf, ang.b, math.pi / 2)
    k.op("act", lambda e: e.activation(cosf, angf, AF.Sin), reads=[ang.b], writes=[cost.b])

    k.dma("sp", hb.ap, halo_bias_d, writes=[hb.b])
    k.op("pool", lambda e: e.memset(mask_std.ap, 0.0), writes=[mask_std.b])
    k.op("pool", lambda e: e.affine_select(mask_std.ap, mask_std.ap, pattern=[[1, 256]], compare_op=ALU.is_ge,
                                           fill=NEG, base=0, channel_multiplier=-1),
         reads=[mask_std.b], writes=[mask_std.b])
    k.op("pool", lambda e: e.affine_select(mask_std.ap, mask_std.ap, pattern=[[-1, 256]], compare_op=ALU.is_ge,
                                           fill=NEG, base=128, channel_multiplier=1),
         reads=[mask_std.b], writes=[mask_std.b])
    k.op("dve", lambda e: e.tensor_copy(mask_first[:, 128:256], mask_std[:, 128:256]),
         reads=[mask_std.b], writes=[mask_first.b])
    k.op("dve", lambda e: e.tensor_scalar(mask_first[:, 0:128], mask_std[:, 0:128], hb[:, 0:1], None, op0=ALU.add),
         reads=[mask_std.b, hb.b, mask_first.b], writes=[mask_first.b])

    Bf = A.alloc("Bf", [16 + 2048], F32)
    for (Bt, mult, pads, width) in ((B1, 4, (3, 4), 516), (B2, 16, (15, 16), 2064)):
        for i, pad in enumerate(pads):
            k.op("pool", lambda e, width=width: e.memset(Bf[:, 0:width], 1.0), reads=[Bf.b], writes=[Bf.b])
            k.op("pool", lambda e, width=width, pad=pad, mult=mult: e.affine_select(
                Bf[:, 0:width], Bf[:, 0:width], pattern=[[1, width]], compare_op=ALU.is_equal,
                fill=0.0, base=-pad, channel_multiplier=-mult), reads=[Bf.b], writes=[Bf.b])
            k.op("dve", lambda e, Bt=Bt, i=i, width=width: e.tensor_copy(Bt[i].ap, Bf[:, 0:width]),
                 reads=[Bf.b], writes=[Bt[i].b])
    k.barrier()
    A.seek(m_work)

    qsb = [A.alloc("qsb%d" % i, [512], BF16) for i in range(2)]
    ksb = [A.alloc("ksb%d" % i, [512], BF16) for i in range(2)]
    qT = [A.alloc("qT%d" % i, [512], BF16) for i in range(2)]
    kT = [A.alloc("kT%d" % i, [512], BF16) for i in range(3)]
    vv = [A.alloc("vv%d" % i, [512], BF16) for i in range(3)]
    sm = [A.alloc("sm%d" % i, [4, 256], F32) for i in range(2)]
    Pm = [A.alloc("P%d" % i, [4, 256], BF16) for i in range(2)]
    PT = [A.alloc("PT%d" % i, [8, 128], BF16) for i in range(2)]
    rt = [A.alloc("rt%d" % i, [16, 16], F32) for i in range(2)]
    stt = [A.alloc("stt%d" % i, [8, 4], F32) for i in range(2)]
    o0 = [A.alloc("o0_0", [512], F32)] * 2
    lse0 = [A.alloc("lse0_%d" % i, [4], F32) for i in range(2)]
    mg = [A.alloc("mg%d" % i, [8, 16], F32) for i in range(2)]
    acc = [A.alloc("acc0", [512], F32)] * 2
    attn_tok = [A.alloc("attn_tok%d" % i, [512], BF16) for i in range(2)]

    def ncols(tile, kk):
        s0, d = tile
        if s0 >= TOK:
            return R1[:, kk, ds(s0 - TOK, 128, d)], R1.bs
        return R2[:, kk, ds(s0, 128, d)], R2.bs

    def proj(tile, w):
        pt = psum()
        for kk in range(8):
            lhs, lb = ncols(tile, kk)
            mm(pt.ap, lhs, w[:, kk, :], kk == 0, kk == 7, lb + [w.b], [pt.b])
        return pt

    def rope(pt, dst, col, par):
        copy("act", dst.ap, pt.ap, [pt.b], [dst.b])
        pv = pt.ap.rearrange("p (h d) -> p h d", h=4)
        dv = dst.ap.rearrange("p (h d) -> p h d", h=4)
        x1 = pv[:, :, 0:16]
        x2 = pv[:, :, 16:32]
        cb = L_cost[:, col, :].unsqueeze(1).to_broadcast([128, 4, 16])
        sb_ = L_sint[:, col, :].unsqueeze(1).to_broadcast([128, 4, 16])
        r = rt[par]
        t1, t2, t3, t4 = r[:, 0:4, :], r[:, 4:8, :], r[:, 8:12, :], r[:, 12:16, :]
        rd = [pt.b, cost.b, sint.b, dst.b]
        k.op("dve", lambda e: e.tensor_tensor(t1, x1, cb, ALU.mult), reads=rd, writes=[r.b])
        k.op("dve", lambda e: e.tensor_tensor(t2, x2, sb_, ALU.mult), reads=rd, writes=[])
        k.op("dve", lambda e: e.tensor_tensor(t3, x2, cb, ALU.mult), reads=rd, writes=[])
        k.op("dve", lambda e: e.tensor_tensor(t4, x1, sb_, ALU.mult), reads=rd, writes=[r.b])
        k.op("dve", lambda e: e.tensor_tensor(dv[:, :, 0:16], t1, t2, ALU.subtract), reads=[r.b, dst.b], writes=[dst.b])
        k.op("dve", lambda e: e.tensor_tensor(dv[:, :, 16:32], t3, t4, ALU.add), reads=[r.b, dst.b], writes=[dst.b])

    L_cost, L_sint = cost.ap, sint.ap

    def transpose4(src, dst):
        pt = psum()
        ptv = pt.ap.bitcast(BF16)
        for h in range(4):
            k.op("pe", lambda e, h=h: e.transpose(ptv[:, h * 128:(h + 1) * 128], src[:, h * 128:(h + 1) * 128], ident.ap),
                 reads=[src.b, ident.b], writes=[pt.b])
        copy(ev_eng(), dst.ap, ptv[:, 0:512], [pt.b], [dst.b])

    unit_ctr = [0]

    def kv_tile(tile, col, slot, wk, wv, par):
        pk = proj(tile, wk)
        rope(pk, ksb[par], col, par)
        transpose4(ksb[par], kT[slot])
        pv = proj(tile, wv)
        copy(ev_eng(), vv[slot].ap, pv.ap, [pv.b], [vv[slot].b])

    CUT = int(os.environ.get("K_CUT", "99"))

    def qpart(tile, col, wq_, par):
        pq = proj(tile, wq_)
        rope(pq, qsb[par], col, par)
        transpose4(qsb[par], qT[par])

    def attend_a(prev_slot, cur_slot, par):
        sv = ps_big.ap.rearrange("p (h c) -> p h c", h=4)
        for h in range(4):
            hs = slice(h * 128, (h + 1) * 128)
            mm(sv[:, h, 0:128], qT[par][:, hs], kT[prev_slot][:, hs], True, True,
               [qT[par].b, kT[prev_slot].b], [ps_big.b])
            mm(sv[:, h, 128:256], qT[par][:, hs], kT[cur_slot][:, hs], True, True,
               [qT[par].b, kT[cur_slot].b], [ps_big.b])

    def attend_b1(first, par):
        sv = ps_big.ap.rearrange("p (h c) -> p h c", h=4)
        mk = mask_first if first else mask_std
        s_ = sm[par]
        st_ = stt[par]
        k.op("dve", lambda e: e.tensor_tensor(s_.ap, sv, mk.ap.unsqueeze(1).to_broadcast([128, 4, 256]), ALU.add),
             reads=[ps_big.b, mk.b], writes=[s_.b])
        k.op("dve", lambda e: e.tensor_reduce(st_[:, 0, :], s_.ap, axis=AX.X, op=ALU.max), reads=[s_.b], writes=[st_.b])
        k.op("dve", lambda e: e.tensor_scalar(st_[:, 1, :], st_[:, 0, :], -SCALE, None, op0=ALU.mult),
             reads=[st_.b], writes=[st_.b])
        P_ = Pm[par]
        for h in range(4):
            k.op("act", lambda e, h=h: e.activation(P_[:, h, :], s_[:, h, :], AF.Exp, bias=st_[:, 1, h:h + 1],
                                                    scale=SCALE, accum_out=st_[:, 2, h:h + 1]),
                 reads=[s_.b, st_.b], writes=[P_.b, st_.b])

    def attend_b2(prev_slot, cur_slot, g, uidx, par):
        st_ = stt[par]
        P_ = Pm[par]
        pt = psum()
        ptv = pt.ap.bitcast(BF16)
        for h in range(4):
            for half in range(2):
                j = h * 2 + half
                k.op("pe", lambda e, h=h, half=half, j=j: e.transpose(
                    ptv[:, j * 128:(j + 1) * 128], P_[:, h, half * 128:(half + 1) * 128], ident.ap),
                     reads=[P_.b, ident.b], writes=[pt.b])
        PT_ = PT[par]
        copy(ev_eng(), PT_.ap, ptv.rearrange("p (a b) -> p a b", a=8), [pt.b], [PT_.b])
        po = psum()
        for h in range(4):
            hs = slice(h * 128, (h + 1) * 128)
            mm(po[:, hs], PT_[:, 2 * h, :], vv[prev_slot][:, hs], True, False, [PT_.b, vv[prev_slot].b], [po.b])
            mm(po[:, hs], PT_[:, 2 * h + 1, :], vv[cur_slot][:, hs], False, True, [PT_.b, vv[cur_slot].b], [po.b])
        if CUT <= 4:
            return par
        k.op("dve", lambda e: e.reciprocal(st_[:, 3, :], st_[:, 2, :]), reads=[st_.b], writes=[st_.b])
        k.op("act", lambda e: e.activation(st_[:, 4, :], st_[:, 2, :], AF.Ln), reads=[st_.b], writes=[st_.b])
        rden_b = st_[:, 3, :].unsqueeze(2).to_broadcast([128, 4, 128])
        pov = po.ap.rearrange("p (h d) -> p h d", h=4)
        if g == 0:
            o_ = o0[par]
            k.op("dve", lambda e: e.tensor_tensor(o_.ap.rearrange("p (h d) -> p h d", h=4), pov, rden_b, ALU.mult),
                 reads=[po.b, st_.b], writes=[o_.b])
            k.op("dve", lambda e: e.tensor_tensor(lse0[par].ap, st_[:, 4, :], st_[:, 1, :], ALU.subtract),
                 reads=[st_.b], writes=[lse0[par].b])
        else:
            oe = oext[g]
            ob = oe.bs[uidx]
            k.op("dve", lambda e: e.tensor_tensor(oe[:, uidx, 0:512].rearrange("p (h d) -> p h d", h=4), pov, rden_b,
                                                  ALU.mult),
                 reads=[po.b, st_.b], writes=[ob])
            k.op("dve", lambda e: e.tensor_tensor(st_[:, 5, :], st_[:, 4, :], st_[:, 1, :], ALU.subtract),
                 reads=[st_.b], writes=[st_.b])
            k.op("dve", lambda e: e.tensor_copy(oe[:, uidx, 512:516], st_[:, 5, :]), reads=[st_.b, ob], writes=[ob])
            k.op("dve", lambda e: e.tensor_copy(st_[:, 6, :], oe[:, uidx, 512:516]), reads=[ob, st_.b], writes=[st_.b])
            k.op("dve", lambda e: e.tensor_tensor(oe[:, uidx, 516:520], st_[:, 5, :], st_[:, 6, :], ALU.subtract),
                 reads=[st_.b, ob], writes=[ob])
        return par

    def merge(T, par):
        n4, q4 = T // 4, T % 4
        p1 = psum()
        p2 = psum()
        pl = psum()
        def b1v(r):
            bt = B1[0] if r % 2 == 1 else B1[1]
            pad = 3 if r % 2 == 1 else 4
            o_ = pad + 128 * q4 - r
            assert o_ % 2 == 0
            return bt[:, o_:o_ + 128], bt.b

        def b2v(j):
            bt = B2[0] if j % 2 == 1 else B2[1]
            pad = 15 if j % 2 == 1 else 16
            o_ = pad + 128 * T - j
            assert o_ % 2 == 0
            return bt[:, o_:o_ + 128], bt.b

        for r in range(4):
            u = r * 4 + n4
            lhs, lb = b1v(r)
            mm(p1.ap, lhs, oext[1][:, u, 0:512], r == 0, r == 3, [lb, oext[1].bs[u]], [p1.b])
        for r in range(4):
            u = r * 4 + n4
            lhs, lb = b1v(r)
            mm(pl[:, 0:8], lhs, oext[1][:, u, 512:520], r == 0, r == 3, [lb, oext[1].bs[u]], [pl.b])
        for j in range(16):
            lhs, lb = b2v(j)
            mm(p2.ap, lhs, oext[2][:, j, 0:512], j == 0, j == 15, [lb, oext[2].bs[j]], [p2.b])
        for j in range(16):
            lhs, lb = b2v(j)
            mm(pl[:, 8:16], lhs, oext[2][:, j, 512:520], j == 0, j == 15, [lb, oext[2].bs[j]], [pl.b])
        m_ = mg[par]
        lv = m_[:, 0, 0:12].rearrange("p (g h) -> p g h", g=3)
        k.op("dve", lambda e: e.tensor_copy(m_[:, 6, :], pl[:, 0:16]), reads=[pl.b, m_.b], writes=[m_.b])
        k.op("dve", lambda e: e.tensor_copy(lv[:, 0, :], lse0[par].ap), reads=[lse0[par].b, m_.b], writes=[m_.b])
        k.op("dve", lambda e: e.tensor_tensor(lv[:, 1, :], m_[:, 6, 0:4], m_[:, 6, 4:8], ALU.add), reads=[m_.b], writes=[m_.b])
        k.op("dve", lambda e: e.tensor_tensor(lv[:, 2, :], m_[:, 6, 8:12], m_[:, 6, 12:16], ALU.add), reads=[m_.b], writes=[m_.b])
        mx = m_[:, 1, 0:4]
        k.op("dve", lambda e: e.tensor_tensor(mx, lv[:, 0, :], lv[:, 1, :], ALU.max), reads=[m_.b], writes=[m_.b])
        k.op("dve", lambda e: e.tensor_tensor(mx, mx, lv[:, 2, :], ALU.max), reads=[m_.b], writes=[m_.b])
        ev = m_[:, 2, 0:12].rearrange("p (g h) -> p g h", g=3)
        k.op("dve", lambda e: e.tensor_tensor(ev, lv, mx.unsqueeze(1).to_broadcast([128, 3, 4]), ALU.subtract),
             reads=[m_.b], writes=[m_.b])
        k.op("act", lambda e: e.activation(m_[:, 3, 0:12], m_[:, 2, 0:12], AF.Exp), reads=[m_.b], writes=[m_.b])
        e3 = m_[:, 3, 0:12].rearrange("p (g h) -> p g h", g=3)
        sm_ = m_[:, 4, 0:4]
        k.op("dve", lambda e: e.tensor_tensor(sm_, e3[:, 0, :], e3[:, 1, :], ALU.add), reads=[m_.b], writes=[m_.b])
        k.op("dve", lambda e: e.tensor_tensor(sm_, sm_, e3[:, 2, :], ALU.add), reads=[m_.b], writes=[m_.b])
        k.op("dve", lambda e: e.reciprocal(m_[:, 4, 4:8], sm_), reads=[m_.b], writes=[m_.b])
        wv = m_[:, 5, 0:12].rearrange("p (g h) -> p g h", g=3)
        k.op("dve", lambda e: e.tensor_tensor(wv, e3, m_[:, 4, 4:8].unsqueeze(1).to_broadcast([128, 3, 4]), ALU.mult),
             reads=[m_.b], writes=[m_.b])
        a_ = acc[par]
        at = attn_tok[par]
        for h in range(4):
            hs = slice(h * 128, (h + 1) * 128)
            k.op("dve", lambda e, h=h, hs=hs: e.tensor_scalar(a_[:, hs], o0[par][:, hs], wv[:, 0, h:h + 1], None,
                                                             op0=ALU.mult),
                 reads=[o0[par].b, m_.b], writes=[a_.b])
            k.op("dve", lambda e, h=h, hs=hs: e.scalar_tensor_tensor(a_[:, hs], p1[:, hs], wv[:, 1, h:h + 1], a_[:, hs],
                                                                    op0=ALU.mult, op1=ALU.add),
                 reads=[p1.b, m_.b, a_.b], writes=[a_.b])
            k.op("dve", lambda e, h=h, hs=hs: e.scalar_tensor_tensor(at[:, hs], p2[:, hs], wv[:, 2, h:h + 1], a_[:, hs],
                                                                    op0=ALU.mult, op1=ALU.add),
                 reads=[p2.b, m_.b, a_.b], writes=[at.b])
        pt = psum()
        ptv = pt.ap.bitcast(BF16)
        for h in range(4):
            k.op("pe", lambda e, h=h: e.transpose(ptv[:, h * 128:(h + 1) * 128], at[:, h * 128:(h + 1) * 128], ident.ap),
                 reads=[at.b, ident.b], writes=[pt.b])
        copy(ev_eng(), attnT[:, :, T * 128:(T + 1) * 128], ptv[:, 0:512].rearrange("p (a b) -> p a b", a=4),
             [pt.b], [attnT.bs[T]])

    items = []
    for g in (2, 1, 0):
        halo, own = group_tiles(g)
        hc0, oc0 = ROPE_COL0[g]
        nseq = len(halo)
        per = len(own) // nseq
        for s_ in range(nseq):
            items.append(("halo", g, halo[s_], hc0 + s_, len(items) % 3, None, None, None, s_ == 0))
            for n in range(per):
                ui = s_ * per + n
                items.append(("unit", g, own[ui], oc0 + ui, len(items) % 3, (len(items) - 1) % 3, n == 0, ui, False))
    upar = [0]

    def stage1a(it):
        kind, g, tile, col, slot, prev, first, ui, newg = it
        if newg:
            for i, c0 in enumerate((g * 512, 1536 + g * 512, 3072 + g * 512)):
                load_w(wq[i], wq[i].ap, w_in[:, c0:c0 + 512], 8)
        par = upar[0] % 2
        upar[0] += 1
        pk = proj(tile, wq[1])
        pv = proj(tile, wq[2])
        pq = proj(tile, wq[0]) if kind == "unit" else None
        return (par, pk, pq, pv)

    def stage1b_rope(it, h_):
        kind, g, tile, col, slot, prev, first, ui, newg = it
        par, pk, pq, pv = h_
        copy("act", vv[slot].ap, pv.ap, [pv.b], [vv[slot].b])
        rope(pk, ksb[par], col, par)
        if pq is not None:
            rope(pq, qsb[par], col, par)

    def stage1b_tr(it, h_):
        kind, g, tile, col, slot, prev, first, ui, newg = it
        par, pk, pq, pv = h_
        transpose4(ksb[par], kT[slot])
        if pq is not None:
            transpose4(qsb[par], qT[par])

    hs_ = {0: stage1a(items[0])}
    stage1b_rope(items[0], hs_[0])
    stage1b_tr(items[0], hs_[0])
    for i, it in enumerate(items):
        kind, g, tile, col, slot, prev, first, ui, newg = it
        nxt = items[i + 1] if i + 1 < len(items) else None
        if kind == "unit":
            attend_a(prev, slot, hs_[i][0])
        if nxt is not None:
            hs_[i + 1] = stage1a(nxt)
        if kind == "unit":
            attend_b1(first, hs_[i][0])
        if nxt is not None:
            stage1b_rope(nxt, hs_[i + 1])
        if kind == "unit":
            attend_b2(prev, slot, g, ui, hs_[i][0])
        if nxt is not None:
            stage1b_tr(nxt, hs_[i + 1])
        if kind == "unit" and g == 0:
            merge(ui, hs_[i][0])


def ssm_phase(k, A, L):
    ident, identf, R1, R2, gyT = L["ident"], L["identf"], L["R1"], L["R2"], L["gyT"]
    psum, copy, mm, ev_eng = L["psum"], L["copy"], L["mm"], L["ev_eng"]
    w_in, onehot_d, R2_off = L["w_in"], L["onehot_d"], L["R2_off"]
    nc = k.nc
    NOCC = os.environ.get("K_NOCC", "0") == "1"
    m_top = A.mark()

    def tt(out, a, b, op, rd, wr):
        k.op("dve", lambda e: e.tensor_tensor(out, a, b, op), reads=rd, writes=wr)

    def ts(out, a, s1, op0, rd, wr, s2=None, op1=None):
        if op1 is None:
            k.op("dve", lambda e: e.tensor_scalar(out, a, s1, None, op0=op0), reads=rd, writes=wr)
        else:
            k.op("dve", lambda e: e.tensor_scalar(out, a, s1, s2, op0=op0, op1=op1), reads=rd, writes=wr)

    def stt(out, in0, sc, in1, rd, wr, op0=ALU.mult, op1=ALU.add):
        k.op("dve", lambda e: e.scalar_tensor_tensor(out, in0, sc, in1, op0=op0, op1=op1), reads=rd, writes=wr)

    WinT = A.alloc("WinT", [16, 2, 128], BF16)
    Wout = A.alloc("Wout", [16, 2, 128], BF16)
    Mbf = A.alloc("Mbf", [32, 128], BF16, nbuf=32)
    Sel = A.alloc("Sel", [64, 128], BF16)
    Eb = A.alloc("Eb", [352], BF16)
    sm_ = A.alloc("ssm_small", [90, 16], F32)
    sb_ = [sm_.b]
    names = {}

    def S(name):
        if name not in names:
            names[name] = len(names)
            assert len(names) <= 90
        return sm_[:, names[name], :]

    bb = A.alloc("ssm_bb", [6, 16, 16], F32)
    dcol = A.alloc("dcol", [32], F32)
    oneh = A.alloc("oneh", [8], F32)
    rowmask = A.alloc("rowmask", [8], F32)
    halfpi = A.alloc("halfpi", [1], F32)
    blockmask = A.alloc("blockmask", [128], F32)
    Fs = A.alloc("Fs", [16, 2], F32)
    Hinit = A.alloc("Hinit", [16, 2], F32)
    nHim = A.alloc("nHim", [16], F32)
    G = A.alloc("Ggath", [8, 32], F32)
    m_tmp = A.mark()
    wu = A.alloc("wu", [8, 512], BF16)

    for nm, src in (("lr", L["lr2_d"]), ("li", L["li2_d"]), ("ldt", L["ldt2_d"])):
        k.dma("sp", S(nm), src, writes=sb_)
    for i, src in enumerate((L["b2re_d"], L["b2im_d"], L["c2re_d"], L["c2im_d"])):
        k.dma("sp", bb[:, i, :, :], src, writes=[bb.b])
    k.dma("sp", dcol.ap, L["dcol_d"], writes=[dcol.b])
    k.dma("sp", oneh.ap, onehot_d, writes=[oneh.b])
    k.dma("pool", wu.ap, w_in[:, 4608:5120].rearrange("(a p) n -> p a n", p=128), writes=[wu.b])

    uT = gyT
    for j in range(4):
        for tc in range(4):
            pt = psum()
            ts_ = slice(tc * 512, (tc + 1) * 512)
            for kk in range(8):
                mm(pt.ap, wu[:, kk, j * 128:(j + 1) * 128], R1[:, kk, ts_], kk == 0, kk == 7,
                   [wu.b] + [R1.bs[i] for i in range(tc * 4, tc * 4 + 4)], [pt.b])
            copy(ev_eng(), uT[:, j, ts_], pt.ap, [pt.b], [uT.bs[j]])

    Ef = A.alloc("Ef", [352], F32)
    k.op("pool", lambda e: e.memset(Ef.ap, 1.0), writes=[Ef.b])
    k.op("pool", lambda e: e.affine_select(Ef.ap, Ef.ap, pattern=[[1, 352]], compare_op=ALU.is_equal, fill=0.0,
                                           base=-112, channel_multiplier=-1), reads=[Ef.b], writes=[Ef.b])
    k.op("dve", lambda e: e.tensor_copy(Eb.ap, Ef.ap), reads=[Ef.b], writes=[Eb.b])
    k.op("pool", lambda e: e.memset(rowmask.ap, 1.0), writes=[rowmask.b])
    k.op("pool", lambda e: e.affine_select(rowmask.ap, rowmask.ap, pattern=[[-16, 8]], compare_op=ALU.is_ge, fill=0.0,
                                           base=0, channel_multiplier=1), reads=[rowmask.b], writes=[rowmask.b])
    k.op("pool", lambda e: e.affine_select(rowmask.ap, rowmask.ap, pattern=[[16, 8]], compare_op=ALU.is_ge, fill=0.0,
                                           base=15, channel_multiplier=-1), reads=[rowmask.b], writes=[rowmask.b])
    for a_ in range(8):
        for b_ in range(8):
            o_ = 112 - 16 * (b_ - a_)
            ts(Sel[:, a_ * 8 + b_, :], Eb[:, o_:o_ + 128], rowmask[:, a_:a_ + 1], ALU.mult, [Eb.b, rowmask.b], [Sel.b])
    k.op("pool", lambda e: e.memset(blockmask.ap, 1.0), writes=[blockmask.b])
    k.op("pool", lambda e: e.affine_select(blockmask.ap.rearrange("p (t c) -> p t c", t=8), blockmask.ap.rearrange("p (t c) -> p t c", t=8),
                                           pattern=[[16, 8], [0, 16]], compare_op=ALU.is_ge, fill=0.0,
                                           base=15, channel_multiplier=-1), reads=[blockmask.b], writes=[blockmask.b])
    k.op("dve", lambda e: e.memset(halfpi.ap, math.pi / 2), writes=[halfpi.b])

    def act(out, in_, func, rd, wr, **kw):
        k.op("act", lambda e: e.activation(out, in_, func, **kw), reads=rd, writes=wr)

    act(S("dt"), S("ldt"), AF.Exp, sb_, sb_)
    tt(S("t0"), S("lr"), S("dt"), ALU.mult, sb_, sb_)
    act(S("mag"), S("t0"), AF.Exp, sb_, sb_, scale=1.0 / 16)
    tt(S("t1"), S("li"), S("dt"), ALU.mult, sb_, sb_)
    act(S("sn"), S("t1"), AF.Sin, sb_, sb_, scale=1.0 / 16)
    act(S("cs"), S("t1"), AF.Sin, sb_ + [halfpi.b], sb_, scale=1.0 / 16, bias=halfpi[:, 0:1])
    tt(S("re"), S("mag"), S("cs"), ALU.mult, sb_, sb_)
    tt(S("im"), S("mag"), S("sn"), ALU.mult, sb_, sb_)

    def csquare(ore, oim, ire, iim):
        tt(S("sq_a"), ire, ire, ALU.mult, sb_, sb_)
        tt(S("sq_b"), iim, iim, ALU.mult, sb_, sb_)
        tt(S("sq_c"), ire, iim, ALU.mult, sb_, sb_)
        tt(ore, S("sq_a"), S("sq_b"), ALU.subtract, sb_, sb_)
        ts(oim, S("sq_c"), 2.0, ALU.mult, sb_, sb_)

    def cmul(ore, oim, are, aim, bre, bim):
        tt(S("cm_a"), are, bre, ALU.mult, sb_, sb_)
        tt(S("cm_b"), aim, bim, ALU.mult, sb_, sb_)
        tt(S("cm_c"), are, bim, ALU.mult, sb_, sb_)
        tt(S("cm_d"), aim, bre, ALU.mult, sb_, sb_)
        tt(ore, S("cm_a"), S("cm_b"), ALU.subtract, sb_, sb_)
        tt(oim, S("cm_c"), S("cm_d"), ALU.add, sb_, sb_)

    for i in range(3):
        csquare(S("re"), S("im"), S("re"), S("im"))
    csquare(S("Qr0"), S("Qi0"), S("re"), S("im"))
    for m in range(1, 12):
        csquare(S("Qr%d" % m), S("Qi%d" % m), S("Qr%d" % (m - 1)), S("Qi%d" % (m - 1)))
    k.op("dve", lambda e: e.memset(S("Pr0"), 1.0), reads=sb_, writes=sb_)
    k.op("dve", lambda e: e.memset(S("Pi0"), 0.0), reads=sb_, writes=sb_)
    k.op("dve", lambda e: e.tensor_copy(S("Pr1"), S("Qr0")), reads=sb_, writes=sb_)
    k.op("dve", lambda e: e.tensor_copy(S("Pi1"), S("Qi0")), reads=sb_, writes=sb_)
    for kk in range(2, 9):
        cmul(S("Pr%d" % kk), S("Pi%d" % kk), S("Pr%d" % (kk - 1)), S("Pi%d" % (kk - 1)), S("Qr0"), S("Qi0"))
    for m in range(3, 11):
        ts(S("nQi%d" % m), S("Qi%d" % m), -1.0, ALU.mult, sb_, sb_)
    ts(S("nr"), S("Qr0"), -1.0, ALU.add, sb_, sb_)
    tt(S("z_a"), S("lr"), S("lr"), ALU.mult, sb_, sb_)
    tt(S("z_b"), S("li"), S("li"), ALU.mult, sb_, sb_)
    tt(S("z_a"), S("z_a"), S("z_b"), ALU.add, sb_, sb_)
    k.op("dve", lambda e: e.reciprocal(S("rden"), S("z_a")), reads=sb_, writes=sb_)
    tt(S("z_c"), S("nr"), S("lr"), ALU.mult, sb_, sb_)
    tt(S("z_d"), S("Qi0"), S("li"), ALU.mult, sb_, sb_)
    tt(S("z_c"), S("z_c"), S("z_d"), ALU.add, sb_, sb_)
    tt(S("zre"), S("z_c"), S("rden"), ALU.mult, sb_, sb_)
    tt(S("z_c"), S("Qi0"), S("lr"), ALU.mult, sb_, sb_)
    tt(S("z_d"), S("nr"), S("li"), ALU.mult, sb_, sb_)
    tt(S("z_c"), S("z_c"), S("z_d"), ALU.subtract, sb_, sb_)
    tt(S("zim"), S("z_c"), S("rden"), ALU.mult, sb_, sb_)
    tt(S("z_a"), S("Qr3"), S("Qr3"), ALU.mult, sb_, sb_)
    tt(S("z_b"), S("Qi3"), S("Qi3"), ALU.mult, sb_, sb_)
    tt(S("z_a"), S("z_a"), S("z_b"), ALU.add, sb_, sb_)
    k.op("dve", lambda e: e.reciprocal(S("z_b"), S("z_a")), reads=sb_, writes=sb_)
    tt(S("ir"), S("Qr3"), S("z_b"), ALU.mult, sb_, sb_)
    tt(S("nii"), S("Qi3"), S("z_b"), ALU.mult, sb_, sb_)
    ts(S("ii"), S("nii"), -1.0, ALU.mult, sb_, sb_)

    def bc16(ap):
        return ap.unsqueeze(2).to_broadcast([128, 16, 16])

    def bc128(ap):
        return ap.unsqueeze(2).to_broadcast([128, 16, 128])

    bre, bim, cre, cim, bbre, bbim = (bb[:, i, :, :] for i in range(6))
    tmp3 = A.alloc("tmp3", [2, 16, 16], F32)
    rdb = sb_ + [bb.b, tmp3.b]
    tt(tmp3[:, 0], bre, bc16(S("zre")), ALU.mult, rdb, [tmp3.b])
    tt(tmp3[:, 1], bim, bc16(S("zim")), ALU.mult, rdb, [tmp3.b])
    tt(bbre, tmp3[:, 0], tmp3[:, 1], ALU.subtract, rdb, [bb.b])
    tt(tmp3[:, 0], bim, bc16(S("zre")), ALU.mult, rdb, [tmp3.b])
    tt(tmp3[:, 1], bre, bc16(S("zim")), ALU.mult, rdb, [tmp3.b])
    tt(bbim, tmp3[:, 0], tmp3[:, 1], ALU.add, rdb, [bb.b])

    m_after = A.mark()
    A.seek(R2_off)
    WB = A.alloc("WB", [16, 2, 128], F32)
    WO = A.alloc("WO", [16, 2, 128], F32)
    A.seek(m_after)
    WCp = A.alloc("WCp", [16, 2, 128], F32)
    tmpM = A.alloc("tmpM", [2, 128], F32)
    for sp in range(8):
        kk = 7 - sp
        cs = slice(sp * 16, (sp + 1) * 16)
        pr, pi = bc16(S("Pr%d" % kk)), bc16(S("Pi%d" % kk))
        tt(tmp3[:, 0], bbre, pr, ALU.mult, rdb, [tmp3.b])
        tt(tmp3[:, 1], bbim, pi, ALU.mult, rdb, [tmp3.b])
        tt(WB[:, :, 0, cs], tmp3[:, 0], tmp3[:, 1], ALU.subtract, [tmp3.b], [WB.b])
        tt(tmp3[:, 0], bbim, pr, ALU.mult, rdb, [tmp3.b])
        tt(tmp3[:, 1], bbre, pi, ALU.mult, rdb, [tmp3.b])
        tt(WB[:, :, 1, cs], tmp3[:, 0], tmp3[:, 1], ALU.add, [tmp3.b], [WB.b])
    for tp in range(8):
        kk = tp + 1
        cs = slice(tp * 16, (tp + 1) * 16)
        pr, pi = bc16(S("Pr%d" % kk)), bc16(S("Pi%d" % kk))
        tt(tmp3[:, 0], cre, pr, ALU.mult, rdb, [tmp3.b])
        tt(tmp3[:, 1], cim, pi, ALU.mult, rdb, [tmp3.b])
        tt(WO[:, :, 0, cs], tmp3[:, 0], tmp3[:, 1], ALU.subtract, [tmp3.b], [WO.b])
        tt(tmp3[:, 0], cre, pi, ALU.mult, rdb, [tmp3.b])
        tt(tmp3[:, 1], cim, pr, ALU.mult, rdb, [tmp3.b])
        tt(tmp3[:, 0], tmp3[:, 0], tmp3[:, 1], ALU.add, [tmp3.b], [tmp3.b])
        ts(WO[:, :, 1, cs], tmp3[:, 0], -1.0, ALU.mult, [tmp3.b], [WO.b])
    k.op("dve", lambda e: e.tensor_copy(Wout.ap, WO.ap), reads=[WO.b], writes=[Wout.b])
    tmpW = A.alloc("tmpW", [16, 128], F32)
    rdw = sb_ + [WO.b, tmpW.b]
    tt(WCp[:, :, 0, :], WO[:, :, 0, :], bc128(S("ir")), ALU.mult, rdw, [WCp.b])
    tt(tmpW.ap, WO[:, :, 1, :], bc128(S("ii")), ALU.mult, rdw, [tmpW.b])
    tt(WCp[:, :, 0, :], WCp[:, :, 0, :], tmpW.ap, ALU.add, [WCp.b, tmpW.b], [WCp.b])
    tt(WCp[:, :, 1, :], WO[:, :, 0, :], bc128(S("nii")), ALU.mult, rdw + [WCp.b], [WCp.b])
    tt(tmpW.ap, WO[:, :, 1, :], bc128(S("ir")), ALU.mult, rdw, [tmpW.b])
    tt(WCp[:, :, 1, :], WCp[:, :, 1, :], tmpW.ap, ALU.add, [WCp.b, tmpW.b], [WCp.b])
    for pair in range(16):
        pt = psum()
        for comp in range(2):
            k.op("pe", lambda e, pair=pair, comp=comp, pt=pt: e.transpose(pt[:, comp * 128:(comp + 1) * 128], WB[:, pair, comp, :], identf.ap),
                 reads=[WB.b, identf.b], writes=[pt.b])
        copy(ev_eng(), WinT[:, pair, :, :], pt[:, 0:256].rearrange("p (c q) -> p c q", c=2), [pt.b], [WinT.b])
    for g in range(32):
        pair, j2 = g // 2, g % 2
        ps_ = slice(64 * j2, 64 * j2 + 64)
        pt = psum()
        mm(pt[:, 0:128], WB[ps_, pair, 0, :], WCp[ps_, pair, 0, :], True, False, [WB.b, WCp.b], [pt.b])
        mm(pt[:, 0:128], WB[ps_, pair, 1, :], WCp[ps_, pair, 1, :], False, True, [WB.b, WCp.b], [pt.b])
        tm = tmpM[:, g % 2, :]
        tt(tm, pt[:, 0:128], blockmask.ap, ALU.mult, [pt.b, blockmask.b, tmpM.b], [tmpM.b])
        stt(Mbf[:, g, :], identf.ap, dcol[:, g:g + 1], tm, [identf.b, dcol.b, tmpM.b], [Mbf.bs[g]])
    k.barrier()

    A.seek(m_tmp)
    U = A.alloc("U", [32, 256], BF16, nbuf=32)
    Xp = [[A.alloc("X%d_%d" % (0, j), [2, 256], F32) for j in range(2)]] * 2
    Hb = [A.alloc("Hb%d" % i, [2, 256], BF16) for i in range(2)]
    Tt = A.alloc("Tt", [2, 8, 256], F32)
    tmpT = A.alloc("tmpT", [8, 128], F32)
    A.seek(R2_off)
    Xall = A.alloc("Xall", [16, 2, 256], F32, nbuf=16)

    for j in range(4):
        for g8 in range(0, 8, 2):
            pt = psum()
            for hh in range(2):
                for sp in range(8):
                    mm(pt[:, hh * 256:(hh + 1) * 256], Sel[:, (g8 + hh) * 8 + sp, :], uT[:, j, ds(sp, 256, 8)], sp == 0, sp == 7,
                       [Sel.b, uT.bs[j]], [pt.b])
            g = 8 * j + g8
            copy(ev_eng(), U[:, g:g + 2, :], pt.ap.rearrange("p (a c) -> p a c", a=2), [pt.b], [U.bs[g], U.bs[g + 1]])

    for pair in range(16):
        pt = psum()
        for j2 in range(2):
            for comp in range(2):
                mm(pt[64 * j2:64 * j2 + 64, comp * 256:(comp + 1) * 256], WinT[:, pair, comp, 64 * j2:64 * j2 + 64], U[:, 2 * pair + j2, :],
                   True, True, [WinT.b, U.bs[2 * pair + j2]], [pt.b])
        Xa, Xb = Xp[pair % 2]
        copy("act", Xa.ap, pt.ap.rearrange("p (c n) -> p c n", c=2), [pt.b], [Xa.b])
        src, dst = Xa, Xb
        for lv in range(8):
            sh = 1 << lv
            n = 256 - sh
            er = S("Qr%d" % (3 + lv))[:, pair:pair + 1]
            ei = S("Qi%d" % (3 + lv))[:, pair:pair + 1]
            nei = S("nQi%d" % (3 + lv))[:, pair:pair + 1]
            last = lv == 7
            d_re = Xall[:, pair, 0, :] if last else dst[:, 0, :]
            d_im = Xall[:, pair, 1, :] if last else dst[:, 1, :]
            db = Xall.bs[pair] if last else dst.b
            dfull = Xall[:, pair, :, 0:sh] if last else dst[:, :, 0:sh]
            copy("act", dfull, src[:, :, 0:sh], [src.b], [db])
            rd = sb_ + [src.b, db]
            stt(d_re[:, sh:256], src[:, 0, 0:n], er, src[:, 0, sh:256], rd, [db])
            stt(d_im[:, sh:256], src[:, 1, 0:n], er, src[:, 1, sh:256], rd, [])
            stt(d_re[:, sh:256], src[:, 1, 0:n], nei, d_re[:, sh:256], rd, [db])
            stt(d_im[:, sh:256], src[:, 0, 0:n], ei, d_im[:, sh:256], rd, [db])
            src, dst = dst, src
        k.op("dve", lambda e, pair=pair: e.tensor_copy(Fs[:, pair, :], Xall[:, pair, :, 255]), reads=[Xall.bs[pair]], writes=[Fs.b])

    if NOCC:
        k.op("dve", lambda e: e.memset(G.ap, 0.0), writes=[G.b])
    else:
        cin = nc.dram_tensor("ssm_cc_in", [128, 32], F32).ap()
        cout = nc.dram_tensor("ssm_cc_out", [NCORES * 128, 32], F32).ap()
        Bcin = Buf("cin")
        Bcout = Buf("cout")
        k.dma("pool", cin, Fs.ap.rearrange("p a b -> p (a b)"), reads=[Fs.b], writes=[Bcin])
        k.wait_bufs("pool", [Bcin])
        ccsem = k.newsem("ccsem")
        k.q["pool"].append(lambda e: e.collective_compute(
            "AllGather", ALU.bypass, replica_groups=[list(range(NCORES))], ins=[cin.opt()], outs=[cout.opt()]).then_inc(ccsem))
        tok = ("dma", ccsem, 1)
        Bcout.writers = {_tkey(tok): tok}
        _tadd(k.dma_toks, tok)
        k.dma("sp", G.ap, cout.rearrange("(r p) c -> p r c", p=128), reads=[Bcout], writes=[G.b])
    Er, Ei = S("Qr11"), S("Qi11")
    Gv = G.ap.rearrange("p r (a c) -> p r a c", c=2)
    k.op("dve", lambda e: e.memset(Hinit.ap, 0.0), writes=[Hinit.b])
    k.op("dve", lambda e: e.memset(S("ac_r"), 0.0), reads=sb_, writes=sb_)
    k.op("dve", lambda e: e.memset(S("ac_i"), 0.0), reads=sb_, writes=sb_)
    for r in range(8):
        rdh = sb_ + [Hinit.b, oneh.b, G.b]
        stt(Hinit[:, :, 0], S("ac_r"), oneh[:, r:r + 1], Hinit[:, :, 0], rdh, [Hinit.b])
        stt(Hinit[:, :, 1], S("ac_i"), oneh[:, r:r + 1], Hinit[:, :, 1], rdh, [Hinit.b])
        if r == 7:
            break
        cmul(S("hn_r"), S("hn_i"), S("ac_r"), S("ac_i"), Er, Ei)
        tt(S("ac_r"), S("hn_r"), Gv[:, r, :, 0], ALU.add, rdh, sb_)
        tt(S("ac_i"), S("hn_i"), Gv[:, r, :, 1], ALU.add, rdh, sb_)
    ts(nHim.ap, Hinit[:, :, 1], -1.0, ALU.mult, [Hinit.b], [nHim.b])

    def bcn(ap, n):
        return ap.unsqueeze(2).to_broadcast([128, 8, n])

    for half in range(2):
        ps8 = slice(half * 8, half * 8 + 8)
        rdt = sb_ + [Tt.b, tmpT.b]
        k.op("dve", lambda e, ps8=ps8: e.tensor_copy(Tt[:, 0, :, 0:1], S("Qr3")[:, ps8].unsqueeze(2)), reads=rdt, writes=[Tt.b])
        k.op("dve", lambda e, ps8=ps8: e.tensor_copy(Tt[:, 1, :, 0:1], S("Qi3")[:, ps8].unsqueeze(2)), reads=rdt, writes=[Tt.b])
        for lv in range(8):
            sh = 1 << lv
            er = bcn(S("Qr%d" % (3 + lv))[:, ps8], sh)
            ei = bcn(S("Qi%d" % (3 + lv))[:, ps8], sh)
            sre, sim = Tt[:, 0, :, 0:sh], Tt[:, 1, :, 0:sh]
            dre, dim = Tt[:, 0, :, sh:2 * sh], Tt[:, 1, :, sh:2 * sh]
            tv = tmpT[:, :, 0:sh]
            tt(dre, sre, er, ALU.mult, rdt, [Tt.b])
            tt(tv, sim, ei, ALU.mult, rdt, [tmpT.b])
            tt(dre, dre, tv, ALU.subtract, rdt, [Tt.b])
            tt(dim, sre, ei, ALU.mult, rdt, [Tt.b])
            tt(tv, sim, er, ALU.mult, rdt, [tmpT.b])
            tt(dim, dim, tv, ALU.add, rdt, [Tt.b])
        for pl_ in range(8):
            pair = half * 8 + pl_
            xb = Xall.bs[pair]
            hr = Hinit[:, pair, 0:1]
            hi = Hinit[:, pair, 1:2]
            nhi = nHim[:, pair:pair + 1]
            rdx = [Tt.b, Hinit.b, nHim.b, xb]
            stt(Xall[:, pair, 0, :], Tt[:, 0, pl_, :], hr, Xall[:, pair, 0, :], rdx, [xb])
            stt(Xall[:, pair, 0, :], Tt[:, 1, pl_, :], nhi, Xall[:, pair, 0, :], rdx, [xb])
            stt(Xall[:, pair, 1, :], Tt[:, 0, pl_, :], hi, Xall[:, pair, 1, :], rdx, [xb])
            stt(Xall[:, pair, 1, :], Tt[:, 1, pl_, :], hr, Xall[:, pair, 1, :], rdx, [xb])
            hb_ = Hb[pair % 2]
            copy("act", hb_[:, :, 1:256], Xall[:, pair, :, 0:255], [xb], [hb_.b])
            k.op("dve", lambda e, hb_=hb_, pair=pair: e.tensor_copy(hb_[:, :, 0:1], Hinit[:, pair, :].unsqueeze(2)),
                 reads=[Hinit.b, hb_.b], writes=[hb_.b])
            for j2 in range(2):
                g = 2 * pair + j2
                ps_ = slice(64 * j2, 64 * j2 + 64)
                pt = psum()
                mm(pt[:, 0:256], Mbf[:, g, :], U[:, g, :], True, False, [Mbf.bs[g], U.bs[g]], [pt.b])
                mm(pt[:, 0:256], Wout[ps_, pair, 0, :], hb_[ps_, 0, :], False, False, [Wout.b, hb_.b], [pt.b])
                mm(pt[:, 0:256], Wout[ps_, pair, 1, :], hb_[ps_, 1, :], False, True, [Wout.b, hb_.b], [pt.b])
                copy(ev_eng(), U[:, g, :], pt[:, 0:256], [pt.b], [U.bs[g]])
    for j in range(4):
        for tp in range(0, 8, 2):
            pt = psum()
            for hh in range(2):
                for g8 in range(8):
                    mm(pt[:, hh * 256:(hh + 1) * 256], Sel[:, (tp + hh) * 8 + g8, :], U[:, 8 * j + g8, :], g8 == 0, g8 == 7,
                       [Sel.b] + [U.bs[8 * j + g8]], [pt.b])
            for hh in range(2):
                k.op("act", lambda e, j=j, tp=tp, hh=hh, pt=pt: e.activation(gyT[:, j, ds(tp + hh, 256, 8)], pt[:, hh * 256:(hh + 1) * 256],
                                                                           AF.Gelu_apprx_tanh),
                     reads=[pt.b], writes=[gyT.bs[j]])


def tail_phase(k, A, L, with_mix):
    ident, R1, R2, attnT, gyT, gb = L["ident"], L["R1"], L["R2"], L["attnT"], L["gyT"], L["gb"]
    psum, copy, mm, ev_eng, build_T, load_w = L["psum"], L["copy"], L["mm"], L["ev_eng"], L["build_T"], L["load_w"]
    x_own, p_own, out_d = L["x_own"], L["p_own"], L["out_d"]
    w_in, w_ap, w_ga, w_gb, w_out = L["w_in"], L["w_ap"], L["w_ga"], L["w_gb"], L["w_out"]
    w_fg, w_fu, w_fd, w_pg, w_pp = L["w_fg"], L["w_fu"], L["w_fd"], L["w_pg"], L["w_pp"]
    stg, X0, R2_off, alloc_staging = L["stg"], L["X0"], L["R2_off"], L["alloc_staging"]
    g_ffn_d, g_final_d = L["g_ffn_d"], L["g_final_d"]
    KB_ = 1024

    A.seek(X0 + 64 * KB_)
    ws = [A.alloc("ws%d" % i, [8, 512], BF16) for i in range(4)]
    ws_i = [0]

    def wslot():
        w = ws[ws_i[0] % 4]
        ws_i[0] += 1
        return w

    M0 = A.mark()

    if with_mix:
        A.seek(X0 + 32 * KB_)
        tmp = [A.alloc("mt%d" % i, [3, 512], F32) for i in range(2)]
        wA = A.alloc("wA", [4, 1024], BF16)
        wGa = A.alloc("wGa", [4, 1024], BF16)
        wGb = A.alloc("wGb", [4, 1024], BF16)
        wGt = A.alloc("wGt", [2, 8, 1024], BF16, nbuf=4)
        for wt, src in ((wA, w_ap), (wGa, w_ga), (wGb, w_gb)):
            k.dma("pool", wt.ap, src.rearrange("(a p) n -> p a n", p=128), writes=[wt.b])
        for gi in range(2):
            for hb_ in range(2):
                c0 = 5120 + gi * 1024 + hb_ * 512
                k.dma("pool", wGt[:, gi, :, hb_ * 512:(hb_ + 1) * 512],
                      w_in[:, c0:c0 + 512].rearrange("(a p) n -> p a n", p=128), writes=[wGt.bs[gi * 2 + hb_]])

        class _V:
            def __init__(self, fn, b):
                self.fn, self.b = fn, b

            def __getitem__(self, key):
                return self.fn(key)

        for f in range(8):
            fs = slice(f * 128, (f + 1) * 128)
            fo = f * 128

            def _wa(key, fo=fo):
                p_, kk, cs = key
                if cs.start == 0:
                    return (wA if kk < 4 else wGa)[:, kk % 4, fo:fo + 128]
                return wGb[:, kk, fo:fo + 128]

            def _wg(key, fo=fo):
                p_, kk, cs = key
                return wGt[:, 0 if cs.start == 0 else 1, kk, fo:fo + 128]

            wa = _V(_wa, Buf("wa_dummy"))
            wg = _V(_wg, Buf("wg_dummy"))
            wa.bl = [wA.b, wGa.b, wGb.b]
            wg.bl = [wGt.bs[(f // 4)], wGt.bs[2 + (f // 4)]]
            for tc in range(4):
                ts_ = slice(tc * 512, (tc + 1) * 512)
                tb = [R1.bs[i] for i in range(tc * 4, tc * 4 + 4)]
                tm = tmp[tc % 2]
                pa = psum()
                for kk in range(4):
                    mm(pa.ap, wa[:, kk, 0:128], attnT[:, kk, ts_], kk == 0, kk == 3, wa.bl + attnT.bs, [pa.b])
                pga = psum()
                for kk in range(8):
                    mm(pga.ap, wg[:, kk, 0:128], R1[:, kk, ts_], kk == 0, kk == 7, wg.bl + tb, [pga.b])
                k.op("act", lambda e, tm=tm, pga=pga: e.activation(tm[:, 0, :], pga.ap, AF.Sigmoid), reads=[pga.b], writes=[tm.b])
                k.op("dve", lambda e, tm=tm, pa=pa: e.tensor_tensor(tm[:, 0, :], tm[:, 0, :], pa.ap, ALU.mult),
                     reads=[pa.b, tm.b], writes=[tm.b])
                pya = psum()
                for kk in range(4):
                    mm(pya.ap, wa[:, 4 + kk, 0:128], gyT[:, kk, ts_], kk == 0, kk == 3, wa.bl + gyT.bs, [pya.b])
                pyb = psum()
                for kk in range(4):
                    mm(pyb.ap, wa[:, kk, 128:256], gyT[:, kk, ts_], kk == 0, kk == 3, wa.bl + gyT.bs, [pyb.b])
                pgs = psum()
                for kk in range(8):
                    mm(pgs.ap, wg[:, kk, 128:256], R1[:, kk, ts_], kk == 0, kk == 7, wg.bl + tb, [pgs.b])
                k.op("act", lambda e, tm=tm, pyb=pyb: e.activation(tm[:, 1, :], pyb.ap, AF.Sigmoid), reads=[pyb.b], writes=[tm.b])
                k.op("act", lambda e, tm=tm, pgs=pgs: e.activation(tm[:, 2, :], pgs.ap, AF.Sigmoid), reads=[pgs.b], writes=[tm.b])
                k.op("dve", lambda e, tm=tm, pya=pya: e.tensor_tensor(tm[:, 1, :], tm[:, 1, :], pya.ap, ALU.mult),
                     reads=[pya.b, tm.b], writes=[tm.b])
                k.op("dve", lambda e, tm=tm: e.tensor_tensor(tm[:, 1, :], tm[:, 1, :], tm[:, 2, :], ALU.mult),
                     reads=[tm.b], writes=[tm.b])
                k.op("dve", lambda e, tm=tm, f=f, ts_=ts_: e.tensor_tensor(R2[:, f, ts_], tm[:, 0, :], tm[:, 1, :], ALU.add),
                     reads=[tm.b], writes=[R2.bs[i] for i in range(tc * 4, tc * 4 + 4)])
        k.barrier()

    A.seek(X0)
    resid = A.alloc("resid", [NT, D], F32, nbuf=NT)
    for t in range(NT):
        k.dma("sp", resid[:, t, :], x_own[t * 128:(t + 1) * 128, :], writes=[resid.bs[t]])

    def add_resid(t, half, pt):
        cs = slice(half * 512, (half + 1) * 512)
        k.op("dve", lambda e: e.tensor_tensor(resid[:, t, cs], resid[:, t, cs], pt.ap, ALU.add),
             reads=[pt.b, resid.bs[t]], writes=[resid.bs[t]])

    if with_mix:
        for half in range(2):
            wo = wslot()
            load_w(wo, wo.ap, w_out[:, half * 512:(half + 1) * 512], 8)
            for t in range(NT):
                pt = psum()
                for kk in range(8):
                    mm(pt.ap, R2[:, kk, t * 128:(t + 1) * 128], wo[:, kk, :], kk == 0, kk == 7, [R2.bs[t], wo.b], [pt.b])
                add_resid(t, half, pt)
        k.barrier()

    A.seek(R2_off + 16 * KB_)
    for nm, src in (("g_ffn", g_ffn_d), ("g_final", g_final_d)):
        t_ = A.alloc(nm, [D], F32)
        k.dma("sp", t_.ap, src[0:1, :].to_broadcast([128, D]), writes=[t_.b])
        gb[nm] = t_

    A.seek(R2_off)
    ws_x = [A.alloc("wsx0", [8, 512], BF16), A.alloc("wsx1", [8, 512], BF16)]
    A.seek(R2_off + 24 * KB_)
    ws_x.append(A.alloc("wsx2", [8, 512], BF16))
    ffn_ws = ws + ws_x
    ffn_i = [0]

    def fslot():
        w = ffn_ws[ffn_i[0] % len(ffn_ws)]
        ffn_i[0] += 1
        return w

    A.seek(M0)
    alloc_staging(256)
    M1 = A.mark()

    def resid_src(t):
        return resid[:, t, :], [resid.bs[t]]

    build_T(resid_src, NT, R1, 0, gb["g_ffn"])
    actT = A.alloc("actT", [4, TOK], BF16, nbuf=16)
    sg = [A.alloc("sg%d" % i, [512], F32) for i in range(2)]
    nfb = (DFF + 511) // 512
    for fb in range(nfb):
        c0 = fb * 512
        cw = min(512, DFF - c0)
        nft = cw // 128
        wg_ = fslot()
        wu_ = fslot()
        load_w(wg_, wg_[:, :, 0:cw], w_fg[:, c0:c0 + cw], 8)
        load_w(wu_, wu_[:, :, 0:cw], w_fu[:, c0:c0 + cw], 8)
        wd_ = [fslot(), fslot()]
        for half in range(2):
            load_w(wd_[half], wd_[half][:, 0:nft, :], w_fd[c0:c0 + cw, half * 512:(half + 1) * 512], nft)
        for ft in range(nft):
            for tc in range(4):
                ts_ = slice(tc * 512, (tc + 1) * 512)
                tb = [R1.bs[i] for i in range(tc * 4, tc * 4 + 4)]
                pg_ = psum()
                for kk in range(8):
                    mm(pg_.ap, wg_[:, kk, ft * 128:(ft + 1) * 128], R1[:, kk, ts_], kk == 0, kk == 7, [wg_.b] + tb, [pg_.b])
                pu_ = psum()
                for kk in range(8):
                    mm(pu_.ap, wu_[:, kk, ft * 128:(ft + 1) * 128], R1[:, kk, ts_], kk == 0, kk == 7, [wu_.b] + tb, [pu_.b])
                s_ = sg[(ft * 4 + tc) % 2]
                k.op("act", lambda e, s_=s_, pg_=pg_: e.activation(s_.ap, pg_.ap, AF.Silu), reads=[pg_.b], writes=[s_.b])
                k.op("dve", lambda e, s_=s_, pu_=pu_, ft=ft, ts_=ts_: e.tensor_tensor(actT[:, ft, ts_], s_.ap, pu_.ap, ALU.mult),
                     reads=[s_.b, pu_.b], writes=[actT.bs[i] for i in range(tc * 4, tc * 4 + 4)])
        for half in range(2):
            for t in range(NT):
                pt = psum()
                for kk in range(nft):
                    mm(pt.ap, actT[:, kk, t * 128:(t + 1) * 128], wd_[half][:, kk, :], kk == 0, kk == nft - 1,
                       [actT.bs[t], wd_[half].b], [pt.b])
                add_resid(t, half, pt)
    k.barrier()

    A.seek(M1)
    build_T(resid_src, NT, R1, 0, None, norm=False)

    def p_src(t):
        xt = stg["xts"][t % 3]
        k.dma("sp", xt[:, 0:256], p_own[t * 128:(t + 1) * 128, :], writes=[xt.b])
        return xt[:, 0:256], [xt.b]

    build_T(p_src, NT, R2, 0, None, norm=False, ncol=256)
    sgp = [A.alloc("sgp%d" % i, [512], F32) for i in range(2)]
    for half in range(2):
        wg_ = wslot()
        wp_ = wslot()
        load_w(wg_, wg_.ap, w_pg[:, half * 512:(half + 1) * 512], 8)
        load_w(wp_, wp_[:, 0:2, :], w_pp[:, half * 512:(half + 1) * 512], 2)
        for t in range(NT):
            pg_ = psum()
            for kk in range(8):
                mm(pg_.ap, R1[:, kk, t * 128:(t + 1) * 128], wg_[:, kk, :], kk == 0, kk == 7, [R1.bs[t], wg_.b], [pg_.b])
            pp_ = psum()
            for kk in range(2):
                mm(pp_.ap, R2[:, kk, t * 128:(t + 1) * 128], wp_[:, kk, :], kk == 0, kk == 1, [R2.bs[t], wp_.b], [pp_.b])
            s_ = sgp[t % 2]
            k.op("act", lambda e, s_=s_, pg_=pg_: e.activation(s_.ap, pg_.ap, AF.Sigmoid), reads=[pg_.b], writes=[s_.b])
            k.op("dve", lambda e, s_=s_, pp_=pp_: e.tensor_tensor(s_.ap, s_.ap, pp_.ap, ALU.mult), reads=[s_.b, pp_.b], writes=[s_.b])
            cs = slice(half * 512, (half + 1) * 512)
            k.op("dve", lambda e, s_=s_, t=t, cs=cs: e.tensor_tensor(resid[:, t, cs], resid[:, t, cs], s_.ap, ALU.add),
                 reads=[s_.b, resid.bs[t]], writes=[resid.bs[t]])
    fst = A.alloc("fst", [NT, 4], F32)
    fj = A.alloc("fj", [D], BF16)
    oo = [A.alloc("oo%d" % i, [D], F32) for i in range(2)]
    gfin = gb["g_final"]
    for t in range(NT):
        k.op("act", lambda e, t=t: e.activation(fj.ap, resid[:, t, :], AF.Square, accum_out=fst[:, t, 0:1]),
             reads=[resid.bs[t]], writes=[fj.b, fst.b])
        k.op("dve", lambda e, t=t: e.tensor_scalar(fst[:, t, 1:2], fst[:, t, 0:1], 1.0 / D, EPS, op0=ALU.mult, op1=ALU.add),
             reads=[fst.b], writes=[fst.b])
        k.op("act", lambda e, t=t: e.activation(fst[:, t, 2:3], fst[:, t, 1:2], AF.Sqrt), reads=[fst.b], writes=[fst.b])
        k.op("dve", lambda e, t=t: e.reciprocal(fst[:, t, 3:4], fst[:, t, 2:3]), reads=[fst.b], writes=[fst.b])
        o_ = oo[t % 2]
        k.op("dve", lambda e, t=t, o_=o_: e.scalar_tensor_tensor(o_.ap, resid[:, t, :], fst[:, t, 3:4], gfin.ap,
                                                                  op0=ALU.mult, op1=ALU.mult),
             reads=[resid.bs[t], fst.b, gfin.b], writes=[o_.b])
        k.dma("sp", out_d[t * 128:(t + 1) * 128, :], o_.ap, reads=[o_.b])
    k.wait_bufs("sp", [oo[0].b, oo[1].b])


_NC_CACHE = {}


def _host_inputs(inp):
    x = np.ascontiguousarray(inp["x"][0])
    p = np.ascontiguousarray(inp["p"][0, 0])
    pos = np.ascontiguousarray(inp["positions"][0]).astype(np.int32)
    half = 16
    invf = (np.float32(500000.0) ** (-np.arange(half, dtype=np.float32) * np.float32(2.0) / np.float32(32))).astype(np.float32)
    invf_t = np.ascontiguousarray(np.broadcast_to(invf[None, :], (128, 16))).astype(np.float32)

    def l2(a):
        return np.ascontiguousarray(a.reshape(16, 2, 64).transpose(1, 2, 0).reshape(128, 16))

    a_re, a_im = inp["a_re"][0], inp["a_im"][0]
    ldt = inp["log_dt"][0]
    ldt2 = np.ascontiguousarray(np.broadcast_to(ldt.reshape(16, 2, 1).transpose(1, 2, 0), (2, 64, 16)).reshape(128, 16))
    b_re, b_im = inp["b_re"][0], inp["b_im"][0]
    c_re, c_im = inp["c_re"][0], inp["c_im"][0]

    def lb(b):
        return np.ascontiguousarray(b.reshape(16, 2, 64, 16).transpose(1, 2, 0, 3).reshape(128, 16, 16))

    def lc(c):
        return np.ascontiguousarray(c.reshape(16, 2, 16, 64).transpose(1, 3, 0, 2).reshape(128, 16, 16))

    d = inp["d_skip"][0]
    dcol = np.ascontiguousarray(np.broadcast_to(d.T[None, :, :], (8, 16, 32)).reshape(128, 32))
    shared = {
        "invf": invf_t,
        "g_mix": np.ascontiguousarray(inp["g_mix"]), "g_ffn": np.ascontiguousarray(inp["g_ffn"]),
        "g_final": np.ascontiguousarray(inp["g_final"].reshape(1, D)),
        "w_in": np.ascontiguousarray(inp["w_in"][0]), "w_attn_proj": np.ascontiguousarray(inp["w_attn_proj"][0]),
        "w_glu_a": np.ascontiguousarray(inp["w_glu_a"][0]), "w_glu_b": np.ascontiguousarray(inp["w_glu_b"][0]),
        "w_out": np.ascontiguousarray(inp["w_out"][0]), "w_ffn_gate": np.ascontiguousarray(inp["w_ffn_gate"][0]),
        "w_ffn_up": np.ascontiguousarray(inp["w_ffn_up"][0]), "w_ffn_down": np.ascontiguousarray(inp["w_ffn_down"][0]),
        "w_ple_gate": np.ascontiguousarray(inp["w_ple_gate"][0]), "w_ple_proj": np.ascontiguousarray(inp["w_ple_proj"][0]),
        "lr2": l2(a_re), "li2": l2(a_im), "ldt2": ldt2.astype(np.float32),
        "b2re": lb(b_re), "b2im": lb(b_im), "c2re": lc(c_re), "c2im": lc(c_im), "dcol": dcol.astype(np.float32),
    }
    maps = []
    for c in range(NCORES):
        xo = x[c * TOK:(c + 1) * TOK]
        if c == 0:
            xp = np.zeros_like(xo)
            pp = np.zeros(TOK, np.int32)
        else:
            xp = x[(c - 1) * TOK:c * TOK]
            pp = pos[(c - 1) * TOK:c * TOK]
        pcat = np.concatenate([pp, pos[c * TOK:(c + 1) * TOK]])
        cols = []
        for g in range(3):
            h, o = group_tiles(g)
            for (s0, dd) in h + o:
                cols.append(pcat[s0 + dd * np.arange(128)])
        pos_tab = np.ascontiguousarray(np.stack(cols, axis=1)).astype(np.int32)
        m = dict(shared)
        m["x_own"] = np.ascontiguousarray(xo)
        m["x_prev"] = np.ascontiguousarray(xp)
        m["p_own"] = np.ascontiguousarray(p[c * TOK:(c + 1) * TOK])
        m["pos_tab"] = pos_tab
        m["halo_bias"] = np.full((128, 1), NEG if c == 0 else 0.0, np.float32)
        oh = np.zeros((128, 8), np.float32)
        oh[:, c] = 1.0
        m["onehot"] = oh
        maps.append(m)
    return maps


def run(inp, stage="full", trace=False):
    if stage not in _NC_CACHE:
        _NC_CACHE[stage] = build(stage)
    nc = _NC_CACHE[stage]
    maps = _host_inputs(inp)
    res = run_bass_kernel_spmd(nc, maps, core_ids=list(range(NCORES)), **({"trace": True} if trace else {}))
    return res


def kernel(**inputs):
    res = run(inputs, "full")
    out = np.concatenate([res.results[c]["out"] for c in range(NCORES)], axis=0)
    return out.reshape(1, NCORES * TOK, D).astype(np.float32)
```
